# Optimizing a Trainium2 kernel written in Bass

```python
import math
import jax, jax.numpy as jnp
from jax import lax
import numpy as np

D_MODEL = 1024
BATCH = 8
SEQ = 4096
DEPTH = 4
DEC_BATCH = 16
DEC_SEQ = 4096
PAST_LEN = 128

HEAD_DIM = 64
A_HEADS = 8
A_KV_HEADS = 2
A_GROUP = A_HEADS // A_KV_HEADS
WIN = 128
WIN_BLOCK = 128
B_HEADS = 8
GRID_W = 64
NA_ROWS = 8
NA_COLS = 16
NA_QC = 16
NA_STRIP = 2 * NA_COLS
NA_NCB = GRID_W // NA_QC
C_WIDTH = 512
SSM_GROUP_CH = 16
SSM_GROUPS = C_WIDTH // SSM_GROUP_CH
SSM_STATE = 64
D_WIDTH = 512
CONV_WIDTH = 31
D_FF = 2816
PLE_DIM = 256
N_ATTN_LAYERS = (DEPTH + 1) // 2
N_SSM_LAYERS = DEPTH // 2

A_Q = A_HEADS * HEAD_DIM
A_KV = A_KV_HEADS * HEAD_DIM
B_QKV = B_HEADS * HEAD_DIM
ATTN_IN = A_Q + 2 * A_KV + 3 * B_QKV
ATTN_OUT = A_Q + B_QKV
SSM_IN = C_WIDTH + 2 * D_WIDTH
SSM_OUT = C_WIDTH + D_WIDTH
NEG_INF = -1e30
EPS = 1e-6

kernel_name = 'hybrid_bidir_encoder'


def rms_norm(x, g):
    xf = x.astype(jnp.float32)
    y = xf * lax.rsqrt(jnp.mean(xf * xf, axis=-1, keepdims=True) + EPS)
    return (y * g.astype(jnp.float32)).astype(x.dtype)


def layer_norm(x, g, b):
    xf = x.astype(jnp.float32)
    mu = jnp.mean(xf, axis=-1, keepdims=True)
    xc = xf - mu
    y = xc * lax.rsqrt(jnp.mean(xc * xc, axis=-1, keepdims=True) + EPS)
    return (y * g.astype(jnp.float32) + b.astype(jnp.float32)).astype(x.dtype)


def swiglu(x, w_in, w_out):
    gate, up = jnp.split(x @ w_in, 2, axis=-1)
    return (jax.nn.silu(gate) * up) @ w_out


def alibi_slopes(n):
    return jnp.exp2(-8.0 * jnp.arange(1, n + 1, dtype=jnp.float32) / n)


def window_gqa(q, k, v, q_gain, k_gain, sink):
    b, L = q.shape[0], q.shape[1]
    nb = L // WIN_BLOCK
    qf = rms_norm(q.astype(jnp.float32), q_gain) * (HEAD_DIM ** -0.5)
    kf = rms_norm(k.astype(jnp.float32), k_gain)
    vf = v.astype(jnp.float32)
    pad = ((0, 0), (WIN_BLOCK, WIN_BLOCK), (0, 0), (0, 0))
    kp = jnp.pad(kf, pad).reshape(b, nb + 2, WIN_BLOCK, A_KV_HEADS, HEAD_DIM)
    vp = jnp.pad(vf, pad).reshape(b, nb + 2, WIN_BLOCK, A_KV_HEADS, HEAD_DIM)
    kw = jnp.concatenate([kp[:, :-2], kp[:, 1:-1], kp[:, 2:]], axis=2)
    vw = jnp.concatenate([vp[:, :-2], vp[:, 1:-1], vp[:, 2:]], axis=2)
    qb = qf.reshape(b, nb, WIN_BLOCK, A_KV_HEADS, A_GROUP, HEAD_DIM)
    s = jnp.einsum('bnqhgd,bnkhd->bnhgqk', qb, kw)
    t_pos = jnp.arange(L).reshape(nb, WIN_BLOCK)
    s_pos = (jnp.arange(nb)[:, None] - 1) * WIN_BLOCK + jnp.arange(3 * WIN_BLOCK)[None, :]
    dist = jnp.abs(s_pos[:, None, :] - t_pos[:, :, None])
    valid = (dist <= WIN) & (s_pos[:, None, :] >= 0) & (s_pos[:, None, :] < L)
    slopes = alibi_slopes(A_HEADS).reshape(A_KV_HEADS, A_GROUP)
    bias = -slopes[None, :, :, None, None] * dist[:, None, None].astype(jnp.float32)
    s = jnp.where(valid[:, None, None], s + bias, NEG_INF)
    sk = sink.astype(jnp.float32).reshape(1, 1, A_KV_HEADS, A_GROUP, 1, 1)
    m = jnp.maximum(jnp.max(s, axis=-1, keepdims=True), sk)
    e = jnp.exp(s - m)
    probs = e / (jnp.sum(e, axis=-1, keepdims=True) + jnp.exp(sk - m))
    o = jnp.einsum('bnhgqk,bnkhd->bnqhgd', probs, vw)
    return o.reshape(b, L, A_HEADS * HEAD_DIM)


def neighborhood_attn(q, k, v, q_gain, k_gain, rpb):
    b, L = q.shape[0], q.shape[1]
    rows = L // GRID_W
    kr = min(NA_ROWS, rows)
    qf = rms_norm(q.astype(jnp.float32), q_gain) * (HEAD_DIM ** -0.5)
    kf = rms_norm(k.astype(jnp.float32), k_gain)
    qg = qf.reshape(b, rows, NA_NCB, NA_QC, B_HEADS, HEAD_DIM)
    kg = kf.reshape(b, rows, GRID_W, B_HEADS, HEAD_DIM)
    vg = v.astype(jnp.float32).reshape(b, rows, GRID_W, B_HEADS, HEAD_DIM)
    col_q = jnp.arange(GRID_W).reshape(NA_NCB, NA_QC)
    col_start = jnp.clip(col_q - NA_COLS // 2, 0, GRID_W - NA_COLS)
    strip_start = jnp.clip(jnp.arange(NA_NCB) * NA_QC - NA_COLS // 2, 0, GRID_W - NA_STRIP)
    strip_cols = strip_start[:, None] + jnp.arange(NA_STRIP)[None, :]
    kc = strip_cols[:, None, :]
    col_ok = (kc >= col_start[:, :, None]) & (kc < col_start[:, :, None] + NA_COLS)
    dc_idx = jnp.clip(kc - col_q[:, :, None] + NA_COLS - 1, 0, 2 * NA_COLS - 2)
    rpb_f = rpb.astype(jnp.float32)

    def row_block(r):
        rs = jnp.clip(r - kr // 2, 0, rows - kr)
        k_rows = lax.dynamic_slice_in_dim(kg, rs, kr, axis=1)
        v_rows = lax.dynamic_slice_in_dim(vg, rs, kr, axis=1)
        k_blk = k_rows[:, :, strip_cols]
        v_blk = v_rows[:, :, strip_cols]
        q_r = lax.dynamic_index_in_dim(qg, r, axis=1, keepdims=False)
        s = jnp.einsum('bcqhd,bicjhd->bhcqij', q_r, k_blk)
        dr_idx = rs + jnp.arange(kr) - r + NA_ROWS - 1
        bias = rpb_f[:, dr_idx[None, None, :, None], dc_idx[:, :, None, :]]
        s = jnp.where(col_ok[:, :, None, :], s + bias, NEG_INF)
        p = jax.nn.softmax(s.reshape(s.shape[:4] + (kr * NA_STRIP,)), axis=-1).reshape(s.shape)
        o = jnp.einsum('bhcqij,bicjhd->bcqhd', p, v_blk)
        return o.reshape(b, GRID_W, B_HEADS * HEAD_DIM)

    out = lax.map(row_block, jnp.arange(rows))
    return jnp.moveaxis(out, 0, 1).reshape(b, L, B_HEADS * HEAD_DIM)


def _ssm_combine(left, right):
    a_l, b_l = left
    a_r, b_r = right
    return a_r * a_l, a_r * b_l + b_r


def s5_mixer(u, lam_re, lam_im, log_dt, b_re, b_im, c_re, c_im, d_skip, w_glu, b_glu):
    b, L = u.shape[0], u.shape[1]
    uf = u.astype(jnp.float32).reshape(b, L, SSM_GROUPS, SSM_GROUP_CH)

    def scan_direction(d, seq):
        lam = lax.complex(lam_re[d].astype(jnp.float32), lam_im[d].astype(jnp.float32))
        dt = jnp.exp(log_dt[d].astype(jnp.float32))[:, None]
        lam_bar = jnp.exp(lam * dt)
        b_bar = ((lam_bar - 1.0) / lam)[:, :, None] * lax.complex(b_re[d].astype(jnp.float32), b_im[d].astype(jnp.float32))
        bu = lax.complex(jnp.einsum('blgp,gnp->blgn', seq, jnp.real(b_bar)),
                         jnp.einsum('blgp,gnp->blgn', seq, jnp.imag(b_bar)))
        a = jnp.broadcast_to(lam_bar, (1, L) + lam_bar.shape)
        _, h = lax.associative_scan(_ssm_combine, (a, bu), axis=1)
        return (jnp.einsum('blgn,gpn->blgp', jnp.real(h), c_re[d].astype(jnp.float32))
                - jnp.einsum('blgn,gpn->blgp', jnp.imag(h), c_im[d].astype(jnp.float32)))

    y = (scan_direction(0, uf)
         + jnp.flip(scan_direction(1, jnp.flip(uf, axis=1)), axis=1)
         + d_skip.astype(jnp.float32).reshape(SSM_GROUPS, SSM_GROUP_CH) * uf)
    z = jax.nn.gelu(y.reshape(b, L, C_WIDTH))
    out = z * jax.nn.sigmoid(z @ w_glu.astype(jnp.float32) + b_glu.astype(jnp.float32))
    return out.astype(u.dtype)


def conv_module(a, g, conv_w, conv_b, ln_g, ln_b):
    h = a * jax.nn.sigmoid(g)
    h = lax.conv_general_dilated(h, conv_w[:, None, :], window_strides=(1,),
                                 padding=[(CONV_WIDTH // 2, CONV_WIDTH // 2)],
                                 dimension_numbers=('NWC', 'WIO', 'NWC'),
                                 feature_group_count=D_WIDTH) + conv_b
    return jax.nn.silu(layer_norm(h, ln_g, ln_b))


def attn_mixer(u, w_in, q_gain_a, k_gain_a, sink_a, q_gain_b, k_gain_b, rpb_b, w_out):
    b, L = u.shape[0], u.shape[1]
    h = u @ w_in
    qa, ka, va, qb, kb, vb = jnp.split(
        h, [A_Q, A_Q + A_KV, A_Q + 2 * A_KV, A_Q + 2 * A_KV + B_QKV, A_Q + 2 * A_KV + 2 * B_QKV], axis=-1)
    ya = window_gqa(qa.reshape(b, L, A_HEADS, HEAD_DIM), ka.reshape(b, L, A_KV_HEADS, HEAD_DIM),
                    va.reshape(b, L, A_KV_HEADS, HEAD_DIM), q_gain_a, k_gain_a, sink_a)
    yb = neighborhood_attn(qb.reshape(b, L, B_HEADS, HEAD_DIM), kb.reshape(b, L, B_HEADS, HEAD_DIM),
                           vb.reshape(b, L, B_HEADS, HEAD_DIM), q_gain_b, k_gain_b, rpb_b)
    return jnp.concatenate([ya, yb], axis=-1).astype(u.dtype) @ w_out


def ssm_conv_mixer(u, w_in, lam_re, lam_im, log_dt, b_re, b_im, c_re, c_im, d_skip, w_glu, b_glu,
                   conv_w, conv_b, ln_g, ln_b, w_out):
    h = u @ w_in
    uc, ad, gd = jnp.split(h, [C_WIDTH, C_WIDTH + D_WIDTH], axis=-1)
    yc = s5_mixer(uc, lam_re, lam_im, log_dt, b_re, b_im, c_re, c_im, d_skip, w_glu, b_glu)
    yd = conv_module(ad, gd, conv_w, conv_b, ln_g, ln_b)
    return jnp.concatenate([yc, yd.astype(yc.dtype)], axis=-1) @ w_out


def encoder_trunk(x, p, w):
    for i in range(DEPTH):
        x = x + 0.5 * swiglu(rms_norm(x, w['norm_ffn1'][i]), w['w_ffn1_in'][i], w['w_ffn1_out'][i])
        u = rms_norm(x, w['norm_mix'][i])
        if i % 2 == 0:
            j = i // 2
            x = x + attn_mixer(u, w['w_attn_in'][j], w['q_gain_a'][j], w['k_gain_a'][j], w['sink_a'][j],
                               w['q_gain_b'][j], w['k_gain_b'][j], w['rpb_b'][j], w['w_attn_out'][j])
        else:
            j = i // 2
            x = x + ssm_conv_mixer(u, w['w_ssm_in'][j], w['lam_re'][j], w['lam_im'][j], w['log_dt'][j],
                                   w['b_re'][j], w['b_im'][j], w['c_re'][j], w['c_im'][j], w['d_skip'][j],
                                   w['w_glu_c'][j], w['b_glu_c'][j], w['conv_w'][j], w['conv_b'][j],
                                   w['ln_g_d'][j], w['ln_b_d'][j], w['w_ssm_out'][j])
        x = x + 0.5 * swiglu(rms_norm(x, w['norm_ffn2'][i]), w['w_ffn2_in'][i], w['w_ffn2_out'][i])
        gate = jax.nn.sigmoid(rms_norm(x, w['norm_ple'][i]) @ w['w_ple_gate'][i])
        x = x + gate * rms_norm(p[i] @ w['w_ple_proj'][i], w['norm_ple_post'][i])
    return x


def setup_inputs(seed: int = 0) -> dict:
    key = jax.random.key(seed)
    ks = iter(jax.random.split(key, 64))
    f32 = jnp.float32

    def nrm(shape, scale):
        return jax.random.normal(next(ks), shape, f32) * scale

    def gain(shape):
        return 1.0 + nrm(shape, 0.02)

    NA, NS = N_ATTN_LAYERS, N_SSM_LAYERS
    G, N, P = SSM_GROUPS, SSM_STATE, SSM_GROUP_CH
    return {
        'x_prompt': nrm((BATCH, SEQ, D_MODEL), 1.0),
        'x_sample': nrm((DEC_BATCH, DEC_SEQ, D_MODEL), 1.0),
        'p_prompt': nrm((DEPTH, BATCH, SEQ, PLE_DIM), 1.0),
        'p_sample': nrm((DEPTH, DEC_BATCH, DEC_SEQ, PLE_DIM), 1.0),
        'norm_ffn1': gain((DEPTH, D_MODEL)),
        'w_ffn1_in': nrm((DEPTH, D_MODEL, 2 * D_FF), D_MODEL ** -0.5),
        'w_ffn1_out': nrm((DEPTH, D_FF, D_MODEL), D_FF ** -0.5),
        'norm_mix': gain((DEPTH, D_MODEL)),
        'norm_ffn2': gain((DEPTH, D_MODEL)),
        'w_ffn2_in': nrm((DEPTH, D_MODEL, 2 * D_FF), D_MODEL ** -0.5),
        'w_ffn2_out': nrm((DEPTH, D_FF, D_MODEL), D_FF ** -0.5),
        'norm_ple': gain((DEPTH, D_MODEL)),
        'w_ple_gate': nrm((DEPTH, D_MODEL, D_MODEL), D_MODEL ** -0.5),
        'w_ple_proj': nrm((DEPTH, PLE_DIM, D_MODEL), PLE_DIM ** -0.5),
        'norm_ple_post': gain((DEPTH, D_MODEL)),
        'w_attn_in': nrm((NA, D_MODEL, ATTN_IN), D_MODEL ** -0.5),
        'q_gain_a': gain((NA, HEAD_DIM)),
        'k_gain_a': gain((NA, HEAD_DIM)),
        'sink_a': nrm((NA, A_HEADS), 0.5),
        'q_gain_b': gain((NA, HEAD_DIM)),
        'k_gain_b': gain((NA, HEAD_DIM)),
        'rpb_b': nrm((NA, B_HEADS, 2 * NA_ROWS - 1, 2 * NA_COLS - 1), 0.1),
        'w_attn_out': nrm((NA, ATTN_OUT, D_MODEL), ATTN_OUT ** -0.5),
        'w_ssm_in': nrm((NS, D_MODEL, SSM_IN), D_MODEL ** -0.5),
        'lam_re': -0.5 + nrm((NS, 2, G, N), 0.01),
        'lam_im': jnp.broadcast_to(math.pi * jnp.arange(N, dtype=f32), (NS, 2, G, N)) + nrm((NS, 2, G, N), 0.01),
        'log_dt': jax.random.uniform(next(ks), (NS, 2, G), f32, math.log(1e-3), math.log(1e-1)),
        'b_re': nrm((NS, 2, G, N, P), (2 * P) ** -0.5),
        'b_im': nrm((NS, 2, G, N, P), (2 * P) ** -0.5),
        'c_re': nrm((NS, 2, G, P, N), N ** -0.5),
        'c_im': nrm((NS, 2, G, P, N), N ** -0.5),
        'd_skip': nrm((NS, C_WIDTH), 1.0),
        'w_glu_c': nrm((NS, C_WIDTH, C_WIDTH), C_WIDTH ** -0.5),
        'b_glu_c': nrm((NS, C_WIDTH), 0.02),
        'conv_w': nrm((NS, CONV_WIDTH, D_WIDTH), CONV_WIDTH ** -0.5),
        'conv_b': nrm((NS, D_WIDTH), 0.02),
        'ln_g_d': gain((NS, D_WIDTH)),
        'ln_b_d': nrm((NS, D_WIDTH), 0.02),
        'w_ssm_out': nrm((NS, SSM_OUT, D_MODEL), SSM_OUT ** -0.5),
    }


def reference(x_prompt, x_sample, p_prompt, p_sample,
              norm_ffn1, w_ffn1_in, w_ffn1_out, norm_mix, norm_ffn2, w_ffn2_in, w_ffn2_out,
              norm_ple, w_ple_gate, w_ple_proj, norm_ple_post,
              w_attn_in, q_gain_a, k_gain_a, sink_a, q_gain_b, k_gain_b, rpb_b, w_attn_out,
              w_ssm_in, lam_re, lam_im, log_dt, b_re, b_im, c_re, c_im, d_skip, w_glu_c, b_glu_c,
              conv_w, conv_b, ln_g_d, ln_b_d, w_ssm_out):
    w = dict(norm_ffn1=norm_ffn1, w_ffn1_in=w_ffn1_in, w_ffn1_out=w_ffn1_out, norm_mix=norm_mix,
             norm_ffn2=norm_ffn2, w_ffn2_in=w_ffn2_in, w_ffn2_out=w_ffn2_out,
             norm_ple=norm_ple, w_ple_gate=w_ple_gate, w_ple_proj=w_ple_proj, norm_ple_post=norm_ple_post,
             w_attn_in=w_attn_in, q_gain_a=q_gain_a, k_gain_a=k_gain_a, sink_a=sink_a,
             q_gain_b=q_gain_b, k_gain_b=k_gain_b, rpb_b=rpb_b, w_attn_out=w_attn_out,
             w_ssm_in=w_ssm_in, lam_re=lam_re, lam_im=lam_im, log_dt=log_dt, b_re=b_re, b_im=b_im,
             c_re=c_re, c_im=c_im, d_skip=d_skip, w_glu_c=w_glu_c, b_glu_c=b_glu_c,
             conv_w=conv_w, conv_b=conv_b, ln_g_d=ln_g_d, ln_b_d=ln_b_d, w_ssm_out=w_ssm_out)
    y_prompt = encoder_trunk(x_prompt, p_prompt, w)
    y_sample = encoder_trunk(x_sample, p_sample, w)
    return (y_prompt, y_sample)
```

```python
import numpy as np
import concourse.bass as bass
import concourse.mybir as mybir
from concourse.bass_utils import run_bass_kernel_spmd

F32 = mybir.dt.float32
BF16 = mybir.dt.bfloat16
AF = mybir.ActivationFunctionType
ALU = mybir.AluOpType

D_MODEL = 1024
SEQ = 4096
DEPTH = 4
D_FF = 2816
NJ = D_FF // 128
PLE_DIM = 256
TC = 512
NCH = SEQ // TC
EPS = 1e-6
N_CORES = 8
SEQ_PER_CORE = 3


class Res:
    __slots__ = ("name", "w", "rd", "psum", "co")

    def __init__(self, name="", psum=False):
        self.name = name
        self.w = None
        self.rd = []
        self.psum = psum
        self.co = []


class Ins:
    __slots__ = ("eng", "meth", "args", "kw", "deps", "pos", "sig", "val", "sem", "is_dma")


ENGS = ["tensor", "scalar", "vector", "gpsimd", "sync"]


class Sched:
    def __init__(self):
        self.by_eng = {e: [] for e in ENGS}
        self.dry = False
        self.sem_counts = {}
        self.n = 0
        self.dma_pending = []
        self.last_dma = {}

    def add(self, eng, meth, *args, rd=(), wr=(), dma_sem=None, extra=(), nowaw=False, grp=False, **kw):
        if self.dry:
            return None
        i = Ins()
        i.eng = eng
        i.meth = meth
        i.args = args
        i.kw = kw
        i.pos = len(self.by_eng[eng])
        i.sig = False
        i.val = 0
        i.sem = dma_sem
        i.is_dma = dma_sem is not None
        if i.is_dma:
            self.sem_counts[id(dma_sem)] = self.sem_counts.get(id(dma_sem), 0) + 16
            i.val = self.sem_counts[id(dma_sem)]
        raw = set()
        oth = set()
        waw = set()
        for r in rd:
            if r.w is not None:
                raw.add(r.w)
            raw.update(r.co)
            if r.psum:
                for j in r.rd:
                    if j.eng != eng:
                        oth.add(j)
        for r in wr:
            if r.w is not None and not nowaw:
                oth.add(r.w)
                oth.update(r.co)
                waw.add(r.w)
                waw.update(r.co)
            oth.update(r.rd)
        deps = []
        raw.update(extra)
        if i.is_dma:
            prev = self.last_dma.get(id(dma_sem))
            if prev is not None and not grp:
                raw.add(prev)
            self.last_dma[id(dma_sem)] = i
        for d in raw | oth:
            if d is i:
                continue
            if d.is_dma or i.is_dma:
                deps.append(d)
            elif d.eng != eng:
                deps.append(d)
            else:
                if eng != "tensor":
                    deps.append(d)
        for d in deps:
            if not d.is_dma:
                d.sig = True
        i.deps = deps
        for r in rd:
            r.rd.append(i)
        for r in wr:
            if nowaw and r.w is not None:
                r.co = r.co[-64:] + [r.w]
            else:
                r.co = []
            r.w = i
            r.rd = []
        self.by_eng[eng].append(i)
        self.n += 1
        if i.is_dma:
            self.dma_pending.append(i)
        return i

    def barrier(self):
        if self.dry:
            return
        lasts = [self.by_eng[e][-1] for e in ENGS if self.by_eng[e]]
        deps = lasts + self.dma_pending
        self.dma_pending = []
        for e in ENGS:
            self.add(e, "nop", extra=deps)

    def emit(self, nc, block, eng_sems):
        for e in ENGS:
            cnt = 0
            for i in self.by_eng[e]:
                if i.is_dma:
                    continue
                if i.sig:
                    cnt += 1
                    i.val = cnt
                    i.sem = eng_sems[e]

        def run(e, handle):
            waited = {}
            for i in self.by_eng[e]:
                need = {}
                for d in i.deps:
                    key = id(d.sem)
                    if key not in need or need[key][1] < d.val:
                        need[key] = (d.sem, d.val)
                for key, (sm, val) in need.items():
                    if waited.get(key, 0) < val:
                        handle.wait_ge(sm, val)
                        waited[key] = val
                bi = getattr(handle, i.meth)(*i.args, **i.kw)
                if i.is_dma:
                    bi.then_inc(i.sem, 16)
                elif i.sig:
                    bi.then_inc(i.sem, 1)

        @block.tensor
        def _(h):
            run("tensor", h)

        @block.scalar
        def _(h):
            run("scalar", h)

        @block.vector
        def _(h):
            run("vector", h)

        @block.gpsimd
        def _(h):
            run("gpsimd", h)

        @block.sync
        def _(h):
            run("sync", h)


def blk_in(w, ncols_blocks=None):
    K, N = w.shape
    return np.ascontiguousarray(w.reshape(K // 128, 128, N // 128, 128).transpose(2, 1, 0, 3))


BIGW = {}


def pack_weights(inp):
    out = {}
    for nm in ("ffn1", "ffn2"):
        wi = inp["w_%s_in" % nm]
        wo = inp["w_%s_out" % nm]
        a = []
        for l in range(DEPTH):
            g = blk_in(wi[l][:, :D_FF])
            u = blk_in(wi[l][:, D_FF:])
            a.append(np.stack([g, u], axis=2))
        out["w%si" % nm[-1]] = np.stack(a).reshape(DEPTH * NJ, 128, 2 * 8 * 128)
        b = [blk_in(wo[l]) for l in range(DEPTH)]
        out["w%so" % nm[-1]] = np.stack(b).reshape(DEPTH * 8, 128, NJ * 128)
    out["wpg"] = np.stack([blk_in(inp["w_ple_gate"][l]) for l in range(DEPTH)]).reshape(DEPTH * 8, 128, 1024)
    out["wpp"] = np.ascontiguousarray(
        inp["w_ple_proj"].reshape(DEPTH, 2, 128, 1024).transpose(0, 2, 1, 3)).reshape(DEPTH, 128, 2048)
    A_Q, A_KV, B_Q = 512, 128, 512
    wai, wav, wao = [], [], []
    for j in range(2):
        wi = inp["w_attn_in"][j]
        qa = wi[:, :A_Q]
        cols = []
        for cc in range(4):
            cols.append(qa[:, cc * 64:(cc + 1) * 64])
            cols.append(qa[:, (4 + cc) * 64:(5 + cc) * 64])
        cols.append(wi[:, A_Q:A_Q + A_KV])
        cols.append(wi[:, A_Q + 2 * A_KV:A_Q + 2 * A_KV + B_Q])
        cols.append(wi[:, A_Q + 2 * A_KV + B_Q:A_Q + 2 * A_KV + 2 * B_Q])
        fm = np.concatenate(cols, axis=1)
        wai.append(blk_in(fm).reshape(13, 128, 1024))
        vv = np.concatenate([wi[:, A_Q + 2 * A_KV + 2 * B_Q:], wi[:, A_Q + A_KV:A_Q + 2 * A_KV]], axis=1)
        vv = vv.reshape(2, 4, 128, 640).transpose(0, 2, 1, 3)
        wav.append(np.ascontiguousarray(vv).reshape(2, 128, 2560))
        wo = inp["w_attn_out"][j]
        rows = []
        for cc in range(4):
            rows.append(wo[cc * 64:(cc + 1) * 64])
            rows.append(wo[(4 + cc) * 64:(5 + cc) * 64])
        rows.append(wo[512:])
        wao.append(blk_in(np.concatenate(rows, axis=0)).reshape(8, 128, 1024))
    out["wai"] = np.concatenate(wai)
    out["wav"] = np.concatenate(wav)
    out["wao"] = np.concatenate(wao)
    out["wsi"] = np.concatenate([blk_in(inp["w_ssm_in"][j]).reshape(12, 128, 1024) for j in range(2)])
    out["wsg"] = np.concatenate([blk_in(inp["w_glu_c"][j]).reshape(4, 128, 512) for j in range(2)])
    out["wso"] = np.concatenate([blk_in(inp["w_ssm_out"][j]).reshape(8, 128, 1024) for j in range(2)])
    return out


def ssm_params(inp):
    G, N, Pc = 32, 64, 16
    s5A = np.zeros((2, 128, 3, 64), np.float32)
    s5B = np.zeros((2, 2, 4, 128, 5, 64), np.float32)
    s5C = np.zeros((2, 2, 4, 128, 2, 128), np.float32)
    for j in range(2):
        for d in range(2):
            lre = inp["lam_re"][j, d]
            lim = inp["lam_im"][j, d]
            ldt = inp["log_dt"][j, d]
            cols = slice(d * 32, (d + 1) * 32)
            s5A[j, :, 0, cols] = np.tile(lre.T, (2, 1))
            s5A[j, :, 1, cols] = np.tile(lim.T, (2, 1))
            s5A[j, :, 2, cols] = np.broadcast_to(ldt[None, :], (128, 32))
            for q in range(4):
                gs = slice(q * 8, (q + 1) * 8)
                s5B[j, d, q, :, 0, :] = inp["b_re"][j, d, gs].transpose(0, 2, 1).reshape(128, N)
                s5B[j, d, q, :, 1, :] = inp["b_im"][j, d, gs].transpose(0, 2, 1).reshape(128, N)
                s5B[j, d, q, :, 2, :] = np.repeat(lre[gs], Pc, axis=0)
                s5B[j, d, q, :, 3, :] = np.repeat(lim[gs], Pc, axis=0)
                s5B[j, d, q, :, 4, :] = np.repeat(ldt[gs], Pc)[:, None]
                cre = inp["c_re"][j, d, gs].transpose(2, 0, 1).reshape(N, 128)
                cim = inp["c_im"][j, d, gs].transpose(2, 0, 1).reshape(N, 128)
                s5C[j, d, q, :64, 0, :] = cre
                s5C[j, d, q, 64:, 0, :] = cim
                s5C[j, d, q, :64, 1, :] = cim
                s5C[j, d, q, 64:, 1, :] = cre
    kc32 = np.zeros((128, 2, 128), np.float32)
    kc32[:, 0, :] = np.eye(128, dtype=np.float32)
    for pp in range(64):
        kc32[pp + 64, 1, pp] = 1.0
        kc32[pp, 1, pp + 64] = -1.0
    return {"s5A": s5A, "s5B": s5B, "s5C": s5C, "kc32": kc32}


NEG = -30000.0


def attn_tables(inp):
    out = np.full((2, 5, 8, 128, 1024), NEG, np.float32)
    k = np.arange(128)[:, None]
    q = np.arange(128)[None, :]
    slopes = [2.0 ** (-(i + 1)) for i in range(8)]
    for ti, n in enumerate((0, 1, 2, 30, 31)):
        lo = min(max(n - 2, 0), 27)
        low = min(max(n - 1, 0), 29)
        qtok = n * 128 + q
        r = qtok // 64
        c = qtok % 64
        rs = np.clip(r - 4, 0, 56)
        cs_ = np.clip(c - 8, 0, 48)
        for jj in range(5):
            ktok = (lo + jj) * 128 + k
            kr = ktok // 64
            kc = ktok % 64
            ok = (kr >= rs) & (kr < rs + 8) & (kc >= cs_) & (kc < cs_ + 16)
            dr = np.clip(kr - r + 7, 0, 14)
            dc = np.clip(kc - c + 15, 0, 30)
            for j in range(2):
                for h in range(8):
                    g = inp["rpb_b"][j, h][dr, dc]
                    out[j, ti, h, :, jj * 128:(jj + 1) * 128] = np.where(ok, g, NEG)
        for jj in range(3):
            ktok = (low + jj) * 128 + k
            dist = np.abs(ktok - qtok)
            okw = dist <= 128
            for h in range(8):
                tab = np.where(okw, (-slopes[h]) * dist.astype(np.float32), NEG).astype(np.float32)
                out[:, ti, h, :, 640 + jj * 128:640 + (jj + 1) * 128] = tab
    return out


def pack_consts(inp):
    cols = []

    def gain(v):
        return v.reshape(8, 128).T

    for l in range(DEPTH):
        for nm in ("norm_ffn1", "norm_mix", "norm_ffn2", "norm_ple", "norm_ple_post"):
            cols.append(gain(inp[nm][l]))
    for j in range(2):
        for nm in ("q_gain_a", "k_gain_a", "q_gain_b", "k_gain_b"):
            cols.append(np.tile(inp[nm][j], 2)[:, None])
        cols.append(np.broadcast_to(inp["sink_a"][j][None, :], (128, 8)))
    for j in range(2):
        cw = inp["conv_w"][j]
        for q in range(4):
            cols.append(cw[:, q * 128:(q + 1) * 128].T)
        for nm in ("conv_b", "ln_g_d", "ln_b_d", "d_skip", "b_glu_c"):
            cols.append(inp[nm][j].reshape(4, 128).T)
    gm = np.zeros((128, 8), np.float32)
    for g8 in range(8):
        gm[g8 * 16:(g8 + 1) * 16, g8] = 1.0
    cols.append(gm)
    cols.append(np.full((128, 1), np.pi / 2, np.float32))
    return np.ascontiguousarray(np.concatenate(cols, axis=1).astype(np.float32))


def gcol(l, which):
    return (l * 5 + which) * 8


ACOL = DEPTH * 5 * 8
SCOL = ACOL + 24
GMC = SCOL + 288
HPIC = GMC + 8
TS = 256


class Cfg:
    nseq = SEQ_PER_CORE
    depth = DEPTH
    mixers = True
    stage = 9


def build_program(cfg, wshapes, ncst):
    nc = bass.Bass("TRN2", target_bir_lowering=False)
    NS = cfg.nseq
    xT = nc.dram_tensor("xT", [NS, 128, 8, SEQ], F32, kind="ExternalInput").ap()
    pT = nc.dram_tensor("pT", [NS, DEPTH, 128, 2, SEQ], F32, kind="ExternalInput").ap()
    yT = nc.dram_tensor("yT", [NS, 128, 8, SEQ], F32, kind="ExternalOutput").ap()
    cst_d = nc.dram_tensor("cst", [128, ncst], F32, kind="ExternalInput").ap()
    wf = {}
    wb = {}
    for nm, (nb, fsz) in wshapes.items():
        wf[nm] = nc.dram_tensor(nm, [nb, 128, fsz], F32, kind="ExternalInput").ap()
        wb[nm] = nc.dram_tensor(nm + "_bf", [nb, 128, fsz], BF16, kind="Internal").ap()

    ball = nc.dram_tensor("ball", [2, 5, 8, 128, 1024], F32, kind="ExternalInput").ap()
    qkT = nc.dram_tensor("qkT_s", [13, 128, SEQ], BF16, kind="Internal").ap()
    vS = nc.dram_tensor("vS_s", [32, 128, 640], BF16, kind="Internal").ap()
    ymix = nc.dram_tensor("ymix_s", [128, 8, SEQ], BF16, kind="Internal").ap()

    s5A_d = nc.dram_tensor("s5A", [2, 128, 3, 64], F32, kind="ExternalInput").ap()
    s5B_d = nc.dram_tensor("s5B", [2, 2, 4, 128, 5, 64], F32, kind="ExternalInput").ap()
    s5C_d = nc.dram_tensor("s5C", [2, 2, 4, 128, 2, 128], F32, kind="ExternalInput").ap()
    kc32_d = nc.dram_tensor("kc32", [128, 2, 128], F32, kind="ExternalInput").ap()
    uS = nc.dram_tensor("uS_s", [4, 128, SEQ], BF16, kind="Internal").ap()
    hhS = nc.dram_tensor("hhS_s", [4, 128, SEQ + 32], BF16, kind="Internal").ap()
    zS = nc.dram_tensor("zS_s", [4, 128, SEQ], BF16, kind="Internal").ap()
    tabS = nc.dram_tensor("tabS_s", [2, 2, 32, 128, 2 * TS], F32, kind="Internal").ap()
    s5W = nc.dram_tensor("s5W_s", [2, 2, 4, 128, 8 * 4 * 128], BF16, kind="Internal").ap()
    dgS = nc.dram_tensor("dgS_s", [2, 4, 2, 128, 2048], BF16, kind="Internal").ap()

    D = 4
    WSLOT = 3072
    S = Sched()
    import contextlib
    with contextlib.ExitStack() as es:
        def sb(name, shape, dt):
            return es.enter_context(nc.sbuf_tensor(name, shape, dt))

        x = sb("x", [128, 8, SEQ], F32)
        wsl = sb("wsl", [128, D, WSLOT], BF16)
        xn = sb("xn", [128, 8, TC], BF16)
        hb = sb("hb", [128, NJ, TC], BF16)
        sq = sb("sq", [128, 2, TC], BF16)
        sg = sb("sg", [128, 2, TC], F32)
        pb = sb("pb", [128, 2, 2, TC], BF16)
        cst = sb("cstt", [128, ncst], F32)
        ones = sb("ones", [128, 128], BF16)
        bones = sb("bones", [128, 128], BF16)
        esink = sb("esink", [128, 16], F32)
        stA = sb("stA", [128, 2, TC], BF16)
        vst = sb("vst", [128, 2, 640], BF16)
        ident = sb("ident", [128, 128], BF16)
        swm = sb("swm", [128, 128], F32)
        s5p = sb("s5p", [128, 2, 3, 64], F32)
        s5wt = sb("s5wt", [128, 4, 4, 128], BF16)
        stc = sb("stc", [128, 16], F32)
        rhoT = sb("rhoT", [128, TS], F32)
        zero16 = sb("zero16", [128, 16], BF16)
        P = [es.enter_context(nc.psum_tensor("ps%d" % i, [128, TC], F32)) for i in range(8)]
        nsem = 0

        def sem(name):
            return es.enter_context(nc.semaphore(name))

        eng_sems = {e: sem("e_" + e) for e in ENGS}
        w_sems = [sem("w%d" % i) for i in range(D)]
        x_sems = [sem("x%d" % i) for i in range(NCH)]
        p_sems = [sem("p%d" % i) for i in range(2)]
        c_sem = sem("cst")
        cv_sems = [sem("cv%d" % i) for i in range(2)]
        cvs_sems = [sem("cvs%d" % i) for i in range(2)]
        sta_sems = [sem("sta%d" % i) for i in range(2)]
        vst_sems = [sem("vst%d" % i) for i in range(2)]
        qt_sems = [sem("qt%d" % i) for i in range(2)]
        kv_sems = [sem("kv%d" % i) for i in range(3)]
        bi_sems = [sem("bi%d" % i) for i in range(2)]
        ys_sems = [sem("ys%d" % i) for i in range(2)]
        ym_sem = sem("ym")
        pr_sems = [sem("pr%d" % i) for i in range(6)]
        tb_sems = [sem("tb%d" % i) for i in range(2)]
        s5_sems = [sem("s5_%d" % i) for i in range(4)]
        zs_sems = [sem("zs%d" % i) for i in range(2)]
        block = es.enter_context(nc.Block())

        hb32 = hb[:].rearrange("p j t -> p (j t)").bitcast(F32)
        hbf = hb[:].rearrange("p j t -> p (j t)")
        Vt = hbf[:, 0:3200].rearrange("p (t f) -> p t f", t=5)
        kbT = hbf[:, 3200:5760].rearrange("p (c t) -> p c t", c=4)
        kaT = hbf[:, 5760:6400]
        pTt = hbf[:, 6400:7680].rearrange("p (u t) -> p u t", u=2)
        et = hbf[:, 7680:10240].bitcast(F32).rearrange("p (u t) -> p u t", u=2)
        biast = xn[:].rearrange("p k t -> p (k t)").bitcast(F32).rearrange("p (u t) -> p u t", u=2)
        qtt = sg[:].rearrange("p u t -> p (u t)").bitcast(BF16).rearrange("p (u c t) -> p u c t", u=2, c=8)
        ystt = pb[:].rearrange("p a b t -> p (a b t)").rearrange("p (u c t) -> p u c t", u=2, c=8)
        dent = sq[:].rearrange("p u t -> p (u t)").bitcast(F32).rearrange("p (u t) -> p u t", u=2)
        sgf = sg[:].rearrange("p u t -> p (u t)")
        pbf = pb[:].rearrange("p a b t -> p (a b t)")
        tab_ap = [sgf[:, 0:512], sgf[:, 512:1024], pbf[:, 0:1024].bitcast(F32), pbf[:, 1024:2048].bitcast(F32)]
        tab_ap = [t.rearrange("p (c t) -> p c t", c=2) for t in tab_ap]
        yq = hbf[:, 0:8192].bitcast(F32)
        rot2 = hbf[:, 8192:9216].bitcast(F32).rearrange("p (u t) -> p u t", u=2)
        rot512 = hbf[:, 8192:9216].bitcast(F32)
        gt2 = hbf[:, 9216:10240].bitcast(F32).rearrange("p (u t) -> p u t", u=2)
        gt512 = hbf[:, 9216:10240].bitcast(F32)
        G4 = hbf[:, 10240:11264].rearrange("p (u t) -> p u t", u=4)
        g512 = hbf[:, 10240:11264].rearrange("p (u t) -> p u t", u=2)
        uTq = xn[:].rearrange("p k t -> p (k t)")
        hhwin = hbf[:, 0:2176].rearrange("p (q t) -> p q t", q=4)
        hcv = hbf[:, 2176:6272].bitcast(F32).rearrange("p (q t) -> p q t", q=4)
        tmpc = hbf[:, 6272:8320].bitcast(F32).rearrange("p (u t) -> p u t", u=2)
        msq = hbf[:, 8320:9344].bitcast(F32)
        ystc = pbf.rearrange("p (q t) -> p q t", q=4)
        wkA = x[:, 2, :]
        wkT = x[:, 3, :]
        wkB = x[:, 4, :]
        wkC = x[:, 5, :]
        wkW = x[:, 6, :].bitcast(BF16)
        wkD = x[:, 7, :].bitcast(BF16)

        def record():
            xr = [[Res("x%d_%d" % (c, k)) for k in range(8)] for c in range(NCH)]
            xn_r = [Res() for _ in range(8)]
            h_r = [Res() for _ in range(NJ)]
            sq_r = [Res(), Res()]
            sg_r = [Res(), Res()]
            pb_r = [Res(), Res()]
            ps_r = [Res(psum=True) for _ in range(8)]
            ws_r = [Res() for _ in range(D)]
            cst_r = Res()
            wb_r = Res()
            sta_r = [Res(), Res()]
            vst_r = [Res(), Res()]
            qkT_r = Res()
            vS_r = Res()
            ym_r = Res()
            qt_r = [Res(), Res()]
            ka_r, kb_r, v_r = Res(), Res(), Res()
            bias_r = [Res(), Res()]
            e_r = [Res(), Res()]
            pT_r = [Res(), Res()]
            yst_r = [Res(), Res()]
            den_r = [Res(), Res()]

            class WS:
                reqs = []
                n = 0
                issued = 0

            if S.dry:
                WS.reqs = []
            else:
                WS.reqs = record.reqs

            def wget(dram_ap, fsz):
                if S.dry:
                    WS.reqs.append((dram_ap, fsz))
                    return wsl[:, 0, :fsz], ws_r[0]
                k = WS.n
                WS.n += 1
                while WS.issued < min(len(WS.reqs), k + D - 1):
                    m = WS.issued
                    ap_m, f_m = WS.reqs[m]
                    S.add("sync", "dma_start", out=wsl[:, m % D, :f_m], in_=ap_m,
                          rd=[wb_r], wr=[ws_r[m % D]], dma_sem=w_sems[m % D])
                    WS.issued += 1
                return wsl[:, k % D, :fsz], ws_r[k % D]

            S.add("gpsimd", "dma_start", out=cst[:], in_=cst_d, wr=[cst_r], dma_sem=c_sem)
            S.add("vector", "memset", ones[:], 1.0, wr=[cst_r])
            S.add("vector", "memset", bones[:], 0.0, wr=[cst_r])
            S.add("vector", "memset", bones[0:64, 0:64], 1.0, wr=[cst_r])
            S.add("vector", "memset", bones[64:128, 64:128], 1.0, wr=[cst_r])
            for j in range(2):
                S.add("scalar", "activation", out=esink[:, j * 8:(j + 1) * 8],
                      in_=cst[:, ACOL + j * 12 + 4:ACOL + j * 12 + 12], func=AF.Exp, rd=[cst_r], wr=[cst_r])
            xb = x[:].rearrange("p k t -> p (k t)").bitcast(BF16)
            stg_r = [Res(), Res()]
            last_st = [None, None]
            nconv = 0
            for nm, (nb, fsz) in wshapes.items():
                if fsz <= 2048:
                    G = 1
                    for g in (8, 4, 2, 1):
                        if g * fsz <= 8192 and nb % g == 0:
                            G = g
                            break
                    sub = None
                else:
                    G = 1
                    sub = 2
                    while fsz // sub > 2048 or fsz % sub:
                        sub += 1
                for b0 in range(0, nb, G):
                    k = nconv % 2
                    nconv += 1
                    st = xb[:, k * 8192: k * 8192 + G * fsz]
                    if sub is None:
                        o = st.rearrange("p (g f) -> p g f", g=G)
                        i_ = wf[nm][b0:b0 + G].rearrange("g p f -> p g f")
                        so = wb[nm][b0:b0 + G].rearrange("g p f -> p g f")
                    else:
                        o = st.rearrange("p (a f) -> p a f", a=sub)
                        i_ = wf[nm][b0].rearrange("p (a f) -> p a f", a=sub)
                        so = wb[nm][b0].rearrange("p (a f) -> p a f", a=sub)
                    S.add("gpsimd", "dma_start", out=o, in_=i_, wr=[stg_r[k]], dma_sem=cv_sems[k])
                    last_st[k] = S.add("sync", "dma_start", out=so, in_=o, rd=[stg_r[k]], dma_sem=cvs_sems[k])
            lst = [] if S.dry else [i for i in last_st if i is not None]
            S.add("sync", "nop", wr=[wb_r], extra=lst)


            def V(meth, *args, rd=(), wr=(), **kw):
                return S.add("vector", meth, *args, rd=rd, wr=wr, **kw)

            def A(meth, *args, rd=(), wr=(), **kw):
                return S.add("scalar", meth, *args, rd=rd, wr=wr, **kw)

            def T(*args, rd=(), wr=(), **kw):
                return S.add("tensor", "matmul", *args, rd=rd, wr=wr, **kw)

            def G(out, in_, rd=(), wr=(), sem=None, **kw):
                return S.add("gpsimd", "dma_start", out=out, in_=in_, rd=rd, wr=wr, dma_sem=sem, **kw)

            def dbl(c_in, s_in, c_out, s_out, t1, t2, r):
                V("tensor_tensor", out=t1, in0=c_in, in1=s_in, op=ALU.mult, rd=[r], wr=[r])
                V("tensor_tensor", out=t2, in0=s_in, in1=s_in, op=ALU.mult, rd=[r], wr=[r])
                V("tensor_tensor", out=c_out, in0=c_in, in1=c_in, op=ALU.mult, rd=[r], wr=[r])
                V("tensor_tensor", out=c_out, in0=c_out, in1=t2, op=ALU.subtract, rd=[r], wr=[r])
                V("tensor_scalar", out=s_out, in0=t1, scalar1=2.0, scalar2=None, op0=ALU.mult, rd=[r], wr=[r])

            def cis(th, c, s_, t1, t2, r):
                A("activation", out=s_, in_=th, func=AF.Sin, scale=0.125, rd=[r, cst_r], wr=[r])
                A("activation", out=c, in_=th, func=AF.Sin, scale=-0.125, bias=cst[:, HPIC:HPIC + 1],
                  rd=[r, cst_r], wr=[r])
                for _ in range(3):
                    dbl(c, s_, c, s_, t1, t2, r)

            def ssm_prologue():
                rA, rB, rC = Res(), Res(), Res()
                rT = [Res(), Res()]
                rW = [Res(), Res()]
                rD = [Res(), Res()]
                S.add("gpsimd", "dma_start", out=ident[:], in_=kc32_d[:, 0, :], wr=[cst_r], dma_sem=pr_sems[0])
                S.add("gpsimd", "dma_start", out=swm[:], in_=kc32_d[:, 1, :], wr=[cst_r], dma_sem=pr_sems[0], nowaw=True, grp=True)
                V("memset", zero16[:], 0.0, wr=[rC])
                for q in range(4):
                    S.add("gpsimd", "dma_start", out=hhS[q, :, 0:16], in_=zero16[:], rd=[rC], dma_sem=pr_sems[1])
                    S.add("gpsimd", "dma_start", out=hhS[q, :, SEQ + 16:SEQ + 32], in_=zero16[:], rd=[rC],
                          dma_sem=pr_sems[1])

                def a64(i):
                    return wkA[:, i * 64:(i + 1) * 64]

                def b64(i):
                    return wkB[:, i * 64:(i + 1) * 64]

                for j in range(2):
                    A3 = wkA[:, 0:192].rearrange("p (a b) -> p a b", a=3)
                    S.add("gpsimd", "dma_start", out=A3, in_=s5A_d[j], wr=[rA], dma_sem=pr_sems[2])
                    dt, th, t1, t2 = a64(3), a64(4), a64(5), a64(6)
                    CK = [a64(8 + k) for k in range(9)]
                    SK = [a64(17 + k) for k in range(9)]
                    A("activation", out=dt, in_=A3[:, 2, :], func=AF.Exp, rd=[rA], wr=[rA])
                    V("tensor_tensor", out=t1, in0=A3[:, 0, :], in1=dt, op=ALU.mult, rd=[rA], wr=[rA])
                    A("activation", out=s5p[:, j, 0, :], in_=t1, func=AF.Exp, rd=[rA], wr=[rA])
                    V("tensor_tensor", out=th, in0=A3[:, 1, :], in1=dt, op=ALU.mult, rd=[rA], wr=[rA])
                    cis(th, CK[0], SK[0], t1, t2, rA)
                    for k in range(8):
                        dbl(CK[k], SK[k], CK[k + 1], SK[k + 1], t1, t2, rA)
                    V("tensor_copy", out=s5p[:, j, 1, :], in_=CK[8], rd=[rA], wr=[rA])
                    V("tensor_copy", out=s5p[:, j, 2, :], in_=SK[8], rd=[rA], wr=[rA])
                    for dg in range(64):
                        k2 = dg % 2
                        base = k2 * 1024
                        tc_ = wkT[:, base:base + TS]
                        ts_ = wkT[:, base + TS:base + 2 * TS]
                        u1 = wkT[:, base + 2 * TS:base + 2 * TS + 128]
                        r = rT[k2]
                        V("tensor_copy", out=tc_[:, 0:1], in_=CK[0][:, dg:dg + 1], rd=[rA], wr=[r])
                        V("tensor_copy", out=ts_[:, 0:1], in_=SK[0][:, dg:dg + 1], rd=[rA], wr=[r])
                        for k in range(8):
                            n = 1 << k
                            ck = CK[k][:, dg:dg + 1]
                            sk = SK[k][:, dg:dg + 1]
                            V("tensor_scalar", out=u1[:, 0:n], in0=ts_[:, 0:n], scalar1=sk, scalar2=None, op0=ALU.mult,
                              rd=[r, rA], wr=[r])
                            V("scalar_tensor_tensor", out=tc_[:, n:2 * n], in0=tc_[:, 0:n], scalar=ck, in1=u1[:, 0:n],
                              op0=ALU.mult, op1=ALU.subtract, rd=[r, rA], wr=[r])
                            V("tensor_scalar", out=u1[:, 0:n], in0=ts_[:, 0:n], scalar1=ck, scalar2=None, op0=ALU.mult,
                              rd=[r, rA], wr=[r])
                            V("scalar_tensor_tensor", out=ts_[:, n:2 * n], in0=tc_[:, 0:n], scalar=sk, in1=u1[:, 0:n],
                              op0=ALU.mult, op1=ALU.add, rd=[r, rA], wr=[r])
                        S.add("gpsimd", "dma_start", out=tabS[j, dg // 32, dg % 32], in_=wkT[:, base:base + 2 * TS],
                              rd=[r], dma_sem=tb_sems[k2])
                    nw = 0
                    for d in range(2):
                        for q in range(4):
                            B5 = wkB[:, 0:320].rearrange("p (a b) -> p a b", a=5)
                            S.add("gpsimd", "dma_start", out=B5, in_=s5B_d[j, d, q], wr=[rB], dma_sem=pr_sems[3])
                            bre, bim, lre, lim, ldt = (B5[:, i, :] for i in range(5))
                            dtb, thb, c1, s1, u1, u2, rho, lbr, lbi, den, cr, ci = (b64(5 + i) for i in range(12))
                            BB = wkB[:, 1280:1408]
                            BBs = wkB[:, 1408:1536]
                            A("activation", out=dtb, in_=ldt, func=AF.Exp, rd=[rB], wr=[rB])
                            V("tensor_tensor", out=u1, in0=lre, in1=dtb, op=ALU.mult, rd=[rB], wr=[rB])
                            A("activation", out=rho, in_=u1, func=AF.Exp, rd=[rB], wr=[rB])
                            V("tensor_tensor", out=thb, in0=lim, in1=dtb, op=ALU.mult, rd=[rB], wr=[rB])
                            cis(thb, c1, s1, u1, u2, rB)
                            V("tensor_tensor", out=lbr, in0=rho, in1=c1, op=ALU.mult, rd=[rB], wr=[rB])
                            V("tensor_tensor", out=lbi, in0=rho, in1=s1, op=ALU.mult, rd=[rB], wr=[rB])
                            V("tensor_scalar", out=lbr, in0=lbr, scalar1=-1.0, scalar2=None, op0=ALU.add, rd=[rB], wr=[rB])
                            V("tensor_tensor", out=den, in0=lre, in1=lre, op=ALU.mult, rd=[rB], wr=[rB])
                            V("tensor_tensor", out=u1, in0=lim, in1=lim, op=ALU.mult, rd=[rB], wr=[rB])
                            V("tensor_tensor", out=den, in0=den, in1=u1, op=ALU.add, rd=[rB], wr=[rB])
                            V("reciprocal", out=den, in_=den, rd=[rB], wr=[rB])
                            V("tensor_tensor", out=cr, in0=lbr, in1=lre, op=ALU.mult, rd=[rB], wr=[rB])
                            V("tensor_tensor", out=u1, in0=lbi, in1=lim, op=ALU.mult, rd=[rB], wr=[rB])
                            V("tensor_tensor", out=cr, in0=cr, in1=u1, op=ALU.add, rd=[rB], wr=[rB])
                            V("tensor_tensor", out=cr, in0=cr, in1=den, op=ALU.mult, rd=[rB], wr=[rB])
                            V("tensor_tensor", out=ci, in0=lbi, in1=lre, op=ALU.mult, rd=[rB], wr=[rB])
                            V("tensor_tensor", out=u1, in0=lbr, in1=lim, op=ALU.mult, rd=[rB], wr=[rB])
                            V("tensor_tensor", out=ci, in0=ci, in1=u1, op=ALU.subtract, rd=[rB], wr=[rB])
                            V("tensor_tensor", out=ci, in0=ci, in1=den, op=ALU.mult, rd=[rB], wr=[rB])
                            V("tensor_tensor", out=BB[:, 0:64], in0=cr, in1=bre, op=ALU.mult, rd=[rB], wr=[rB])
                            V("tensor_tensor", out=u1, in0=ci, in1=bim, op=ALU.mult, rd=[rB], wr=[rB])
                            V("tensor_tensor", out=BB[:, 0:64], in0=BB[:, 0:64], in1=u1, op=ALU.subtract, rd=[rB], wr=[rB])
                            V("tensor_tensor", out=BB[:, 64:128], in0=cr, in1=bim, op=ALU.mult, rd=[rB], wr=[rB])
                            V("tensor_tensor", out=u1, in0=ci, in1=bre, op=ALU.mult, rd=[rB], wr=[rB])
                            V("tensor_tensor", out=BB[:, 64:128], in0=BB[:, 64:128], in1=u1, op=ALU.add, rd=[rB], wr=[rB])
                            V("tensor_copy", out=BBs[:, 0:64], in_=BB[:, 64:128], rd=[rB], wr=[rB])
                            V("tensor_scalar", out=BBs[:, 64:128], in0=BB[:, 0:64], scalar1=-1.0, scalar2=None,
                              op0=ALU.mult, rd=[rB], wr=[rB])
                            C2 = wkC[:, 0:256].rearrange("p (a b) -> p a b", a=2)
                            S.add("gpsimd", "dma_start", out=C2, in_=s5C_d[j, d, q], wr=[rC], dma_sem=pr_sems[4])
                            V("tensor_scalar", out=C2[64:128, 0, :], in0=C2[64:128, 0, :], scalar1=-1.0, scalar2=None,
                              op0=ALU.mult, rd=[rC], wr=[rC])
                            V("tensor_scalar", out=C2[:, 1, :], in0=C2[:, 1, :], scalar1=-1.0, scalar2=None,
                              op0=ALU.mult, rd=[rC], wr=[rC])
                            k2 = nw % 2
                            nw += 1
                            W8 = wkW[:, k2 * 4096:(k2 + 1) * 4096].rearrange("p (g a n) -> p g a n", g=8, a=4)
                            V("memset", wkW[:, k2 * 4096:(k2 + 1) * 4096], 0.0, wr=[rW[k2]])
                            for g8 in range(8):
                                gm = cst[:, GMC + g8:GMC + g8 + 1]
                                V("tensor_scalar", out=W8[:, g8, 0, :], in0=BB, scalar1=gm, scalar2=None, op0=ALU.mult,
                                  rd=[rB, cst_r, rW[k2]], wr=[rW[k2]])
                                V("tensor_scalar", out=W8[:, g8, 1, :], in0=BBs, scalar1=gm, scalar2=None, op0=ALU.mult,
                                  rd=[rB, cst_r, rW[k2]], wr=[rW[k2]])
                                cs16 = slice(16 * g8, 16 * g8 + 16)
                                V("tensor_copy", out=W8[:, g8, 2, cs16], in_=C2[:, 0, cs16], rd=[rC, rW[k2]], wr=[rW[k2]])
                                V("tensor_copy", out=W8[:, g8, 3, cs16], in_=C2[:, 1, cs16], rd=[rC, rW[k2]], wr=[rW[k2]])
                            S.add("gpsimd", "dma_start", out=s5W[j, d, q], in_=wkW[:, k2 * 4096:(k2 + 1) * 4096],
                                  rd=[rW[k2]], dma_sem=s5_sems[k2])
                    nd = 0
                    for q in range(4):
                        for half in range(2):
                            k2 = nd % 2
                            nd += 1
                            st_ = wkD[:, k2 * 2048:(k2 + 1) * 2048].rearrange("p (k n) -> p k n", k=16)
                            if half == 1:
                                V("memset", st_[:, 15, :], 0.0, wr=[rD[k2]])
                            for kk in range(16 if half == 0 else 15):
                                col = SCOL + j * 144 + q * 31 + half * 16 + kk
                                V("tensor_scalar", out=st_[:, kk, :], in0=ident[:], scalar1=cst[:, col:col + 1],
                                  scalar2=None, op0=ALU.mult, rd=[cst_r, rD[k2]], wr=[rD[k2]])
                            S.add("gpsimd", "dma_start", out=dgS[j, q, half], in_=wkD[:, k2 * 2048:(k2 + 1) * 2048],
                                  rd=[rD[k2]], dma_sem=s5_sems[2 + k2])

            if cfg.mixers and cfg.depth > 1:
                ssm_prologue()
            S.barrier()

            def rmsnorm(c, gc):
                cs = slice(c * TC, (c + 1) * TC)
                for kc in range(8):
                    S.add("scalar", "activation", out=sq[:, kc % 2, :], in_=x[:, kc, cs], func=AF.Square,
                          rd=[xr[c][kc]], wr=[sq_r[kc % 2]])
                    S.add("tensor", "matmul", P[6][:], lhsT=ones[:], rhs=sq[:, kc % 2, :],
                          start=(kc == 0), stop=(kc == 7), rd=[sq_r[kc % 2], cst_r], wr=[ps_r[6]])
                S.add("scalar", "activation", out=P[7][:], in_=P[6][:], func=AF.Sqrt, scale=1.0 / D_MODEL,
                      bias=cst[:, EPSC:EPSC + 1], rd=[ps_r[6], cst_r], wr=[ps_r[7]])
                S.add("vector", "reciprocal", out=P[7][:], in_=P[7][:], rd=[ps_r[7]], wr=[ps_r[7]])
                for kc in range(8):
                    S.add("vector", "scalar_tensor_tensor", out=xn[:, kc, :], in0=x[:, kc, cs],
                          scalar=cst[:, gc + kc:gc + kc + 1], in1=P[7][:], op0=ALU.mult, op1=ALU.mult,
                          rd=[xr[c][kc], ps_r[7], cst_r], wr=[xn_r[kc]])

            def ffn(c, l, which):
                cs = slice(c * TC, (c + 1) * TC)
                nm = "w1" if which == 0 else "w2"
                rmsnorm(c, gcol(l, 0 if which == 0 else 2))
                for j in range(NJ):
                    w, wr_ = wget(wb[nm + "i"][l * NJ + j], 2048)
                    wv = w.rearrange("p (g k n) -> p g k n", g=2, k=8)
                    pg = 2 * (j % 2)
                    pu = pg + 1
                    for kc in range(8):
                        S.add("tensor", "matmul", P[pg][:], lhsT=wv[:, 0, kc, :], rhs=xn[:, kc, :],
                              start=(kc == 0), stop=(kc == 7), rd=[wr_, xn_r[kc]], wr=[ps_r[pg]])
                    for kc in range(8):
                        S.add("tensor", "matmul", P[pu][:], lhsT=wv[:, 1, kc, :], rhs=xn[:, kc, :],
                              start=(kc == 0), stop=(kc == 7), rd=[wr_, xn_r[kc]], wr=[ps_r[pu]])
                    S.add("scalar", "activation", out=sg[:, j % 2, :], in_=P[pg][:], func=AF.Silu,
                          rd=[ps_r[pg]], wr=[sg_r[j % 2]])
                    S.add("vector", "tensor_tensor", out=hb[:, j, :], in0=sg[:, j % 2, :], in1=P[pu][:],
                          op=ALU.mult, rd=[sg_r[j % 2], ps_r[pu]], wr=[h_r[j]])
                for oc in range(8):
                    w, wr_ = wget(wb[nm + "o"][l * 8 + oc], NJ * 128)
                    wv = w.rearrange("p (h n) -> p h n", h=NJ)
                    po = 4 + oc % 2
                    for hc in range(NJ):
                        S.add("tensor", "matmul", P[po][:], lhsT=wv[:, hc, :], rhs=hb[:, hc, :],
                              start=(hc == 0), stop=(hc == NJ - 1), rd=[wr_, h_r[hc]], wr=[ps_r[po]])
                    S.add("vector", "scalar_tensor_tensor", out=x[:, oc, cs], in0=P[po][:], scalar=0.5,
                          in1=x[:, oc, cs], op0=ALU.mult, op1=ALU.add,
                          rd=[ps_r[po], xr[c][oc]], wr=[xr[c][oc]])

            def ple(s, c, l):
                cs = slice(c * TC, (c + 1) * TC)
                k = ple.n % 2
                ple.n += 1
                S.add("gpsimd", "dma_start", out=pb[:, k, :, :], in_=pT[s, l, :, :, cs],
                      wr=[pb_r[k]], dma_sem=p_sems[k])
                rmsnorm(c, gcol(l, 3))
                w, wr_ = wget(wb["wpp"][l], 2048)
                wv = w.rearrange("p (k n) -> p k n", k=2)
                for oc in range(8):
                    pp = oc % 2
                    for kc in range(2):
                        S.add("tensor", "matmul", P[pp][:], lhsT=wv[:, kc, oc * 128:(oc + 1) * 128],
                              rhs=pb[:, k, kc, :], start=(kc == 0), stop=(kc == 1),
                              rd=[wr_, pb_r[k]], wr=[ps_r[pp]])
                    S.add("vector", "tensor_copy", out=hb32[:, oc * TC:(oc + 1) * TC], in_=P[pp][:],
                          rd=[ps_r[pp]], wr=[h_r[2 * oc], h_r[2 * oc + 1]])
                    S.add("scalar", "activation", out=sq[:, oc % 2, :], in_=P[pp][:], func=AF.Square,
                          rd=[ps_r[pp]], wr=[sq_r[oc % 2]])
                    S.add("tensor", "matmul", P[6][:], lhsT=ones[:], rhs=sq[:, oc % 2, :],
                          start=(oc == 0), stop=(oc == 7), rd=[sq_r[oc % 2], cst_r], wr=[ps_r[6]])
                S.add("scalar", "activation", out=P[7][:], in_=P[6][:], func=AF.Sqrt, scale=1.0 / D_MODEL,
                      bias=cst[:, EPSC:EPSC + 1], rd=[ps_r[6], cst_r], wr=[ps_r[7]])
                S.add("vector", "reciprocal", out=P[7][:], in_=P[7][:], rd=[ps_r[7]], wr=[ps_r[7]])
                gp = gcol(l, 4)
                for oc in range(8):
                    w, wr_ = wget(wb["wpg"][l * 8 + oc], 1024)
                    wv = w.rearrange("p (k n) -> p k n", k=8)
                    pg = 2 + oc % 2
                    for kc in range(8):
                        S.add("tensor", "matmul", P[pg][:], lhsT=wv[:, kc, :], rhs=xn[:, kc, :],
                              start=(kc == 0), stop=(kc == 7), rd=[wr_, xn_r[kc]], wr=[ps_r[pg]])
                    S.add("scalar", "activation", out=P[pg][:], in_=P[pg][:], func=AF.Sigmoid,
                          rd=[ps_r[pg]], wr=[ps_r[pg]])
                    t = sg[:, oc % 2, :]
                    S.add("vector", "scalar_tensor_tensor", out=t, in0=hb32[:, oc * TC:(oc + 1) * TC],
                          scalar=cst[:, gp + oc:gp + oc + 1], in1=P[7][:], op0=ALU.mult, op1=ALU.mult,
                          rd=[h_r[2 * oc], h_r[2 * oc + 1], ps_r[7], cst_r], wr=[sg_r[oc % 2]])
                    S.add("vector", "tensor_tensor", out=t, in0=t, in1=P[pg][:], op=ALU.mult,
                          rd=[sg_r[oc % 2], ps_r[pg]], wr=[sg_r[oc % 2]])
                    S.add("vector", "tensor_tensor", out=x[:, oc, cs], in0=x[:, oc, cs], in1=t, op=ALU.add,
                          rd=[sg_r[oc % 2], xr[c][oc]], wr=[xr[c][oc]])

            ple.n = 0

            def attn_in(c, l):
                j = l // 2
                cs = slice(c * TC, (c + 1) * TC)
                rmsnorm(c, gcol(l, 1))
                for pc in range(13):
                    w, wr_ = wget(wb["wai"][j * 13 + pc], 1024)
                    wv = w.rearrange("p (k n) -> p k n", k=8)
                    pp = pc % 2
                    k = attn_in.n % 2
                    attn_in.n += 1
                    for kc in range(8):
                        S.add("tensor", "matmul", P[pp][:], lhsT=wv[:, kc, :], rhs=xn[:, kc, :],
                              start=(kc == 0), stop=(kc == 7), rd=[wr_, xn_r[kc]], wr=[ps_r[pp]])
                    S.add("scalar", "activation", out=sq[:, k, :], in_=P[pp][:], func=AF.Square,
                          rd=[ps_r[pp]], wr=[sq_r[k]])
                    S.add("tensor", "matmul", P[2 + pp][:], lhsT=bones[:], rhs=sq[:, k, :], start=True, stop=True,
                          rd=[sq_r[k], cst_r], wr=[ps_r[2 + pp]])
                    S.add("scalar", "activation", out=P[2 + pp][:], in_=P[2 + pp][:], func=AF.Sqrt, scale=1.0 / 64,
                          bias=cst[:, EPSC:EPSC + 1], rd=[ps_r[2 + pp], cst_r], wr=[ps_r[2 + pp]])
                    S.add("vector", "reciprocal", out=sg[:, k, :], in_=P[2 + pp][:], rd=[ps_r[2 + pp]], wr=[sg_r[k]])
                    gi = 0 if pc < 4 else 1 if pc == 4 else 2 if pc < 9 else 3
                    gc = ACOL + j * 12 + gi
                    S.add("vector", "scalar_tensor_tensor", out=stA[:, k, :], in0=P[pp][:], scalar=cst[:, gc:gc + 1],
                          in1=sg[:, k, :], op0=ALU.mult, op1=ALU.mult,
                          rd=[ps_r[pp], sg_r[k], cst_r], wr=[sta_r[k]])
                    S.add("gpsimd", "dma_start", out=qkT[pc, :, cs], in_=stA[:, k, :], rd=[sta_r[k]], wr=[qkT_r],
                          dma_sem=sta_sems[k], nowaw=True)
                w0, wr0 = wget(wb["wav"][j * 2 + 0], 2560)
                w1, wr1 = wget(wb["wav"][j * 2 + 1], 2560)
                wvh = [w0.rearrange("p (k f) -> p k f", k=4), w1.rearrange("p (k f) -> p k f", k=4)]
                wrh = [wr0, wr1]
                for tt in range(4):
                    k = attn_in.nv % 2
                    attn_in.nv += 1
                    for kc in range(8):
                        S.add("tensor", "matmul", P[4][:], lhsT=xn[:, kc, tt * 128:(tt + 1) * 128],
                              rhs=wvh[kc // 4][:, kc % 4, 0:512], start=(kc == 0), stop=(kc == 7),
                              rd=[wrh[kc // 4], xn_r[kc]], wr=[ps_r[4]])
                    for kc in range(8):
                        S.add("tensor", "matmul", P[5][:, 0:128], lhsT=xn[:, kc, tt * 128:(tt + 1) * 128],
                              rhs=wvh[kc // 4][:, kc % 4, 512:640], start=(kc == 0), stop=(kc == 7),
                              rd=[wrh[kc // 4], xn_r[kc]], wr=[ps_r[5]])
                    S.add("scalar", "activation", out=vst[:, k, 0:512], in_=P[4][:], func=AF.Copy,
                          rd=[ps_r[4]], wr=[vst_r[k]])
                    S.add("scalar", "activation", out=vst[:, k, 512:640], in_=P[5][:, 0:128], func=AF.Copy,
                          rd=[ps_r[5], vst_r[k]], wr=[vst_r[k]])
                    S.add("gpsimd", "dma_start", out=vS[c * 4 + tt], in_=vst[:, k, :], rd=[vst_r[k]], wr=[vS_r],
                          dma_sem=vst_sems[k], nowaw=True)

            attn_in.n = 0
            attn_in.nv = 0

            def attn_tile(n, l):
                j = l // 2
                lo = min(max(n - 2, 0), 27)
                low = min(max(n - 1, 0), 29)
                dw = low - lo
                typ = {0: 0, 1: 1, 30: 3, 31: 4}.get(n, 2)
                qs = n % 2
                ts = slice(n * 128, (n + 1) * 128)
                S.add("gpsimd", "dma_start", out=qtt[:, qs, 0:4, :], in_=qkT[0:4, :, ts].rearrange("j p t -> p j t"),
                      rd=[qkT_r], wr=[qt_r[qs]], dma_sem=qt_sems[qs])
                S.add("gpsimd", "dma_start", out=qtt[:, qs, 4:8, :], in_=qkT[5:9, :, ts].rearrange("j p t -> p j t"),
                      rd=[qkT_r], wr=[qt_r[qs]], dma_sem=qt_sems[qs], nowaw=True, grp=True)
                ks = slice(lo * 128, lo * 128 + 640)
                S.add("gpsimd", "dma_start", out=kaT, in_=qkT[4, :, ks], rd=[qkT_r], wr=[ka_r], dma_sem=kv_sems[0])
                S.add("gpsimd", "dma_start", out=kbT, in_=qkT[9:13, :, ks].rearrange("j p t -> p j t"),
                      rd=[qkT_r], wr=[kb_r], dma_sem=kv_sems[1])
                S.add("gpsimd", "dma_start", out=Vt, in_=vS[lo:lo + 5].rearrange("t p f -> p t f"),
                      rd=[vS_r], wr=[v_r], dma_sem=kv_sems[2])
                for i in range(8):
                    bs = attn_tile.nb % 2
                    attn_tile.nb += 1
                    S.add("gpsimd", "dma_start", out=biast[:, bs, :], in_=ball[j, typ, i], wr=[bias_r[bs]],
                          dma_sem=bi_sems[bs])
                    for kind in (0, 1):
                        u = attn_tile.it % 2
                        attn_tile.it += 1
                        sa = u
                        if kind == 0:
                            cc, hf, nk = i % 4, i // 4, 3
                        else:
                            cc, hf, nk = i // 2, i % 2, 5
                        rows = slice(64 * hf, 64 * hf + 64)
                        qap = qtt[rows, qs, (cc if kind == 0 else 4 + cc), :]
                        for jj in range(nk):
                            if kind == 0:
                                kap = kaT[rows, (dw + jj) * 128:(dw + jj + 1) * 128]
                                kres = ka_r
                            else:
                                kap = kbT[rows, cc, jj * 128:(jj + 1) * 128]
                                kres = kb_r
                            if jj < 4:
                                o, ores = P[sa][:, jj * 128:(jj + 1) * 128], ps_r[sa]
                            else:
                                o, ores = P[2 + sa][:, 0:128], ps_r[2 + sa]
                            S.add("tensor", "matmul", o, lhsT=kap, rhs=qap, start=True, stop=True,
                                  rd=[kres, qt_r[qs]], wr=[ores])
                        if kind == 0:
                            S.add("vector", "scalar_tensor_tensor", out=et[:, u, 0:384], in0=P[sa][:, 0:384],
                                  scalar=0.125, in1=biast[:, bs, 640:1024], op0=ALU.mult, op1=ALU.add,
                                  rd=[ps_r[sa], bias_r[bs]], wr=[e_r[u]])
                        else:
                            S.add("vector", "scalar_tensor_tensor", out=et[:, u, 0:512], in0=P[sa][:, 0:512],
                                  scalar=0.125, in1=biast[:, bs, 0:512], op0=ALU.mult, op1=ALU.add,
                                  rd=[ps_r[sa], bias_r[bs]], wr=[e_r[u]])
                            S.add("vector", "scalar_tensor_tensor", out=et[:, u, 512:640], in0=P[2 + sa][:, 0:128],
                                  scalar=0.125, in1=biast[:, bs, 512:640], op0=ALU.mult, op1=ALU.add,
                                  rd=[ps_r[2 + sa], bias_r[bs], e_r[u]], wr=[e_r[u]])
                        ncol = nk * 128
                        S.add("scalar", "activation", out=pTt[:, u, 0:ncol], in_=et[:, u, 0:ncol], func=AF.Exp,
                              rd=[e_r[u]], wr=[pT_r[u]])
                        for jj in range(nk):
                            if kind == 0:
                                vap = Vt[:, dw + jj, 512 + 64 * hf:512 + 64 * hf + 64]
                            else:
                                vap = Vt[:, jj, i * 64:(i + 1) * 64]
                            S.add("tensor", "matmul", P[4 + sa][rows, 0:128], lhsT=vap,
                                  rhs=pTt[:, u, jj * 128:(jj + 1) * 128], start=(jj == 0), stop=(jj == nk - 1),
                                  rd=[v_r, pT_r[u]], wr=[ps_r[4 + sa]])
                        for jj in range(nk):
                            S.add("tensor", "matmul", P[6 + sa][:, 0:128], lhsT=ones[:],
                                  rhs=pTt[:, u, jj * 128:(jj + 1) * 128], start=(jj == 0), stop=(jj == nk - 1),
                                  rd=[cst_r, pT_r[u]], wr=[ps_r[6 + sa]])
                        if kind == 0:
                            S.add("vector", "tensor_scalar", out=dent[rows, u, 0:128], in0=P[6 + sa][rows, 0:128],
                                  scalar1=esink[rows, j * 8 + i:j * 8 + i + 1], scalar2=None, op0=ALU.add,
                                  rd=[ps_r[6 + sa], cst_r], wr=[den_r[u]])
                            S.add("vector", "reciprocal", out=dent[rows, u, 0:128], in_=dent[rows, u, 0:128],
                                  rd=[den_r[u]], wr=[den_r[u]])
                        else:
                            S.add("vector", "reciprocal", out=dent[rows, u, 0:128], in_=P[6 + sa][rows, 0:128],
                                  rd=[ps_r[6 + sa]], wr=[den_r[u]])
                        ych = cc if kind == 0 else 4 + cc
                        S.add("vector", "tensor_tensor", out=ystt[rows, qs, ych, :], in0=P[4 + sa][rows, 0:128],
                              in1=dent[rows, u, 0:128], op=ALU.mult,
                              rd=[ps_r[4 + sa], den_r[u]], wr=[yst_r[qs]], nowaw=True)
                S.add("gpsimd", "dma_start", out=ymix[:, :, ts], in_=ystt[:, qs, :, :], rd=[yst_r[qs]], wr=[ym_r],
                      dma_sem=ys_sems[qs], nowaw=True)

            attn_tile.nb = 0
            attn_tile.it = 0

            def mix_out(c, l, wname):
                j = l // 2
                cs = slice(c * TC, (c + 1) * TC)
                S.add("gpsimd", "dma_start", out=xn[:], in_=ymix[:, :, cs], rd=[ym_r], wr=xn_r, dma_sem=ym_sem)
                for oc in range(8):
                    w, wr_ = wget(wb[wname][j * 8 + oc], 1024)
                    wv = w.rearrange("p (k n) -> p k n", k=8)
                    po = 4 + oc % 2
                    for kc in range(8):
                        S.add("tensor", "matmul", P[po][:], lhsT=wv[:, kc, :], rhs=xn[:, kc, :],
                              start=(kc == 0), stop=(kc == 7), rd=[wr_, xn_r[kc]], wr=[ps_r[po]])
                    S.add("vector", "tensor_tensor", out=x[:, oc, cs], in0=P[po][:], in1=x[:, oc, cs], op=ALU.add,
                          rd=[ps_r[po], xr[c][oc]], wr=[xr[c][oc]])


            uS_r, hhS_r, zS_r = Res(), Res(), Res()
            hh_r, hc_r, tmpc_r, ystc_r = Res(), Res(), [Res(), Res()], Res()
            uq_r, yq_r, s5w_r = Res(), Res(), Res()
            tab_r = [Res() for _ in range(8)]
            rot_r, gt_r, G_r = [Res(), Res()], [Res(), Res()], [Res(), Res()]
            stc_r = [Res() for _ in range(8)]
            rho_r = Res()
            zt_r = Res()

            def ssm_in(c, l):
                j = l // 2
                cs = slice(c * TC, (c + 1) * TC)
                rmsnorm(c, gcol(l, 1))
                for q in range(4):
                    w, wr_ = wget(wb["wsi"][j * 12 + q], 1024)
                    wv = w.rearrange("p (k n) -> p k n", k=8)
                    pp = q % 2
                    k = attn_in.n % 2
                    attn_in.n += 1
                    for kc in range(8):
                        T(P[pp][:], lhsT=wv[:, kc, :], rhs=xn[:, kc, :], start=(kc == 0), stop=(kc == 7),
                          rd=[wr_, xn_r[kc]], wr=[ps_r[pp]])
                    A("activation", out=stA[:, k, :], in_=P[pp][:], func=AF.Copy, rd=[ps_r[pp]], wr=[sta_r[k]])
                    G(uS[q, :, cs], stA[:, k, :], rd=[sta_r[k]], wr=[uS_r], sem=sta_sems[k], nowaw=True)
                for q in range(4):
                    pa = 2 + 2 * (q % 2)
                    pg = pa + 1
                    k = attn_in.n % 2
                    attn_in.n += 1
                    for pc, pbank in ((4 + q, pa), (8 + q, pg)):
                        w, wr_ = wget(wb["wsi"][j * 12 + pc], 1024)
                        wv = w.rearrange("p (k n) -> p k n", k=8)
                        for kc in range(8):
                            T(P[pbank][:], lhsT=wv[:, kc, :], rhs=xn[:, kc, :], start=(kc == 0), stop=(kc == 7),
                              rd=[wr_, xn_r[kc]], wr=[ps_r[pbank]])
                    A("activation", out=sg[:, k, :], in_=P[pg][:], func=AF.Sigmoid, rd=[ps_r[pg]], wr=[sg_r[k]])
                    V("tensor_tensor", out=stA[:, k, :], in0=sg[:, k, :], in1=P[pa][:], op=ALU.mult,
                      rd=[sg_r[k], ps_r[pa]], wr=[sta_r[k]])
                    G(hhS[q, :, 16 + c * TC:16 + (c + 1) * TC], stA[:, k, :], rd=[sta_r[k]], wr=[hhS_r],
                      sem=sta_sems[k], nowaw=True)

            def conv_chunk(c, l):
                j = l // 2
                cs = slice(c * TC, (c + 1) * TC)
                base = SCOL + j * 144
                G(hhwin[:, :, 0:542], hhS[:, :, c * TC + 1:c * TC + 543].rearrange("q p t -> p q t"),
                  rd=[hhS_r], wr=[hh_r], sem=kv_sems[0])
                for q in range(4):
                    wA, rA_ = wget(dgS[j, q, 0], 2048)
                    wB, rB_ = wget(dgS[j, q, 1], 2048)
                    wv = [wA.rearrange("p (k n) -> p k n", k=16), wB.rearrange("p (k n) -> p k n", k=16)]
                    rr = [rA_, rB_]
                    pp = q % 2
                    for k in range(31):
                        T(P[pp][:], lhsT=wv[k // 16][:, k % 16, :], rhs=hhwin[:, q, k:k + TC], start=(k == 0),
                          stop=(k == 30), rd=[rr[k // 16], hh_r], wr=[ps_r[pp]])
                    cb = cst[:, base + 124 + q:base + 125 + q]
                    A("activation", out=hcv[:, q, :], in_=P[pp][:], func=AF.Identity, bias=cb,
                      rd=[ps_r[pp], cst_r], wr=[hc_r], nowaw=True)
                    A("activation", out=sq[:, 0, :], in_=P[pp][:], func=AF.Identity, bias=cb,
                      rd=[ps_r[pp], cst_r], wr=[sq_r[0]])
                    A("activation", out=sq[:, 1, :], in_=P[pp][:], func=AF.Square, bias=cb,
                      rd=[ps_r[pp], cst_r], wr=[sq_r[1]])
                    T(P[2][:], lhsT=ones[:], rhs=sq[:, 0, :], start=(q == 0), stop=(q == 3),
                      rd=[sq_r[0], cst_r], wr=[ps_r[2]])
                    T(P[3][:], lhsT=ones[:], rhs=sq[:, 1, :], start=(q == 0), stop=(q == 3),
                      rd=[sq_r[1], cst_r], wr=[ps_r[3]])
                A("activation", out=msq, in_=P[2][:], func=AF.Square, scale=1.0 / 512, rd=[ps_r[2]], wr=[tmpc_r[0]])
                A("activation", out=P[4][:], in_=P[2][:], func=AF.Copy, scale=1.0 / 512, rd=[ps_r[2]], wr=[ps_r[4]])
                V("scalar_tensor_tensor", out=msq, in0=P[3][:], scalar=1.0 / 512, in1=msq, op0=ALU.mult,
                  op1=ALU.subtract, rd=[ps_r[3], tmpc_r[0]], wr=[tmpc_r[0]])
                A("activation", out=P[5][:], in_=msq, func=AF.Sqrt, bias=cst[:, EPSC:EPSC + 1],
                  rd=[tmpc_r[0], cst_r], wr=[ps_r[5]])
                V("reciprocal", out=P[5][:], in_=P[5][:], rd=[ps_r[5]], wr=[ps_r[5]])
                for q in range(4):
                    k = q % 2
                    V("tensor_tensor", out=tmpc[:, k, :], in0=hcv[:, q, :], in1=P[4][:], op=ALU.subtract,
                      rd=[hc_r, ps_r[4]], wr=[tmpc_r[k]] if False else [rot_r[k]])
                    V("scalar_tensor_tensor", out=tmpc[:, k, :], in0=tmpc[:, k, :],
                      scalar=cst[:, base + 128 + q:base + 129 + q], in1=P[5][:], op0=ALU.mult, op1=ALU.mult,
                      rd=[rot_r[k], ps_r[5], cst_r], wr=[rot_r[k]])
                    A("activation", out=ystc[:, q, :], in_=tmpc[:, k, :], func=AF.Silu,
                      bias=cst[:, base + 132 + q:base + 133 + q], rd=[rot_r[k], cst_r], wr=[ystc_r], nowaw=True)
                G(ymix[:, 4:8, cs], ystc, rd=[ystc_r], wr=[ym_r], sem=ys_sems[0], nowaw=True)

            def s5_scan(l):
                j = l // 2
                base = SCOL + j * 144
                nch = SEQ // TS
                for q in range(4):
                    G(uTq, uS[q], rd=[uS_r], wr=[uq_r], sem=kv_sems[1])
                    for d, gh in ((0, 0), (0, 1), (1, 0), (1, 1)):
                        G(s5wt[:].rearrange("p g a n -> p (g a n)"), s5W[j, d, q][:, gh * 2048:(gh + 1) * 2048],
                          wr=[s5w_r], sem=kv_sems[2])
                        for g4 in range(4):
                            G(tab_ap[g4], tabS[j, d, q * 8 + gh * 4 + g4].rearrange("p (c t) -> p c t", c=2),
                              wr=[tab_r[g4]], sem=tb_sems[g4 % 2])
                            V("memset", stc[:, g4:g4 + 1], 0.0, wr=[stc_r[g4]])
                        for cc in range(nch):
                            if d == 0:
                                rng = slice(cc * TS, (cc + 1) * TS)
                                urhs = uTq[:, rng]
                            else:
                                rng = slice(SEQ - (cc + 1) * TS, SEQ - cc * TS)
                                urhs = uTq[:, rng][:, ::-1]
                            py = 2 + cc % 2
                            for g8 in range(4):
                                u = s5_scan.it % 2
                                s5_scan.it += 1
                                dg = d * 32 + q * 8 + gh * 4 + g8
                                pbk = u
                                T(P[pbk][:, 0:TS], lhsT=s5wt[:, g8, 0, :], rhs=urhs, start=True, stop=True,
                                  rd=[s5w_r, uq_r], wr=[ps_r[pbk]])
                                T(P[pbk][:, TS:2 * TS], lhsT=s5wt[:, g8, 1, :], rhs=urhs, start=True, stop=True,
                                  rd=[s5w_r, uq_r], wr=[ps_r[pbk]])
                                cosT = tab_ap[g8][:, 0, :]
                                sinT = tab_ap[g8][:, 1, :]
                                V("tensor_tensor", out=rot2[:, 0, :], in0=P[pbk][:, TS:2 * TS], in1=sinT, op=ALU.mult,
                                  rd=[ps_r[pbk], tab_r[g8]], wr=[rot_r[0]])
                                V("tensor_tensor", out=rot2[:, 1, :], in0=P[pbk][:, 0:TS], in1=cosT, op=ALU.mult,
                                  rd=[ps_r[pbk], tab_r[g8]], wr=[rot_r[1]])
                                V("tensor_tensor", out=rot2[:, 1, :], in0=rot2[:, 1, :], in1=rot2[:, 0, :], op=ALU.add,
                                  rd=[rot_r[0], rot_r[1]], wr=[rot_r[1]])
                                A("activation", out=rhoT[:], in_=cosT, func=AF.Identity, scale=0.0,
                                  bias=s5p[:, j, 0, dg:dg + 1], rd=[tab_r[g8], cst_r], wr=[rho_r])
                                V("tensor_tensor_scan", out=gt2[:, u, :], data0=rhoT[:], data1=rot2[:, 1, :],
                                  initial=stc[:, g8:g8 + 1], op0=ALU.mult, op1=ALU.add,
                                  rd=[rho_r, rot_r[1], stc_r[g8]], wr=[gt_r[u]])
                                T(P[4][:, 2 * g8:2 * g8 + 2], lhsT=swm[:], rhs=gt2[:, u, TS - 2:TS], start=True, stop=True,
                                  rd=[gt_r[u], cst_r], wr=[ps_r[4]])
                                V("tensor_scalar", out=stc[:, 8 + g8:9 + g8], in0=P[4][:, 2 * g8 + 1:2 * g8 + 2],
                                  scalar1=s5p[:, j, 2, dg:dg + 1], scalar2=None, op0=ALU.mult,
                                  rd=[ps_r[4], cst_r], wr=[stc_r[g8]])
                                V("scalar_tensor_tensor", out=stc[:, g8:g8 + 1], in0=gt2[:, u, TS - 1:TS],
                                  scalar=s5p[:, j, 1, dg:dg + 1], in1=stc[:, 8 + g8:9 + g8], op0=ALU.mult,
                                  op1=ALU.subtract, rd=[gt_r[u], stc_r[g8], cst_r], wr=[stc_r[g8]])
                                V("tensor_tensor", out=G4[:, u, :], in0=gt2[:, u, :], in1=cosT, op=ALU.mult,
                                  rd=[gt_r[u], tab_r[g8]], wr=[G_r[u]])
                                V("tensor_tensor", out=G4[:, 2 + u, :], in0=gt2[:, u, :], in1=sinT, op=ALU.mult,
                                  rd=[gt_r[u], tab_r[g8]], wr=[G_r[u]])
                                T(P[py][:, 0:TS], lhsT=s5wt[:, g8, 2, :], rhs=G4[:, u, :], start=(g8 == 0), stop=False,
                                  rd=[s5w_r, G_r[u]], wr=[ps_r[py]])
                                T(P[py][:, 0:TS], lhsT=s5wt[:, g8, 3, :], rhs=G4[:, 2 + u, :], start=False,
                                  stop=(g8 == 3), rd=[s5w_r, G_r[u]], wr=[ps_r[py]])
                            if d == 0 and gh == 0:
                                V("scalar_tensor_tensor", out=yq[:, rng], in0=uTq[:, rng],
                                  scalar=cst[:, base + 136 + q:base + 137 + q], in1=P[py][:, 0:TS], op0=ALU.mult,
                                  op1=ALU.add, rd=[uq_r, ps_r[py], cst_r], wr=[yq_r], nowaw=True)
                            elif d == 0:
                                V("tensor_tensor", out=yq[:, rng], in0=yq[:, rng], in1=P[py][:, 0:TS],
                                  op=ALU.add, rd=[yq_r, ps_r[py]], wr=[yq_r], nowaw=True)
                            else:
                                V("tensor_tensor", out=yq[:, rng][:, ::-1], in0=yq[:, rng][:, ::-1], in1=P[py][:, 0:TS],
                                  op=ALU.add, rd=[yq_r, ps_r[py]], wr=[yq_r], nowaw=True)
                    for c in range(NCH):
                        cs = slice(c * TC, (c + 1) * TC)
                        k = c % 2
                        A("activation", out=rot512, in_=yq[:, cs], func=AF.Square, rd=[yq_r], wr=[rot_r[0], rot_r[1]])
                        V("tensor_scalar", out=rot512, in0=rot512, scalar1=0.044715, scalar2=1.0, op0=ALU.mult,
                          op1=ALU.add, rd=[rot_r[0], rot_r[1]], wr=[rot_r[0], rot_r[1]])
                        V("tensor_tensor", out=rot512, in0=rot512, in1=yq[:, cs], op=ALU.mult,
                          rd=[rot_r[0], rot_r[1], yq_r], wr=[rot_r[0], rot_r[1]])
                        A("activation", out=gt512, in_=rot512, func=AF.Sigmoid, scale=1.5957691216057308,
                          rd=[rot_r[0], rot_r[1]], wr=[gt_r[0], gt_r[1]])
                        V("tensor_tensor", out=g512[:, k, :], in0=yq[:, cs], in1=gt512, op=ALU.mult,
                          rd=[yq_r, gt_r[0], gt_r[1]], wr=[G_r[0], G_r[1]])
                        G(zS[q, :, cs], g512[:, k, :], rd=[G_r[0], G_r[1]], wr=[zS_r], sem=zs_sems[k], nowaw=True)

            s5_scan.it = 0

            def ssm_out(c, l):
                j = l // 2
                cs = slice(c * TC, (c + 1) * TC)
                base = SCOL + j * 144
                zt = pbf.rearrange("p (q t) -> p q t", q=4)
                G(zt, zS[:, :, cs].rearrange("q p t -> p q t"), rd=[zS_r], wr=[pb_r[0], pb_r[1]], sem=p_sems[0])
                G(xn[:, 4:8, :], ymix[:, 4:8, cs], rd=[ym_r], wr=xn_r[4:8], sem=ym_sem)
                for oc in range(4):
                    w, wr_ = wget(wb["wsg"][j * 4 + oc], 512)
                    wv = w.rearrange("p (k n) -> p k n", k=4)
                    pp = oc % 2
                    for kc in range(4):
                        T(P[pp][:], lhsT=wv[:, kc, :], rhs=zt[:, kc, :], start=(kc == 0), stop=(kc == 3),
                          rd=[wr_, pb_r[0], pb_r[1]], wr=[ps_r[pp]])
                    A("activation", out=sg[:, pp, :], in_=P[pp][:], func=AF.Sigmoid,
                      bias=cst[:, base + 140 + oc:base + 141 + oc], rd=[ps_r[pp], cst_r], wr=[sg_r[pp]])
                    V("tensor_tensor", out=xn[:, oc, :], in0=zt[:, oc, :], in1=sg[:, pp, :], op=ALU.mult,
                      rd=[pb_r[0], pb_r[1], sg_r[pp]], wr=[xn_r[oc]])
                for oc in range(8):
                    w, wr_ = wget(wb["wso"][j * 8 + oc], 1024)
                    wv = w.rearrange("p (k n) -> p k n", k=8)
                    po = 4 + oc % 2
                    for kc in range(8):
                        T(P[po][:], lhsT=wv[:, kc, :], rhs=xn[:, kc, :], start=(kc == 0), stop=(kc == 7),
                          rd=[wr_, xn_r[kc]], wr=[ps_r[po]])
                    V("tensor_tensor", out=x[:, oc, cs], in0=P[po][:], in1=x[:, oc, cs], op=ALU.add,
                      rd=[ps_r[po], xr[c][oc]], wr=[xr[c][oc]])

            outs = []
            for s in range(NS):
                for c in range(NCH):
                    cs = slice(c * TC, (c + 1) * TC)
                    S.add("gpsimd", "dma_start", out=x[:, :, cs], in_=xT[s, :, :, cs],
                          wr=xr[c], dma_sem=x_sems[c], extra=(lst if s == 0 else []))
                for l in range(cfg.depth):
                    for c in range(NCH):
                        if cfg.stage >= 1:
                            ffn(c, l, 0)
                    has_mix = cfg.mixers and (l % 2 == 0)
                    has_ssm = cfg.mixers and (l % 2 == 1)
                    if has_ssm:
                        for c in range(NCH):
                            ssm_in(c, l)
                        S.barrier()
                        for c in range(NCH):
                            conv_chunk(c, l)
                        S.barrier()
                        s5_scan(l)
                        S.barrier()
                    if has_mix:
                        for c in range(NCH):
                            attn_in(c, l)
                        S.barrier()
                        for n in range(32):
                            attn_tile(n, l)
                        S.barrier()
                    for c in range(NCH):
                        if has_mix:
                            mix_out(c, l, "wao")
                        if has_ssm:
                            ssm_out(c, l)
                        if cfg.stage >= 2:
                            ffn(c, l, 1)
                        if cfg.stage >= 3:
                            ple(s, c, l)
                        if l == cfg.depth - 1:
                            cs = slice(c * TC, (c + 1) * TC)
                            o = S.add("gpsimd", "dma_start", out=yT[s, :, :, cs], in_=x[:, :, cs],
                                      rd=xr[c], dma_sem=x_sems[c])
                            outs.append(o)
            S.add("gpsimd", "nop", extra=([] if S.dry else outs))
            if S.dry:
                record.reqs = WS.reqs

        EPSC = ncst - 1
        S.dry = True
        record()
        S.dry = False
        record()
        S.emit(nc, block, eng_sems)
    return nc, S


_CACHE = {}


def prep_inputs(inp, cfg):
    xs = np.concatenate([np.asarray(inp["x_prompt"]), np.asarray(inp["x_sample"])], axis=0)
    ps = np.concatenate([np.asarray(inp["p_prompt"]), np.asarray(inp["p_sample"])], axis=1)
    W = pack_weights(inp)
    W["ball"] = attn_tables(inp)
    W.update(ssm_params(inp))
    cst = pack_consts(inp)
    cst = np.concatenate([cst, np.full((128, 1), EPS, np.float32)], axis=1)
    return xs, ps, W, cst


def core_inputs(xs, ps, W, cst, sl):
    NS = len(sl)
    xT = np.ascontiguousarray(xs[sl].reshape(NS, SEQ, 8, 128).transpose(0, 3, 2, 1))
    pT = np.ascontiguousarray(ps[:, sl].reshape(DEPTH, NS, SEQ, 2, 128).transpose(1, 0, 4, 3, 2))
    m = {"xT": xT, "pT": pT, "cst": cst}
    m.update(W)
    return m


def kernel(**inp):
    cfg = Cfg()
    inp = {k: np.asarray(v) for k, v in inp.items()}
    xs, ps, W, cst = prep_inputs(inp, cfg)
    nseq_total = xs.shape[0]
    wshapes = {nm: (a.shape[0], a.shape[2]) for nm, a in W.items() if nm.startswith("w")}
    nc, S = build_program(cfg, wshapes, cst.shape[1])
    in_maps = []
    NS = cfg.nseq
    for core in range(N_CORES):
        sl = [(core * NS + i) % nseq_total for i in range(NS)]
        in_maps.append(core_inputs(xs, ps, W, cst, sl))
    res = run_bass_kernel_spmd(nc, in_maps, core_ids=list(range(N_CORES)))
    ys = np.zeros((nseq_total, SEQ, D_MODEL), np.float32)
    for core in range(N_CORES):
        yT = res.results[core]["yT"]
        y = yT.transpose(0, 3, 2, 1).reshape(NS, SEQ, D_MODEL)
        for i in range(NS):
            ys[(core * NS + i) % nseq_total] = y[i]
    nb = inp["x_prompt"].shape[0]
    return ys[:nb], ys[nb:]
```

```python
import numpy as np
import concourse.bass as bass
import concourse.mybir as mybir
from concourse.bass_utils import run_bass_kernel_spmd

F32 = mybir.dt.float32
BF16 = mybir.dt.bfloat16
AF = mybir.ActivationFunctionType
ALU = mybir.AluOpType

D_MODEL = 1024
SEQ = 4096
DEPTH = 4
D_FF = 2816
NJ = D_FF // 128
PLE_DIM = 256
TC = 512
NCH = SEQ // TC
EPS = 1e-6
N_CORES = 8
SEQ_PER_CORE = 3


class Res:
    __slots__ = ("name", "w", "rd", "psum", "co")

    def __init__(self, name="", psum=False):
        self.name = name
        self.w = None
        self.rd = []
        self.psum = psum
        self.co = []


class Ins:
    __slots__ = ("eng", "meth", "args", "kw", "deps", "pos", "sig", "val", "sem", "is_dma")


ENGS = ["tensor", "scalar", "vector", "gpsimd", "sync"]


class Sched:
    def __init__(self):
        self.by_eng = {e: [] for e in ENGS}
        self.dry = False
        self.sem_counts = {}
        self.n = 0
        self.dma_pending = []
        self.last_dma = {}

    def add(self, eng, meth, *args, rd=(), wr=(), dma_sem=None, extra=(), nowaw=False, grp=False, **kw):
        if self.dry:
            return None
        i = Ins()
        i.eng = eng
        i.meth = meth
        i.args = args
        i.kw = kw
        i.pos = len(self.by_eng[eng])
        i.sig = False
        i.val = 0
        i.sem = dma_sem
        i.is_dma = dma_sem is not None
        if i.is_dma:
            self.sem_counts[id(dma_sem)] = self.sem_counts.get(id(dma_sem), 0) + 16
            i.val = self.sem_counts[id(dma_sem)]
        raw = set()
        oth = set()
        waw = set()
        for r in rd:
            if r.w is not None:
                raw.add(r.w)
            raw.update(r.co)
            if r.psum:
                for j in r.rd:
                    if j.eng != eng:
                        oth.add(j)
        for r in wr:
            if r.w is not None and not nowaw:
                oth.add(r.w)
                oth.update(r.co)
                waw.add(r.w)
                waw.update(r.co)
            oth.update(r.rd)
        deps = []
        raw.update(extra)
        if i.is_dma:
            prev = self.last_dma.get(id(dma_sem))
            if prev is not None and not grp:
                raw.add(prev)
            self.last_dma[id(dma_sem)] = i
        for d in raw | oth:
            if d is i:
                continue
            if d.is_dma or i.is_dma:
                deps.append(d)
            elif d.eng != eng:
                deps.append(d)
            else:
                if eng != "tensor":
                    deps.append(d)
        for d in deps:
            if not d.is_dma:
                d.sig = True
        i.deps = deps
        for r in rd:
            r.rd.append(i)
        for r in wr:
            if nowaw and r.w is not None:
                r.co = r.co[-64:] + [r.w]
            else:
                r.co = []
            r.w = i
            r.rd = []
        self.by_eng[eng].append(i)
        self.n += 1
        if i.is_dma:
            self.dma_pending.append(i)
        return i

    def barrier(self):
        if self.dry:
            return
        lasts = [self.by_eng[e][-1] for e in ENGS if self.by_eng[e]]
        deps = lasts + self.dma_pending
        self.dma_pending = []
        for e in ENGS:
            self.add(e, "nop", extra=deps)

    def emit(self, nc, block, eng_sems):
        for e in ENGS:
            cnt = 0
            for i in self.by_eng[e]:
                if i.is_dma:
                    continue
                if i.sig:
                    cnt += 1
                    i.val = cnt
                    i.sem = eng_sems[e]

        def run(e, handle):
            waited = {}
            for i in self.by_eng[e]:
                need = {}
                for d in i.deps:
                    key = id(d.sem)
                    if key not in need or need[key][1] < d.val:
                        need[key] = (d.sem, d.val)
                for key, (sm, val) in need.items():
                    if waited.get(key, 0) < val:
                        handle.wait_ge(sm, val)
                        waited[key] = val
                bi = getattr(handle, i.meth)(*i.args, **i.kw)
                if i.is_dma:
                    bi.then_inc(i.sem, 16)
                elif i.sig:
                    bi.then_inc(i.sem, 1)

        @block.tensor
        def _(h):
            run("tensor", h)

        @block.scalar
        def _(h):
            run("scalar", h)

        @block.vector
        def _(h):
            run("vector", h)

        @block.gpsimd
        def _(h):
            run("gpsimd", h)

        @block.sync
        def _(h):
            run("sync", h)


def blk_in(w, ncols_blocks=None):
    K, N = w.shape
    return np.ascontiguousarray(w.reshape(K // 128, 128, N // 128, 128).transpose(2, 1, 0, 3))


BIGW = {}


def pack_weights(inp):
    out = {}
    ffn_w = {"ffn1": (inp["w_ffn1_in"], inp["w_ffn1_out"]), "ffn2": (inp["w_ffn2_in"], inp["w_ffn2_out"])}
    for nm in ("ffn1", "ffn2"):
        wi, wo = ffn_w[nm]
        a = []
        for l in range(DEPTH):
            g = blk_in(wi[l][:, :D_FF])
            u = blk_in(wi[l][:, D_FF:])
            a.append(np.stack([g, u], axis=2))
        out["w%si" % nm[-1]] = np.stack(a).reshape(DEPTH * NJ, 128, 2 * 8 * 128)
        b = [blk_in(wo[l]) for l in range(DEPTH)]
        out["w%so" % nm[-1]] = np.stack(b).reshape(DEPTH * 8, 128, NJ * 128)
    out["wpg"] = np.stack([blk_in(inp["w_ple_gate"][l]) for l in range(DEPTH)]).reshape(DEPTH * 8, 128, 1024)
    out["wpp"] = np.ascontiguousarray(
        inp["w_ple_proj"].reshape(DEPTH, 2, 128, 1024).transpose(0, 2, 1, 3)).reshape(DEPTH, 128, 2048)
    A_Q, A_KV, B_Q = 512, 128, 512
    wai, wav, wao = [], [], []
    for j in range(2):
        wi = inp["w_attn_in"][j]
        qa = wi[:, :A_Q]
        cols = []
        for cc in range(4):
            cols.append(qa[:, cc * 64:(cc + 1) * 64])
            cols.append(qa[:, (4 + cc) * 64:(5 + cc) * 64])
        cols.append(wi[:, A_Q:A_Q + A_KV])
        cols.append(wi[:, A_Q + 2 * A_KV:A_Q + 2 * A_KV + B_Q])
        cols.append(wi[:, A_Q + 2 * A_KV + B_Q:A_Q + 2 * A_KV + 2 * B_Q])
        fm = np.concatenate(cols, axis=1)
        wai.append(blk_in(fm).reshape(13, 128, 1024))
        vv = np.concatenate([wi[:, A_Q + 2 * A_KV + 2 * B_Q:], wi[:, A_Q + A_KV:A_Q + 2 * A_KV]], axis=1)
        vv = vv.reshape(2, 4, 128, 640).transpose(0, 2, 1, 3)
        wav.append(np.ascontiguousarray(vv).reshape(2, 128, 2560))
        wo = inp["w_attn_out"][j]
        rows = []
        for cc in range(4):
            rows.append(wo[cc * 64:(cc + 1) * 64])
            rows.append(wo[(4 + cc) * 64:(5 + cc) * 64])
        rows.append(wo[512:])
        wao.append(blk_in(np.concatenate(rows, axis=0)).reshape(8, 128, 1024))
    out["wai"] = np.concatenate(wai)
    out["wav"] = np.concatenate(wav)
    out["wao"] = np.concatenate(wao)
    out["wsi"] = np.concatenate([blk_in(inp["w_ssm_in"][j]).reshape(12, 128, 1024) for j in range(2)])
    out["wsg"] = np.concatenate([blk_in(inp["w_glu_c"][j]).reshape(4, 128, 512) for j in range(2)])
    out["wso"] = np.concatenate([blk_in(inp["w_ssm_out"][j]).reshape(8, 128, 1024) for j in range(2)])
    return out


def ssm_params(inp):
    G, N, Pc = 32, 64, 16
    s5A = np.zeros((2, 128, 3, 64), np.float32)
    s5B = np.zeros((2, 2, 4, 128, 5, 64), np.float32)
    s5C = np.zeros((2, 2, 4, 128, 2, 128), np.float32)
    for j in range(2):
        for d in range(2):
            lre = inp["lam_re"][j, d]
            lim = inp["lam_im"][j, d]
            ldt = inp["log_dt"][j, d]
            cols = slice(d * 32, (d + 1) * 32)
            s5A[j, :, 0, cols] = np.tile(lre.T, (2, 1))
            s5A[j, :, 1, cols] = np.tile(lim.T, (2, 1))
            s5A[j, :, 2, cols] = np.broadcast_to(ldt[None, :], (128, 32))
            for q in range(4):
                gs = slice(q * 8, (q + 1) * 8)
                s5B[j, d, q, :, 0, :] = inp["b_re"][j, d, gs].transpose(0, 2, 1).reshape(128, N)
                s5B[j, d, q, :, 1, :] = inp["b_im"][j, d, gs].transpose(0, 2, 1).reshape(128, N)
                s5B[j, d, q, :, 2, :] = np.repeat(lre[gs], Pc, axis=0)
                s5B[j, d, q, :, 3, :] = np.repeat(lim[gs], Pc, axis=0)
                s5B[j, d, q, :, 4, :] = np.repeat(ldt[gs], Pc)[:, None]
                cre = inp["c_re"][j, d, gs].transpose(2, 0, 1).reshape(N, 128)
                cim = inp["c_im"][j, d, gs].transpose(2, 0, 1).reshape(N, 128)
                s5C[j, d, q, :64, 0, :] = cre
                s5C[j, d, q, 64:, 0, :] = cim
                s5C[j, d, q, :64, 1, :] = cim
                s5C[j, d, q, 64:, 1, :] = cre
    kc32 = np.zeros((128, 2, 128), np.float32)
    kc32[:, 0, :] = np.eye(128, dtype=np.float32)
    for pp in range(64):
        kc32[pp + 64, 1, pp] = 1.0
        kc32[pp, 1, pp + 64] = -1.0
    return {"s5A": s5A, "s5B": s5B, "s5C": s5C, "kc32": kc32}


NEG = -30000.0


def attn_tables(inp):
    out = np.full((2, 5, 8, 128, 1024), NEG, np.float32)
    k = np.arange(128)[:, None]
    q = np.arange(128)[None, :]
    slopes = [2.0 ** (-(i + 1)) for i in range(8)]
    for ti, n in enumerate((0, 1, 2, 30, 31)):
        lo = min(max(n - 2, 0), 27)
        low = min(max(n - 1, 0), 29)
        qtok = n * 128 + q
        r = qtok // 64
        c = qtok % 64
        rs = np.clip(r - 4, 0, 56)
        cs_ = np.clip(c - 8, 0, 48)
        for jj in range(5):
            ktok = (lo + jj) * 128 + k
            kr = ktok // 64
            kc = ktok % 64
            ok = (kr >= rs) & (kr < rs + 8) & (kc >= cs_) & (kc < cs_ + 16)
            dr = np.clip(kr - r + 7, 0, 14)
            dc = np.clip(kc - c + 15, 0, 30)
            for j in range(2):
                for h in range(8):
                    g = inp["rpb_b"][j, h][dr, dc]
                    out[j, ti, h, :, jj * 128:(jj + 1) * 128] = np.where(ok, g, NEG)
        for jj in range(3):
            ktok = (low + jj) * 128 + k
            dist = np.abs(ktok - qtok)
            okw = dist <= 128
            for h in range(8):
                tab = np.where(okw, (-slopes[h]) * dist.astype(np.float32), NEG).astype(np.float32)
                out[:, ti, h, :, 640 + jj * 128:640 + (jj + 1) * 128] = tab
    return out


def pack_consts(inp):
    cols = []

    def gain(v):
        return v.reshape(8, 128).T

    for l in range(DEPTH):
        for nm in ("norm_ffn1", "norm_mix", "norm_ffn2", "norm_ple", "norm_ple_post"):
            cols.append(gain(inp[nm][l]))
    for j in range(2):
        for nm in ("q_gain_a", "k_gain_a", "q_gain_b", "k_gain_b"):
            cols.append(np.tile(inp[nm][j], 2)[:, None])
        cols.append(np.broadcast_to(inp["sink_a"][j][None, :], (128, 8)))
    for j in range(2):
        cw = inp["conv_w"][j]
        for q in range(4):
            cols.append(cw[:, q * 128:(q + 1) * 128].T)
        for nm in ("conv_b", "ln_g_d", "ln_b_d", "d_skip", "b_glu_c"):
            cols.append(inp[nm][j].reshape(4, 128).T)
    gm = np.zeros((128, 8), np.float32)
    for g8 in range(8):
        gm[g8 * 16:(g8 + 1) * 16, g8] = 1.0
    cols.append(gm)
    cols.append(np.full((128, 1), np.pi / 2, np.float32))
    return np.ascontiguousarray(np.concatenate(cols, axis=1).astype(np.float32))


def gcol(l, which):
    return (l * 5 + which) * 8


ACOL = DEPTH * 5 * 8
SCOL = ACOL + 24
GMC = SCOL + 288
HPIC = GMC + 8
TS = 256


class Cfg:
    nseq = SEQ_PER_CORE
    depth = DEPTH
    mixers = True
    stage = 9


def build_program(cfg, wshapes, ncst):
    nc = bass.Bass("TRN2", target_bir_lowering=False)
    NS = cfg.nseq
    xT = nc.dram_tensor("xT", [NS, 128, 8, SEQ], F32, kind="ExternalInput").ap()
    pT = nc.dram_tensor("pT", [NS, DEPTH, 128, 2, SEQ], F32, kind="ExternalInput").ap()
    yT = nc.dram_tensor("yT", [NS, 128, 8, SEQ], F32, kind="ExternalOutput").ap()
    cst_d = nc.dram_tensor("cst", [128, ncst], F32, kind="ExternalInput").ap()
    wf = {}
    wb = {}
    for nm, (nb, fsz) in wshapes.items():
        wf[nm] = nc.dram_tensor(nm, [nb, 128, fsz], F32, kind="ExternalInput").ap()
        wb[nm] = nc.dram_tensor(nm + "_bf", [nb, 128, fsz], BF16, kind="Internal").ap()

    ball = nc.dram_tensor("ball", [2, 5, 8, 128, 1024], F32, kind="ExternalInput").ap()
    qkT = nc.dram_tensor("qkT_s", [13, 128, SEQ], BF16, kind="Internal").ap()
    vS = nc.dram_tensor("vS_s", [32, 128, 640], BF16, kind="Internal").ap()
    ymix = nc.dram_tensor("ymix_s", [128, 8, SEQ], BF16, kind="Internal").ap()

    s5A_d = nc.dram_tensor("s5A", [2, 128, 3, 64], F32, kind="ExternalInput").ap()
    s5B_d = nc.dram_tensor("s5B", [2, 2, 4, 128, 5, 64], F32, kind="ExternalInput").ap()
    s5C_d = nc.dram_tensor("s5C", [2, 2, 4, 128, 2, 128], F32, kind="ExternalInput").ap()
    kc32_d = nc.dram_tensor("kc32", [128, 2, 128], F32, kind="ExternalInput").ap()
    uS = nc.dram_tensor("uS_s", [4, 128, SEQ], BF16, kind="Internal").ap()
    hhS = nc.dram_tensor("hhS_s", [4, 128, SEQ + 32], BF16, kind="Internal").ap()
    zS = nc.dram_tensor("zS_s", [4, 128, SEQ], BF16, kind="Internal").ap()
    tabS = nc.dram_tensor("tabS_s", [2, 2, 32, 128, 2 * TS], F32, kind="Internal").ap()
    s5W = nc.dram_tensor("s5W_s", [2, 2, 4, 128, 8 * 4 * 128], BF16, kind="Internal").ap()
    dgS = nc.dram_tensor("dgS_s", [2, 4, 2, 128, 2048], BF16, kind="Internal").ap()

    D = 4
    WSLOT = 3072
    S = Sched()
    import contextlib
    with contextlib.ExitStack() as es:
        def sb(name, shape, dt):
            return es.enter_context(nc.sbuf_tensor(name, shape, dt))

        x = sb("x", [128, 8, SEQ], F32)
        wsl = sb("wsl", [128, D, WSLOT], BF16)
        xn = sb("xn", [128, 8, TC], BF16)
        hb = sb("hb", [128, NJ, TC], BF16)
        sq = sb("sq", [128, 2, TC], BF16)
        sg = sb("sg", [128, 2, TC], F32)
        pb = sb("pb", [128, 2, 2, TC], BF16)
        cst = sb("cstt", [128, ncst], F32)
        ones = sb("ones", [128, 128], BF16)
        bones = sb("bones", [128, 128], BF16)
        esink = sb("esink", [128, 16], F32)
        stA = sb("stA", [128, 2, TC], BF16)
        vst = sb("vst", [128, 2, 640], BF16)
        ident = sb("ident", [128, 128], BF16)
        swm = sb("swm", [128, 128], F32)
        s5p = sb("s5p", [128, 2, 3, 64], F32)
        s5wt = sb("s5wt", [128, 4, 4, 128], BF16)
        stc = sb("stc", [128, 16], F32)
        rhoT = sb("rhoT", [128, 2, TS], F32)
        zero16 = sb("zero16", [128, 16], BF16)
        P = [es.enter_context(nc.psum_tensor("ps%d" % i, [128, TC], F32)) for i in range(8)]
        nsem = 0

        def sem(name):
            return es.enter_context(nc.semaphore(name))

        eng_sems = {e: sem("e_" + e) for e in ENGS}
        w_sems = [sem("w%d" % i) for i in range(D)]
        x_sems = [sem("x%d" % i) for i in range(NCH)]
        p_sems = [sem("p%d" % i) for i in range(2)]
        c_sem = sem("cst")
        cv_sems = [sem("cv%d" % i) for i in range(2)]
        cvs_sems = [sem("cvs%d" % i) for i in range(2)]
        sta_sems = [sem("sta%d" % i) for i in range(2)]
        vst_sems = [sem("vst%d" % i) for i in range(2)]
        qt_sems = [sem("qt%d" % i) for i in range(2)]
        kv_sems = [sem("kv%d" % i) for i in range(3)]
        bi_sems = [sem("bi%d" % i) for i in range(2)]
        ys_sems = [sem("ys%d" % i) for i in range(2)]
        ym_sem = sem("ym")
        pr_sems = [sem("pr%d" % i) for i in range(6)]
        tb_sems = [sem("tb%d" % i) for i in range(2)]
        s5_sems = [sem("s5_%d" % i) for i in range(4)]
        zs_sems = [sem("zs%d" % i) for i in range(2)]
        block = es.enter_context(nc.Block())

        hb32 = hb[:].rearrange("p j t -> p (j t)").bitcast(F32)
        hbf = hb[:].rearrange("p j t -> p (j t)")
        Vt = hbf[:, 0:3200].rearrange("p (t f) -> p t f", t=5)
        kbT = hbf[:, 3200:5760].rearrange("p (c t) -> p c t", c=4)
        kaT = hbf[:, 5760:6400]
        pTt = hbf[:, 6400:7680].rearrange("p (u t) -> p u t", u=2)
        et = hbf[:, 7680:10240].bitcast(F32).rearrange("p (u t) -> p u t", u=2)
        biast = xn[:].rearrange("p k t -> p (k t)").bitcast(F32).rearrange("p (u t) -> p u t", u=2)
        qtt = sg[:].rearrange("p u t -> p (u t)").bitcast(BF16).rearrange("p (u c t) -> p u c t", u=2, c=8)
        ystt = pb[:].rearrange("p a b t -> p (a b t)").rearrange("p (u c t) -> p u c t", u=2, c=8)
        dent = sq[:].rearrange("p u t -> p (u t)").bitcast(F32).rearrange("p (u t) -> p u t", u=2)
        sgf = sg[:].rearrange("p u t -> p (u t)")
        pbf = pb[:].rearrange("p a b t -> p (a b t)")
        tab_ap = [sgf[:, 0:512], sgf[:, 512:1024], pbf[:, 0:1024].bitcast(F32), pbf[:, 1024:2048].bitcast(F32)]
        tab_ap = [t.rearrange("p (c t) -> p c t", c=2) for t in tab_ap]
        yq = hbf[:, 0:8192].bitcast(F32)
        rot2 = hbf[:, 8192:9216].bitcast(F32).rearrange("p (u t) -> p u t", u=2)
        rot512 = hbf[:, 8192:9216].bitcast(F32)
        gt2 = hbf[:, 9216:10240].bitcast(F32).rearrange("p (u t) -> p u t", u=2)
        gt512 = hbf[:, 9216:10240].bitcast(F32)
        G4 = hbf[:, 10240:11264].rearrange("p (u t) -> p u t", u=4)
        g512 = hbf[:, 10240:11264].rearrange("p (u t) -> p u t", u=2)
        uTq = xn[:].rearrange("p k t -> p (k t)")
        hhwin = hbf[:, 0:2176].rearrange("p (q t) -> p q t", q=4)
        hcv = hbf[:, 2176:6272].bitcast(F32).rearrange("p (q t) -> p q t", q=4)
        tmpc = hbf[:, 6272:8320].bitcast(F32).rearrange("p (u t) -> p u t", u=2)
        msq = hbf[:, 8320:9344].bitcast(F32)
        ystc = pbf.rearrange("p (q t) -> p q t", q=4)
        wkA = x[:, 2, :]
        wkT = x[:, 3, :]
        wkB = x[:, 4, :]
        wkC = x[:, 5, :]
        wkW = x[:, 6, :].bitcast(BF16)
        wkD = x[:, 7, :].bitcast(BF16)

        def record():
            xr = [[Res("x%d_%d" % (c, k)) for k in range(8)] for c in range(NCH)]
            xn_r = [Res() for _ in range(8)]
            h_r = [Res() for _ in range(NJ)]
            sq_r = [Res(), Res()]
            sg_r = [Res(), Res()]
            pb_r = [Res(), Res()]
            ps_r = [Res(psum=True) for _ in range(8)]
            ws_r = [Res() for _ in range(D)]
            cst_r = Res()
            wb_r = Res()
            sta_r = [Res(), Res()]
            vst_r = [Res(), Res()]
            qkT_r = Res()
            vS_r = Res()
            ym_r = Res()
            qt_r = [Res(), Res()]
            ka_r, kb_r, v_r = Res(), Res(), Res()
            bias_r = [Res(), Res()]
            e_r = [Res(), Res()]
            pT_r = [Res(), Res()]
            yst_r = [Res(), Res()]
            den_r = [Res(), Res()]

            class WS:
                reqs = []
                n = 0
                issued = 0

            if S.dry:
                WS.reqs = []
            else:
                WS.reqs = record.reqs

            def wget(dram_ap, fsz):
                if S.dry:
                    WS.reqs.append((dram_ap, fsz))
                    return wsl[:, 0, :fsz], ws_r[0]
                k = WS.n
                WS.n += 1
                while WS.issued < min(len(WS.reqs), k + D - 1):
                    m = WS.issued
                    ap_m, f_m = WS.reqs[m]
                    S.add("sync", "dma_start", out=wsl[:, m % D, :f_m], in_=ap_m,
                          rd=[wb_r], wr=[ws_r[m % D]], dma_sem=w_sems[m % D])
                    WS.issued += 1
                return wsl[:, k % D, :fsz], ws_r[k % D]

            S.add("gpsimd", "dma_start", out=cst[:], in_=cst_d, wr=[cst_r], dma_sem=c_sem)
            S.add("vector", "memset", ones[:], 1.0, wr=[cst_r])
            S.add("vector", "memset", bones[:], 0.0, wr=[cst_r])
            S.add("vector", "memset", bones[0:64, 0:64], 1.0, wr=[cst_r])
            S.add("vector", "memset", bones[64:128, 64:128], 1.0, wr=[cst_r])
            for j in range(2):
                S.add("scalar", "activation", out=esink[:, j * 8:(j + 1) * 8],
                      in_=cst[:, ACOL + j * 12 + 4:ACOL + j * 12 + 12], func=AF.Exp, rd=[cst_r], wr=[cst_r])
            xb = x[:].rearrange("p k t -> p (k t)").bitcast(BF16)
            stg_r = [Res(), Res()]
            last_st = [None, None]
            nconv = 0
            for nm, (nb, fsz) in wshapes.items():
                if fsz <= 2048:
                    G = 1
                    for g in (8, 4, 2, 1):
                        if g * fsz <= 8192 and nb % g == 0:
                            G = g
                            break
                    sub = None
                else:
                    G = 1
                    sub = 2
                    while fsz // sub > 2048 or fsz % sub:
                        sub += 1
                for b0 in range(0, nb, G):
                    k = nconv % 2
                    nconv += 1
                    st = xb[:, k * 8192: k * 8192 + G * fsz]
                    if sub is None:
                        o = st.rearrange("p (g f) -> p g f", g=G)
                        i_ = wf[nm][b0:b0 + G].rearrange("g p f -> p g f")
                        so = wb[nm][b0:b0 + G].rearrange("g p f -> p g f")
                    else:
                        o = st.rearrange("p (a f) -> p a f", a=sub)
                        i_ = wf[nm][b0].rearrange("p (a f) -> p a f", a=sub)
                        so = wb[nm][b0].rearrange("p (a f) -> p a f", a=sub)
                    S.add("gpsimd", "dma_start", out=o, in_=i_, wr=[stg_r[k]], dma_sem=cv_sems[k])
                    last_st[k] = S.add("sync", "dma_start", out=so, in_=o, rd=[stg_r[k]], dma_sem=cvs_sems[k])
            lst = [] if S.dry else [i for i in last_st if i is not None]
            S.add("sync", "nop", wr=[wb_r], extra=lst)


            def V(meth, *args, rd=(), wr=(), **kw):
                return S.add("vector", meth, *args, rd=rd, wr=wr, **kw)

            def A(meth, *args, rd=(), wr=(), **kw):
                return S.add("scalar", meth, *args, rd=rd, wr=wr, **kw)

            def T(*args, rd=(), wr=(), **kw):
                return S.add("tensor", "matmul", *args, rd=rd, wr=wr, **kw)

            def G(out, in_, rd=(), wr=(), sem=None, **kw):
                return S.add("gpsimd", "dma_start", out=out, in_=in_, rd=rd, wr=wr, dma_sem=sem, **kw)

            def dbl(c_in, s_in, c_out, s_out, t1, t2, r):
                V("tensor_tensor", out=t1, in0=c_in, in1=s_in, op=ALU.mult, rd=[r], wr=[r])
                V("tensor_tensor", out=t2, in0=s_in, in1=s_in, op=ALU.mult, rd=[r], wr=[r])
                V("tensor_tensor", out=c_out, in0=c_in, in1=c_in, op=ALU.mult, rd=[r], wr=[r])
                V("tensor_tensor", out=c_out, in0=c_out, in1=t2, op=ALU.subtract, rd=[r], wr=[r])
                V("tensor_scalar", out=s_out, in0=t1, scalar1=2.0, scalar2=None, op0=ALU.mult, rd=[r], wr=[r])

            def cis(th, c, s_, t1, t2, r):
                A("activation", out=s_, in_=th, func=AF.Sin, scale=0.125, rd=[r, cst_r], wr=[r])
                A("activation", out=c, in_=th, func=AF.Sin, scale=-0.125, bias=cst[:, HPIC:HPIC + 1],
                  rd=[r, cst_r], wr=[r])
                for _ in range(3):
                    dbl(c, s_, c, s_, t1, t2, r)

            def ssm_prologue():
                rA, rB, rC = Res(), Res(), Res()
                rT = [Res(), Res()]
                rW = [Res(), Res()]
                rD = [Res(), Res()]
                S.add("gpsimd", "dma_start", out=ident[:], in_=kc32_d[:, 0, :], wr=[cst_r], dma_sem=pr_sems[0])
                S.add("gpsimd", "dma_start", out=swm[:], in_=kc32_d[:, 1, :], wr=[cst_r], dma_sem=pr_sems[0], nowaw=True, grp=True)
                V("memset", zero16[:], 0.0, wr=[rC])
                for q in range(4):
                    S.add("gpsimd", "dma_start", out=hhS[q, :, 0:16], in_=zero16[:], rd=[rC], dma_sem=pr_sems[1])
                    S.add("gpsimd", "dma_start", out=hhS[q, :, SEQ + 16:SEQ + 32], in_=zero16[:], rd=[rC],
                          dma_sem=pr_sems[1])

                def a64(i):
                    return wkA[:, i * 64:(i + 1) * 64]

                def b64(i):
                    return wkB[:, i * 64:(i + 1) * 64]

                for j in range(2):
                    A3 = wkA[:, 0:192].rearrange("p (a b) -> p a b", a=3)
                    S.add("gpsimd", "dma_start", out=A3, in_=s5A_d[j], wr=[rA], dma_sem=pr_sems[2])
                    dt, th, t1, t2 = a64(3), a64(4), a64(5), a64(6)
                    CK = [a64(8 + k) for k in range(9)]
                    SK = [a64(17 + k) for k in range(9)]
                    A("activation", out=dt, in_=A3[:, 2, :], func=AF.Exp, rd=[rA], wr=[rA])
                    V("tensor_tensor", out=t1, in0=A3[:, 0, :], in1=dt, op=ALU.mult, rd=[rA], wr=[rA])
                    A("activation", out=s5p[:, j, 0, :], in_=t1, func=AF.Exp, rd=[rA], wr=[rA])
                    V("tensor_tensor", out=th, in0=A3[:, 1, :], in1=dt, op=ALU.mult, rd=[rA], wr=[rA])
                    cis(th, CK[0], SK[0], t1, t2, rA)
                    for k in range(8):
                        dbl(CK[k], SK[k], CK[k + 1], SK[k + 1], t1, t2, rA)
                    V("tensor_copy", out=s5p[:, j, 1, :], in_=CK[8], rd=[rA], wr=[rA])
                    V("tensor_copy", out=s5p[:, j, 2, :], in_=SK[8], rd=[rA], wr=[rA])
                    for dg in range(64):
                        k2 = dg % 2
                        base = k2 * 1024
                        tc_ = wkT[:, base:base + TS]
                        ts_ = wkT[:, base + TS:base + 2 * TS]
                        u1 = wkT[:, base + 2 * TS:base + 2 * TS + 128]
                        r = rT[k2]
                        V("tensor_copy", out=tc_[:, 0:1], in_=CK[0][:, dg:dg + 1], rd=[rA], wr=[r])
                        V("tensor_copy", out=ts_[:, 0:1], in_=SK[0][:, dg:dg + 1], rd=[rA], wr=[r])
                        for k in range(8):
                            n = 1 << k
                            ck = CK[k][:, dg:dg + 1]
                            sk = SK[k][:, dg:dg + 1]
                            V("tensor_scalar", out=u1[:, 0:n], in0=ts_[:, 0:n], scalar1=sk, scalar2=None, op0=ALU.mult,
                              rd=[r, rA], wr=[r])
                            V("scalar_tensor_tensor", out=tc_[:, n:2 * n], in0=tc_[:, 0:n], scalar=ck, in1=u1[:, 0:n],
                              op0=ALU.mult, op1=ALU.subtract, rd=[r, rA], wr=[r])
                            V("tensor_scalar", out=u1[:, 0:n], in0=ts_[:, 0:n], scalar1=ck, scalar2=None, op0=ALU.mult,
                              rd=[r, rA], wr=[r])
                            V("scalar_tensor_tensor", out=ts_[:, n:2 * n], in0=tc_[:, 0:n], scalar=sk, in1=u1[:, 0:n],
                              op0=ALU.mult, op1=ALU.add, rd=[r, rA], wr=[r])
                        S.add("gpsimd", "dma_start", out=tabS[j, dg // 32, dg % 32], in_=wkT[:, base:base + 2 * TS],
                              rd=[r], dma_sem=tb_sems[k2])
                    nw = 0
                    for d in range(2):
                        for q in range(4):
                            B5 = wkB[:, 0:320].rearrange("p (a b) -> p a b", a=5)
                            S.add("gpsimd", "dma_start", out=B5, in_=s5B_d[j, d, q], wr=[rB], dma_sem=pr_sems[3])
                            bre, bim, lre, lim, ldt = (B5[:, i, :] for i in range(5))
                            dtb, thb, c1, s1, u1, u2, rho, lbr, lbi, den, cr, ci = (b64(5 + i) for i in range(12))
                            BB = wkB[:, 1280:1408]
                            BBs = wkB[:, 1408:1536]
                            A("activation", out=dtb, in_=ldt, func=AF.Exp, rd=[rB], wr=[rB])
                            V("tensor_tensor", out=u1, in0=lre, in1=dtb, op=ALU.mult, rd=[rB], wr=[rB])
                            A("activation", out=rho, in_=u1, func=AF.Exp, rd=[rB], wr=[rB])
                            V("tensor_tensor", out=thb, in0=lim, in1=dtb, op=ALU.mult, rd=[rB], wr=[rB])
                            cis(thb, c1, s1, u1, u2, rB)
                            V("tensor_tensor", out=lbr, in0=rho, in1=c1, op=ALU.mult, rd=[rB], wr=[rB])
                            V("tensor_tensor", out=lbi, in0=rho, in1=s1, op=ALU.mult, rd=[rB], wr=[rB])
                            V("tensor_scalar", out=lbr, in0=lbr, scalar1=-1.0, scalar2=None, op0=ALU.add, rd=[rB], wr=[rB])
                            V("tensor_tensor", out=den, in0=lre, in1=lre, op=ALU.mult, rd=[rB], wr=[rB])
                            V("tensor_tensor", out=u1, in0=lim, in1=lim, op=ALU.mult, rd=[rB], wr=[rB])
                            V("tensor_tensor", out=den, in0=den, in1=u1, op=ALU.add, rd=[rB], wr=[rB])
                            V("reciprocal", out=den, in_=den, rd=[rB], wr=[rB])
                            V("tensor_tensor", out=cr, in0=lbr, in1=lre, op=ALU.mult, rd=[rB], wr=[rB])
                            V("tensor_tensor", out=u1, in0=lbi, in1=lim, op=ALU.mult, rd=[rB], wr=[rB])
                            V("tensor_tensor", out=cr, in0=cr, in1=u1, op=ALU.add, rd=[rB], wr=[rB])
                            V("tensor_tensor", out=cr, in0=cr, in1=den, op=ALU.mult, rd=[rB], wr=[rB])
                            V("tensor_tensor", out=ci, in0=lbi, in1=lre, op=ALU.mult, rd=[rB], wr=[rB])
                            V("tensor_tensor", out=u1, in0=lbr, in1=lim, op=ALU.mult, rd=[rB], wr=[rB])
                            V("tensor_tensor", out=ci, in0=ci, in1=u1, op=ALU.subtract, rd=[rB], wr=[rB])
                            V("tensor_tensor", out=ci, in0=ci, in1=den, op=ALU.mult, rd=[rB], wr=[rB])
                            V("tensor_tensor", out=BB[:, 0:64], in0=cr, in1=bre, op=ALU.mult, rd=[rB], wr=[rB])
                            V("tensor_tensor", out=u1, in0=ci, in1=bim, op=ALU.mult, rd=[rB], wr=[rB])
                            V("tensor_tensor", out=BB[:, 0:64], in0=BB[:, 0:64], in1=u1, op=ALU.subtract, rd=[rB], wr=[rB])
                            V("tensor_tensor", out=BB[:, 64:128], in0=cr, in1=bim, op=ALU.mult, rd=[rB], wr=[rB])
                            V("tensor_tensor", out=u1, in0=ci, in1=bre, op=ALU.mult, rd=[rB], wr=[rB])
                            V("tensor_tensor", out=BB[:, 64:128], in0=BB[:, 64:128], in1=u1, op=ALU.add, rd=[rB], wr=[rB])
                            V("tensor_copy", out=BBs[:, 0:64], in_=BB[:, 64:128], rd=[rB], wr=[rB])
                            V("tensor_scalar", out=BBs[:, 64:128], in0=BB[:, 0:64], scalar1=-1.0, scalar2=None,
                              op0=ALU.mult, rd=[rB], wr=[rB])
                            C2 = wkC[:, 0:256].rearrange("p (a b) -> p a b", a=2)
                            S.add("gpsimd", "dma_start", out=C2, in_=s5C_d[j, d, q], wr=[rC], dma_sem=pr_sems[4])
                            V("tensor_scalar", out=C2[64:128, 0, :], in0=C2[64:128, 0, :], scalar1=-1.0, scalar2=None,
                              op0=ALU.mult, rd=[rC], wr=[rC])
                            V("tensor_scalar", out=C2[:, 1, :], in0=C2[:, 1, :], scalar1=-1.0, scalar2=None,
                              op0=ALU.mult, rd=[rC], wr=[rC])
                            k2 = nw % 2
                            nw += 1
                            W8 = wkW[:, k2 * 4096:(k2 + 1) * 4096].rearrange("p (g a n) -> p g a n", g=8, a=4)
                            V("memset", wkW[:, k2 * 4096:(k2 + 1) * 4096], 0.0, wr=[rW[k2]])
                            for g8 in range(8):
                                gm = cst[:, GMC + g8:GMC + g8 + 1]
                                V("tensor_scalar", out=W8[:, g8, 0, :], in0=BB, scalar1=gm, scalar2=None, op0=ALU.mult,
                                  rd=[rB, cst_r, rW[k2]], wr=[rW[k2]])
                                V("tensor_scalar", out=W8[:, g8, 1, :], in0=BBs, scalar1=gm, scalar2=None, op0=ALU.mult,
                                  rd=[rB, cst_r, rW[k2]], wr=[rW[k2]])
                                cs16 = slice(16 * g8, 16 * g8 + 16)
                                V("tensor_copy", out=W8[:, g8, 2, cs16], in_=C2[:, 0, cs16], rd=[rC, rW[k2]], wr=[rW[k2]])
                                V("tensor_copy", out=W8[:, g8, 3, cs16], in_=C2[:, 1, cs16], rd=[rC, rW[k2]], wr=[rW[k2]])
                            S.add("gpsimd", "dma_start", out=s5W[j, d, q], in_=wkW[:, k2 * 4096:(k2 + 1) * 4096],
                                  rd=[rW[k2]], dma_sem=s5_sems[k2])
                    nd = 0
                    for q in range(4):
                        for half in range(2):
                            k2 = nd % 2
                            nd += 1
                            st_ = wkD[:, k2 * 2048:(k2 + 1) * 2048].rearrange("p (k n) -> p k n", k=16)
                            if half == 1:
                                V("memset", st_[:, 15, :], 0.0, wr=[rD[k2]])
                            for kk in range(16 if half == 0 else 15):
                                col = SCOL + j * 144 + q * 31 + half * 16 + kk
                                V("tensor_scalar", out=st_[:, kk, :], in0=ident[:], scalar1=cst[:, col:col + 1],
                                  scalar2=None, op0=ALU.mult, rd=[cst_r, rD[k2]], wr=[rD[k2]])
                            S.add("gpsimd", "dma_start", out=dgS[j, q, half], in_=wkD[:, k2 * 2048:(k2 + 1) * 2048],
                                  rd=[rD[k2]], dma_sem=s5_sems[2 + k2])

            if cfg.mixers and cfg.depth > 1:
                ssm_prologue()
            S.barrier()

            def rmsnorm(c, gc):
                cs = slice(c * TC, (c + 1) * TC)
                for kc in range(8):
                    S.add("scalar", "activation", out=sq[:, kc % 2, :], in_=x[:, kc, cs], func=AF.Square,
                          rd=[xr[c][kc]], wr=[sq_r[kc % 2]])
                    S.add("tensor", "matmul", P[6][:], lhsT=ones[:], rhs=sq[:, kc % 2, :],
                          start=(kc == 0), stop=(kc == 7), rd=[sq_r[kc % 2], cst_r], wr=[ps_r[6]])
                S.add("scalar", "activation", out=P[7][:], in_=P[6][:], func=AF.Sqrt, scale=1.0 / D_MODEL,
                      bias=cst[:, EPSC:EPSC + 1], rd=[ps_r[6], cst_r], wr=[ps_r[7]])
                S.add("vector", "reciprocal", out=P[7][:], in_=P[7][:], rd=[ps_r[7]], wr=[ps_r[7]])
                for kc in range(8):
                    S.add("vector", "scalar_tensor_tensor", out=xn[:, kc, :], in0=x[:, kc, cs],
                          scalar=cst[:, gc + kc:gc + kc + 1], in1=P[7][:], op0=ALU.mult, op1=ALU.mult,
                          rd=[xr[c][kc], ps_r[7], cst_r], wr=[xn_r[kc]])

            def ffn(c, l, which):
                cs = slice(c * TC, (c + 1) * TC)
                nm = "w1" if which == 0 else "w2"
                rmsnorm(c, gcol(l, 0 if which == 0 else 2))
                for j in range(NJ):
                    w, wr_ = wget(wb[nm + "i"][l * NJ + j], 2048)
                    wv = w.rearrange("p (g k n) -> p g k n", g=2, k=8)
                    pg = 2 * (j % 2)
                    pu = pg + 1
                    for kc in range(8):
                        S.add("tensor", "matmul", P[pg][:], lhsT=wv[:, 0, kc, :], rhs=xn[:, kc, :],
                              start=(kc == 0), stop=(kc == 7), rd=[wr_, xn_r[kc]], wr=[ps_r[pg]])
                    for kc in range(8):
                        S.add("tensor", "matmul", P[pu][:], lhsT=wv[:, 1, kc, :], rhs=xn[:, kc, :],
                              start=(kc == 0), stop=(kc == 7), rd=[wr_, xn_r[kc]], wr=[ps_r[pu]])
                    S.add("scalar", "activation", out=sg[:, j % 2, :], in_=P[pg][:], func=AF.Silu,
                          rd=[ps_r[pg]], wr=[sg_r[j % 2]])
                    S.add("vector", "tensor_tensor", out=hb[:, j, :], in0=sg[:, j % 2, :], in1=P[pu][:],
                          op=ALU.mult, rd=[sg_r[j % 2], ps_r[pu]], wr=[h_r[j]])
                for oc in range(8):
                    w, wr_ = wget(wb[nm + "o"][l * 8 + oc], NJ * 128)
                    wv = w.rearrange("p (h n) -> p h n", h=NJ)
                    po = 4 + oc % 2
                    for hc in range(NJ):
                        S.add("tensor", "matmul", P[po][:], lhsT=wv[:, hc, :], rhs=hb[:, hc, :],
                              start=(hc == 0), stop=(hc == NJ - 1), rd=[wr_, h_r[hc]], wr=[ps_r[po]])
                    S.add("vector", "scalar_tensor_tensor", out=x[:, oc, cs], in0=P[po][:], scalar=0.5,
                          in1=x[:, oc, cs], op0=ALU.mult, op1=ALU.add,
                          rd=[ps_r[po], xr[c][oc]], wr=[xr[c][oc]])

            def ple(s, c, l):
                cs = slice(c * TC, (c + 1) * TC)
                k = ple.n % 2
                ple.n += 1
                S.add("gpsimd", "dma_start", out=pb[:, k, :, :], in_=pT[s, l, :, :, cs],
                      wr=[pb_r[k]], dma_sem=p_sems[k])
                rmsnorm(c, gcol(l, 3))
                w, wr_ = wget(wb["wpp"][l], 2048)
                wv = w.rearrange("p (k n) -> p k n", k=2)
                for oc in range(8):
                    pp = oc % 2
                    for kc in range(2):
                        S.add("tensor", "matmul", P[pp][:], lhsT=wv[:, kc, oc * 128:(oc + 1) * 128],
                              rhs=pb[:, k, kc, :], start=(kc == 0), stop=(kc == 1),
                              rd=[wr_, pb_r[k]], wr=[ps_r[pp]])
                    S.add("vector", "tensor_copy", out=hb32[:, oc * TC:(oc + 1) * TC], in_=P[pp][:],
                          rd=[ps_r[pp]], wr=[h_r[2 * oc], h_r[2 * oc + 1]])
                    S.add("scalar", "activation", out=sq[:, oc % 2, :], in_=P[pp][:], func=AF.Square,
                          rd=[ps_r[pp]], wr=[sq_r[oc % 2]])
                    S.add("tensor", "matmul", P[6][:], lhsT=ones[:], rhs=sq[:, oc % 2, :],
                          start=(oc == 0), stop=(oc == 7), rd=[sq_r[oc % 2], cst_r], wr=[ps_r[6]])
                S.add("scalar", "activation", out=P[7][:], in_=P[6][:], func=AF.Sqrt, scale=1.0 / D_MODEL,
                      bias=cst[:, EPSC:EPSC + 1], rd=[ps_r[6], cst_r], wr=[ps_r[7]])
                S.add("vector", "reciprocal", out=P[7][:], in_=P[7][:], rd=[ps_r[7]], wr=[ps_r[7]])
                gp = gcol(l, 4)
                for oc in range(8):
                    w, wr_ = wget(wb["wpg"][l * 8 + oc], 1024)
                    wv = w.rearrange("p (k n) -> p k n", k=8)
                    pg = 2 + oc % 2
                    for kc in range(8):
                        S.add("tensor", "matmul", P[pg][:], lhsT=wv[:, kc, :], rhs=xn[:, kc, :],
                              start=(kc == 0), stop=(kc == 7), rd=[wr_, xn_r[kc]], wr=[ps_r[pg]])
                    S.add("scalar", "activation", out=P[pg][:], in_=P[pg][:], func=AF.Sigmoid,
                          rd=[ps_r[pg]], wr=[ps_r[pg]])
                    t = sg[:, oc % 2, :]
                    S.add("vector", "scalar_tensor_tensor", out=t, in0=hb32[:, oc * TC:(oc + 1) * TC],
                          scalar=cst[:, gp + oc:gp + oc + 1], in1=P[7][:], op0=ALU.mult, op1=ALU.mult,
                          rd=[h_r[2 * oc], h_r[2 * oc + 1], ps_r[7], cst_r], wr=[sg_r[oc % 2]])
                    S.add("vector", "tensor_tensor", out=t, in0=t, in1=P[pg][:], op=ALU.mult,
                          rd=[sg_r[oc % 2], ps_r[pg]], wr=[sg_r[oc % 2]])
                    S.add("vector", "tensor_tensor", out=x[:, oc, cs], in0=x[:, oc, cs], in1=t, op=ALU.add,
                          rd=[sg_r[oc % 2], xr[c][oc]], wr=[xr[c][oc]])

            ple.n = 0

            def attn_in(c, l):
                j = l // 2
                cs = slice(c * TC, (c + 1) * TC)
                rmsnorm(c, gcol(l, 1))
                for pc in range(13):
                    w, wr_ = wget(wb["wai"][j * 13 + pc], 1024)
                    wv = w.rearrange("p (k n) -> p k n", k=8)
                    pp = pc % 2
                    k = attn_in.n % 2
                    attn_in.n += 1
                    for kc in range(8):
                        S.add("tensor", "matmul", P[pp][:], lhsT=wv[:, kc, :], rhs=xn[:, kc, :],
                              start=(kc == 0), stop=(kc == 7), rd=[wr_, xn_r[kc]], wr=[ps_r[pp]])
                    S.add("scalar", "activation", out=sq[:, k, :], in_=P[pp][:], func=AF.Square,
                          rd=[ps_r[pp]], wr=[sq_r[k]])
                    S.add("tensor", "matmul", P[2 + pp][:], lhsT=bones[:], rhs=sq[:, k, :], start=True, stop=True,
                          rd=[sq_r[k], cst_r], wr=[ps_r[2 + pp]])
                    S.add("scalar", "activation", out=P[2 + pp][:], in_=P[2 + pp][:], func=AF.Sqrt, scale=1.0 / 64,
                          bias=cst[:, EPSC:EPSC + 1], rd=[ps_r[2 + pp], cst_r], wr=[ps_r[2 + pp]])
                    S.add("vector", "reciprocal", out=sg[:, k, :], in_=P[2 + pp][:], rd=[ps_r[2 + pp]], wr=[sg_r[k]])
                    gi = 0 if pc < 4 else 1 if pc == 4 else 2 if pc < 9 else 3
                    gc = ACOL + j * 12 + gi
                    S.add("vector", "scalar_tensor_tensor", out=stA[:, k, :], in0=P[pp][:], scalar=cst[:, gc:gc + 1],
                          in1=sg[:, k, :], op0=ALU.mult, op1=ALU.mult,
                          rd=[ps_r[pp], sg_r[k], cst_r], wr=[sta_r[k]])
                    S.add("gpsimd", "dma_start", out=qkT[pc, :, cs], in_=stA[:, k, :], rd=[sta_r[k]], wr=[qkT_r],
                          dma_sem=sta_sems[k], nowaw=True)
                w0, wr0 = wget(wb["wav"][j * 2 + 0], 2560)
                w1, wr1 = wget(wb["wav"][j * 2 + 1], 2560)
                wvh = [w0.rearrange("p (k f) -> p k f", k=4), w1.rearrange("p (k f) -> p k f", k=4)]
                wrh = [wr0, wr1]
                for tt in range(4):
                    k = attn_in.nv % 2
                    attn_in.nv += 1
                    for kc in range(8):
                        S.add("tensor", "matmul", P[4][:], lhsT=xn[:, kc, tt * 128:(tt + 1) * 128],
                              rhs=wvh[kc // 4][:, kc % 4, 0:512], start=(kc == 0), stop=(kc == 7),
                              rd=[wrh[kc // 4], xn_r[kc]], wr=[ps_r[4]])
                    for kc in range(8):
                        S.add("tensor", "matmul", P[5][:, 0:128], lhsT=xn[:, kc, tt * 128:(tt + 1) * 128],
                              rhs=wvh[kc // 4][:, kc % 4, 512:640], start=(kc == 0), stop=(kc == 7),
                              rd=[wrh[kc // 4], xn_r[kc]], wr=[ps_r[5]])
                    S.add("scalar", "activation", out=vst[:, k, 0:512], in_=P[4][:], func=AF.Copy,
                          rd=[ps_r[4]], wr=[vst_r[k]])
                    S.add("scalar", "activation", out=vst[:, k, 512:640], in_=P[5][:, 0:128], func=AF.Copy,
                          rd=[ps_r[5], vst_r[k]], wr=[vst_r[k]])
                    S.add("gpsimd", "dma_start", out=vS[c * 4 + tt], in_=vst[:, k, :], rd=[vst_r[k]], wr=[vS_r],
                          dma_sem=vst_sems[k], nowaw=True)

            attn_in.n = 0
            attn_in.nv = 0

            def attn_layer(l):
                j = l // 2
                passes = []
                for n in range(32):
                    for i in range(8):
                        for kind in (0, 1):
                            passes.append((n, i, kind, len(passes) % 2))

                def ctx(pz):
                    n, i, kind, u = pz
                    lo = min(max(n - 2, 0), 27)
                    low = min(max(n - 1, 0), 29)
                    if kind == 0:
                        cc, hf, nk = i % 4, i // 4, 3
                    else:
                        cc, hf, nk = i // 2, i % 2, 5
                    return dict(n=n, i=i, kind=kind, u=u, lo=lo, dw=low - lo, qs=n % 2, bs=(n * 8 + i) % 2,
                                typ={0: 0, 1: 1, 30: 3, 31: 4}.get(n, 2), ts=slice(n * 128, (n + 1) * 128),
                                cc=cc, hf=hf, nk=nk, rows=slice(64 * hf, 64 * hf + 64))

                def front(pz):
                    c_ = ctx(pz)
                    n, i, kind, u, qs, bs, rows, cc, nk = (c_[k] for k in ("n", "i", "kind", "u", "qs", "bs", "rows", "cc", "nk"))
                    if i == 0 and kind == 0:
                        ts = c_["ts"]
                        ks = slice(c_["lo"] * 128, c_["lo"] * 128 + 640)
                        G(qtt[:, qs, 0:4, :], qkT[0:4, :, ts].rearrange("j p t -> p j t"), rd=[qkT_r], wr=[qt_r[qs]],
                          sem=qt_sems[qs])
                        G(qtt[:, qs, 4:8, :], qkT[5:9, :, ts].rearrange("j p t -> p j t"), rd=[qkT_r], wr=[qt_r[qs]],
                          sem=qt_sems[qs], nowaw=True, grp=True)
                        G(kaT, qkT[4, :, ks], rd=[qkT_r], wr=[ka_r], sem=kv_sems[0])
                        G(kbT, qkT[9:13, :, ks].rearrange("j p t -> p j t"), rd=[qkT_r], wr=[kb_r], sem=kv_sems[1])
                    if kind == 0:
                        G(biast[:, bs, :], ball[j, c_["typ"], i], wr=[bias_r[bs]], sem=bi_sems[bs])
                    qap = qtt[rows, qs, (cc if kind == 0 else 4 + cc), :]
                    for jj in range(nk):
                        if kind == 0:
                            kap = kaT[rows, (c_["dw"] + jj) * 128:(c_["dw"] + jj + 1) * 128]
                            kres = ka_r
                        else:
                            kap = kbT[rows, cc, jj * 128:(jj + 1) * 128]
                            kres = kb_r
                        if jj < 4:
                            o, ores = P[u][:, jj * 128:(jj + 1) * 128], ps_r[u]
                        else:
                            o, ores = P[2 + u][:, 0:128], ps_r[2 + u]
                        T(o, lhsT=kap, rhs=qap, start=True, stop=True, rd=[kres, qt_r[qs]], wr=[ores])

                def mid(pz):
                    c_ = ctx(pz)
                    kind, u, bs, nk = c_["kind"], c_["u"], c_["bs"], c_["nk"]
                    if kind == 0:
                        V("scalar_tensor_tensor", out=et[:, u, 0:384], in0=P[u][:, 0:384], scalar=0.125,
                          in1=biast[:, bs, 640:1024], op0=ALU.mult, op1=ALU.add, rd=[ps_r[u], bias_r[bs]], wr=[e_r[u]])
                    else:
                        V("scalar_tensor_tensor", out=et[:, u, 0:512], in0=P[u][:, 0:512], scalar=0.125,
                          in1=biast[:, bs, 0:512], op0=ALU.mult, op1=ALU.add, rd=[ps_r[u], bias_r[bs]], wr=[e_r[u]])
                        V("scalar_tensor_tensor", out=et[:, u, 512:640], in0=P[2 + u][:, 0:128], scalar=0.125,
                          in1=biast[:, bs, 512:640], op0=ALU.mult, op1=ALU.add,
                          rd=[ps_r[2 + u], bias_r[bs], e_r[u]], wr=[e_r[u]])
                    A("activation", out=pTt[:, u, 0:nk * 128], in_=et[:, u, 0:nk * 128], func=AF.Exp,
                      rd=[e_r[u]], wr=[pT_r[u]])

                def back(pz):
                    c_ = ctx(pz)
                    n, i, kind, u, rows, nk, hf = (c_[k] for k in ("n", "i", "kind", "u", "rows", "nk", "hf"))
                    if i == 0 and kind == 0:
                        G(Vt, vS[c_["lo"]:c_["lo"] + 5].rearrange("t p f -> p t f"), rd=[vS_r], wr=[v_r], sem=kv_sems[2])
                    for jj in range(nk):
                        if kind == 0:
                            vap = Vt[:, c_["dw"] + jj, 512 + 64 * hf:512 + 64 * hf + 64]
                        else:
                            vap = Vt[:, jj, i * 64:(i + 1) * 64]
                        T(P[4 + u][rows, 0:128], lhsT=vap, rhs=pTt[:, u, jj * 128:(jj + 1) * 128], start=(jj == 0),
                          stop=(jj == nk - 1), rd=[v_r, pT_r[u]], wr=[ps_r[4 + u]])
                    for jj in range(nk):
                        T(P[6 + u][:, 0:128], lhsT=ones[:], rhs=pTt[:, u, jj * 128:(jj + 1) * 128], start=(jj == 0),
                          stop=(jj == nk - 1), rd=[cst_r, pT_r[u]], wr=[ps_r[6 + u]])

                def post(pz):
                    c_ = ctx(pz)
                    n, i, kind, u, rows, qs, cc = (c_[k] for k in ("n", "i", "kind", "u", "rows", "qs", "cc"))
                    if kind == 0:
                        V("tensor_scalar", out=dent[rows, u, 0:128], in0=P[6 + u][rows, 0:128],
                          scalar1=esink[rows, j * 8 + i:j * 8 + i + 1], scalar2=None, op0=ALU.add,
                          rd=[ps_r[6 + u], cst_r], wr=[den_r[u]])
                        V("reciprocal", out=dent[rows, u, 0:128], in_=dent[rows, u, 0:128], rd=[den_r[u]], wr=[den_r[u]])
                    else:
                        V("reciprocal", out=dent[rows, u, 0:128], in_=P[6 + u][rows, 0:128], rd=[ps_r[6 + u]],
                          wr=[den_r[u]])
                    ych = cc if kind == 0 else 4 + cc
                    V("tensor_tensor", out=ystt[rows, qs, ych, :], in0=P[4 + u][rows, 0:128], in1=dent[rows, u, 0:128],
                      op=ALU.mult, rd=[ps_r[4 + u], den_r[u]], wr=[yst_r[qs]], nowaw=True)
                    if i == 7 and kind == 1:
                        G(ymix[:, :, c_["ts"]], ystt[:, qs, :, :], rd=[yst_r[qs]], wr=[ym_r], sem=ys_sems[qs], nowaw=True)

                front(passes[0])
                for ii, pz in enumerate(passes):
                    if ii + 1 < len(passes):
                        front(passes[ii + 1])
                    mid(pz)
                    if ii >= 1:
                        post(passes[ii - 1])
                    back(pz)
                post(passes[-1])

            def mix_out(c, l, wname):
                j = l // 2
                cs = slice(c * TC, (c + 1) * TC)
                S.add("gpsimd", "dma_start", out=xn[:], in_=ymix[:, :, cs], rd=[ym_r], wr=xn_r, dma_sem=ym_sem)
                for oc in range(8):
                    w, wr_ = wget(wb[wname][j * 8 + oc], 1024)
                    wv = w.rearrange("p (k n) -> p k n", k=8)
                    po = 4 + oc % 2
                    for kc in range(8):
                        S.add("tensor", "matmul", P[po][:], lhsT=wv[:, kc, :], rhs=xn[:, kc, :],
                              start=(kc == 0), stop=(kc == 7), rd=[wr_, xn_r[kc]], wr=[ps_r[po]])
                    S.add("vector", "tensor_tensor", out=x[:, oc, cs], in0=P[po][:], in1=x[:, oc, cs], op=ALU.add,
                          rd=[ps_r[po], xr[c][oc]], wr=[xr[c][oc]])


            uS_r, hhS_r, zS_r = Res(), Res(), Res()
            hh_r, hc_r, tmpc_r, ystc_r = Res(), Res(), [Res(), Res()], Res()
            uq_r, yq_r, s5w_r = Res(), Res(), Res()
            tab_r = [Res() for _ in range(8)]
            rot_r, gt_r, G_r = [Res(), Res()], [Res(), Res()], [Res(), Res()]
            stc_r = [Res() for _ in range(8)]
            rho_r = [Res(), Res()]
            zt_r = Res()

            def ssm_in(c, l):
                j = l // 2
                cs = slice(c * TC, (c + 1) * TC)
                rmsnorm(c, gcol(l, 1))
                for q in range(4):
                    w, wr_ = wget(wb["wsi"][j * 12 + q], 1024)
                    wv = w.rearrange("p (k n) -> p k n", k=8)
                    pp = q % 2
                    k = attn_in.n % 2
                    attn_in.n += 1
                    for kc in range(8):
                        T(P[pp][:], lhsT=wv[:, kc, :], rhs=xn[:, kc, :], start=(kc == 0), stop=(kc == 7),
                          rd=[wr_, xn_r[kc]], wr=[ps_r[pp]])
                    A("activation", out=stA[:, k, :], in_=P[pp][:], func=AF.Copy, rd=[ps_r[pp]], wr=[sta_r[k]])
                    G(uS[q, :, cs], stA[:, k, :], rd=[sta_r[k]], wr=[uS_r], sem=sta_sems[k], nowaw=True)
                for q in range(4):
                    pa = 2 + 2 * (q % 2)
                    pg = pa + 1
                    k = attn_in.n % 2
                    attn_in.n += 1
                    for pc, pbank in ((4 + q, pa), (8 + q, pg)):
                        w, wr_ = wget(wb["wsi"][j * 12 + pc], 1024)
                        wv = w.rearrange("p (k n) -> p k n", k=8)
                        for kc in range(8):
                            T(P[pbank][:], lhsT=wv[:, kc, :], rhs=xn[:, kc, :], start=(kc == 0), stop=(kc == 7),
                              rd=[wr_, xn_r[kc]], wr=[ps_r[pbank]])
                    A("activation", out=sg[:, k, :], in_=P[pg][:], func=AF.Sigmoid, rd=[ps_r[pg]], wr=[sg_r[k]])
                    V("tensor_tensor", out=stA[:, k, :], in0=sg[:, k, :], in1=P[pa][:], op=ALU.mult,
                      rd=[sg_r[k], ps_r[pa]], wr=[sta_r[k]])
                    G(hhS[q, :, 16 + c * TC:16 + (c + 1) * TC], stA[:, k, :], rd=[sta_r[k]], wr=[hhS_r],
                      sem=sta_sems[k], nowaw=True)

            def conv_chunk(c, l):
                j = l // 2
                cs = slice(c * TC, (c + 1) * TC)
                base = SCOL + j * 144
                G(hhwin[:, :, 0:542], hhS[:, :, c * TC + 1:c * TC + 543].rearrange("q p t -> p q t"),
                  rd=[hhS_r], wr=[hh_r], sem=kv_sems[0])
                for q in range(4):
                    wA, rA_ = wget(dgS[j, q, 0], 2048)
                    wB, rB_ = wget(dgS[j, q, 1], 2048)
                    wv = [wA.rearrange("p (k n) -> p k n", k=16), wB.rearrange("p (k n) -> p k n", k=16)]
                    rr = [rA_, rB_]
                    pp = q % 2
                    for k in range(31):
                        T(P[pp][:], lhsT=wv[k // 16][:, k % 16, :], rhs=hhwin[:, q, k:k + TC], start=(k == 0),
                          stop=(k == 30), rd=[rr[k // 16], hh_r], wr=[ps_r[pp]])
                    cb = cst[:, base + 124 + q:base + 125 + q]
                    A("activation", out=hcv[:, q, :], in_=P[pp][:], func=AF.Identity, bias=cb,
                      rd=[ps_r[pp], cst_r], wr=[hc_r], nowaw=True)
                    A("activation", out=sq[:, 0, :], in_=P[pp][:], func=AF.Identity, bias=cb,
                      rd=[ps_r[pp], cst_r], wr=[sq_r[0]])
                    A("activation", out=sq[:, 1, :], in_=P[pp][:], func=AF.Square, bias=cb,
                      rd=[ps_r[pp], cst_r], wr=[sq_r[1]])
                    T(P[2][:], lhsT=ones[:], rhs=sq[:, 0, :], start=(q == 0), stop=(q == 3),
                      rd=[sq_r[0], cst_r], wr=[ps_r[2]])
                    T(P[3][:], lhsT=ones[:], rhs=sq[:, 1, :], start=(q == 0), stop=(q == 3),
                      rd=[sq_r[1], cst_r], wr=[ps_r[3]])
                A("activation", out=msq, in_=P[2][:], func=AF.Square, scale=1.0 / 512, rd=[ps_r[2]], wr=[tmpc_r[0]])
                A("activation", out=P[4][:], in_=P[2][:], func=AF.Copy, scale=1.0 / 512, rd=[ps_r[2]], wr=[ps_r[4]])
                V("scalar_tensor_tensor", out=msq, in0=P[3][:], scalar=1.0 / 512, in1=msq, op0=ALU.mult,
                  op1=ALU.subtract, rd=[ps_r[3], tmpc_r[0]], wr=[tmpc_r[0]])
                A("activation", out=P[5][:], in_=msq, func=AF.Sqrt, bias=cst[:, EPSC:EPSC + 1],
                  rd=[tmpc_r[0], cst_r], wr=[ps_r[5]])
                V("reciprocal", out=P[5][:], in_=P[5][:], rd=[ps_r[5]], wr=[ps_r[5]])
                for q in range(4):
                    k = q % 2
                    V("tensor_tensor", out=tmpc[:, k, :], in0=hcv[:, q, :], in1=P[4][:], op=ALU.subtract,
                      rd=[hc_r, ps_r[4]], wr=[tmpc_r[k]] if False else [rot_r[k]])
                    V("scalar_tensor_tensor", out=tmpc[:, k, :], in0=tmpc[:, k, :],
                      scalar=cst[:, base + 128 + q:base + 129 + q], in1=P[5][:], op0=ALU.mult, op1=ALU.mult,
                      rd=[rot_r[k], ps_r[5], cst_r], wr=[rot_r[k]])
                    A("activation", out=ystc[:, q, :], in_=tmpc[:, k, :], func=AF.Silu,
                      bias=cst[:, base + 132 + q:base + 133 + q], rd=[rot_r[k], cst_r], wr=[ystc_r], nowaw=True)
                G(ymix[:, 4:8, cs], ystc, rd=[ystc_r], wr=[ym_r], sem=ys_sems[0], nowaw=True)

            def s5_scan(l):
                j = l // 2
                base = SCOL + j * 144
                nch = SEQ // TS
                for q in range(4):
                    G(uTq, uS[q], rd=[uS_r], wr=[uq_r], sem=kv_sems[1])
                    for d, gh in ((0, 0), (0, 1), (1, 0), (1, 1)):
                        G(s5wt[:].rearrange("p g a n -> p (g a n)"), s5W[j, d, q][:, gh * 2048:(gh + 1) * 2048],
                          wr=[s5w_r], sem=kv_sems[2])
                        for g4 in range(4):
                            G(tab_ap[g4], tabS[j, d, q * 8 + gh * 4 + g4].rearrange("p (c t) -> p c t", c=2),
                              wr=[tab_r[g4]], sem=tb_sems[g4 % 2])
                            V("memset", stc[:, g4:g4 + 1], 0.0, wr=[stc_r[g4]])
                        its = []
                        for cc in range(nch):
                            for g8 in range(4):
                                u = s5_scan.it % 2
                                s5_scan.it += 1
                                its.append((cc, g8, u))

                        def ctx(it):
                            cc, g8, u = it
                            if d == 0:
                                rng = slice(cc * TS, (cc + 1) * TS)
                                urhs = uTq[:, rng]
                            else:
                                rng = slice(SEQ - (cc + 1) * TS, SEQ - cc * TS)
                                urhs = uTq[:, rng][:, ::-1]
                            return cc, g8, u, rng, urhs, 2 + cc % 2, d * 32 + q * 8 + gh * 4 + g8

                        def front(it):
                            cc, g8, u, rng, urhs, py, dg = ctx(it)
                            T(P[u][:, 0:TS], lhsT=s5wt[:, g8, 0, :], rhs=urhs, start=True, stop=True,
                              rd=[s5w_r, uq_r], wr=[ps_r[u]])
                            T(P[u][:, TS:2 * TS], lhsT=s5wt[:, g8, 1, :], rhs=urhs, start=True, stop=True,
                              rd=[s5w_r, uq_r], wr=[ps_r[u]])

                        def mid(it):
                            cc, g8, u, rng, urhs, py, dg = ctx(it)
                            cosT = tab_ap[g8][:, 0, :]
                            sinT = tab_ap[g8][:, 1, :]
                            A("activation", out=rhoT[:, u, :], in_=cosT, func=AF.Identity, scale=0.0,
                              bias=s5p[:, j, 0, dg:dg + 1], rd=[tab_r[g8], cst_r], wr=[rho_r[u]])
                            V("tensor_tensor", out=rot2[:, 0, :], in0=P[u][:, TS:2 * TS], in1=sinT, op=ALU.mult,
                              rd=[ps_r[u], tab_r[g8]], wr=[rot_r[0]])
                            V("tensor_tensor", out=rot2[:, 1, :], in0=P[u][:, 0:TS], in1=cosT, op=ALU.mult,
                              rd=[ps_r[u], tab_r[g8]], wr=[rot_r[1]])
                            V("tensor_tensor", out=rot2[:, 1, :], in0=rot2[:, 1, :], in1=rot2[:, 0, :], op=ALU.add,
                              rd=[rot_r[0], rot_r[1]], wr=[rot_r[1]])
                            V("tensor_tensor_scan", out=gt2[:, u, :], data0=rhoT[:, u, :], data1=rot2[:, 1, :],
                              initial=stc[:, g8:g8 + 1], op0=ALU.mult, op1=ALU.add,
                              rd=[rho_r[u], rot_r[1], stc_r[g8]], wr=[gt_r[u]])
                            V("tensor_tensor", out=G4[:, u, :], in0=gt2[:, u, :], in1=cosT, op=ALU.mult,
                              rd=[gt_r[u], tab_r[g8]], wr=[G_r[u]])
                            V("tensor_tensor", out=G4[:, 2 + u, :], in0=gt2[:, u, :], in1=sinT, op=ALU.mult,
                              rd=[gt_r[u], tab_r[g8]], wr=[G_r[u]])

                        def back(it):
                            cc, g8, u, rng, urhs, py, dg = ctx(it)
                            T(P[4 + u][:, 0:2], lhsT=swm[:], rhs=gt2[:, u, TS - 2:TS], start=True, stop=True,
                              rd=[gt_r[u], cst_r], wr=[ps_r[4 + u]])
                            T(P[py][:, 0:TS], lhsT=s5wt[:, g8, 2, :], rhs=G4[:, u, :], start=(g8 == 0), stop=False,
                              rd=[s5w_r, G_r[u]], wr=[ps_r[py]])
                            T(P[py][:, 0:TS], lhsT=s5wt[:, g8, 3, :], rhs=G4[:, 2 + u, :], start=False,
                              stop=(g8 == 3), rd=[s5w_r, G_r[u]], wr=[ps_r[py]])

                        def post(it):
                            cc, g8, u, rng, urhs, py, dg = ctx(it)
                            V("tensor_scalar", out=stc[:, 8 + g8:9 + g8], in0=P[4 + u][:, 1:2],
                              scalar1=s5p[:, j, 2, dg:dg + 1], scalar2=None, op0=ALU.mult,
                              rd=[ps_r[4 + u], cst_r], wr=[stc_r[g8]])
                            V("scalar_tensor_tensor", out=stc[:, g8:g8 + 1], in0=gt2[:, u, TS - 1:TS],
                              scalar=s5p[:, j, 1, dg:dg + 1], in1=stc[:, 8 + g8:9 + g8], op0=ALU.mult,
                              op1=ALU.subtract, rd=[gt_r[u], stc_r[g8], cst_r], wr=[stc_r[g8]])
                            if g8 != 3:
                                return
                            if d == 0 and gh == 0:
                                V("scalar_tensor_tensor", out=yq[:, rng], in0=uTq[:, rng],
                                  scalar=cst[:, base + 136 + q:base + 137 + q], in1=P[py][:, 0:TS], op0=ALU.mult,
                                  op1=ALU.add, rd=[uq_r, ps_r[py], cst_r], wr=[yq_r], nowaw=True)
                            elif d == 0:
                                V("tensor_tensor", out=yq[:, rng], in0=yq[:, rng], in1=P[py][:, 0:TS],
                                  op=ALU.add, rd=[yq_r, ps_r[py]], wr=[yq_r], nowaw=True)
                            else:
                                V("tensor_tensor", out=yq[:, rng][:, ::-1], in0=yq[:, rng][:, ::-1], in1=P[py][:, 0:TS],
                                  op=ALU.add, rd=[yq_r, ps_r[py]], wr=[yq_r], nowaw=True)

                        front(its[0])
                        for ii, it in enumerate(its):
                            if ii + 1 < len(its):
                                front(its[ii + 1])
                            mid(it)
                            if ii >= 1:
                                post(its[ii - 1])
                            back(it)
                        post(its[-1])
                    for c in range(NCH):
                        cs = slice(c * TC, (c + 1) * TC)
                        k = c % 2
                        A("activation", out=rot512, in_=yq[:, cs], func=AF.Square, rd=[yq_r], wr=[rot_r[0], rot_r[1]])
                        V("tensor_scalar", out=rot512, in0=rot512, scalar1=0.044715, scalar2=1.0, op0=ALU.mult,
                          op1=ALU.add, rd=[rot_r[0], rot_r[1]], wr=[rot_r[0], rot_r[1]])
                        V("tensor_tensor", out=rot512, in0=rot512, in1=yq[:, cs], op=ALU.mult,
                          rd=[rot_r[0], rot_r[1], yq_r], wr=[rot_r[0], rot_r[1]])
                        A("activation", out=gt512, in_=rot512, func=AF.Sigmoid, scale=1.5957691216057308,
                          rd=[rot_r[0], rot_r[1]], wr=[gt_r[0], gt_r[1]])
                        V("tensor_tensor", out=g512[:, k, :], in0=yq[:, cs], in1=gt512, op=ALU.mult,
                          rd=[yq_r, gt_r[0], gt_r[1]], wr=[G_r[0], G_r[1]])
                        G(zS[q, :, cs], g512[:, k, :], rd=[G_r[0], G_r[1]], wr=[zS_r], sem=zs_sems[k], nowaw=True)

            s5_scan.it = 0

            def ssm_out(c, l):
                j = l // 2
                cs = slice(c * TC, (c + 1) * TC)
                base = SCOL + j * 144
                zt = pbf.rearrange("p (q t) -> p q t", q=4)
                G(zt, zS[:, :, cs].rearrange("q p t -> p q t"), rd=[zS_r], wr=[pb_r[0], pb_r[1]], sem=p_sems[0])
                G(xn[:, 4:8, :], ymix[:, 4:8, cs], rd=[ym_r], wr=xn_r[4:8], sem=ym_sem)
                for oc in range(4):
                    w, wr_ = wget(wb["wsg"][j * 4 + oc], 512)
                    wv = w.rearrange("p (k n) -> p k n", k=4)
                    pp = oc % 2
                    for kc in range(4):
                        T(P[pp][:], lhsT=wv[:, kc, :], rhs=zt[:, kc, :], start=(kc == 0), stop=(kc == 3),
                          rd=[wr_, pb_r[0], pb_r[1]], wr=[ps_r[pp]])
                    A("activation", out=sg[:, pp, :], in_=P[pp][:], func=AF.Sigmoid,
                      bias=cst[:, base + 140 + oc:base + 141 + oc], rd=[ps_r[pp], cst_r], wr=[sg_r[pp]])
                    V("tensor_tensor", out=xn[:, oc, :], in0=zt[:, oc, :], in1=sg[:, pp, :], op=ALU.mult,
                      rd=[pb_r[0], pb_r[1], sg_r[pp]], wr=[xn_r[oc]])
                for oc in range(8):
                    w, wr_ = wget(wb["wso"][j * 8 + oc], 1024)
                    wv = w.rearrange("p (k n) -> p k n", k=8)
                    po = 4 + oc % 2
                    for kc in range(8):
                        T(P[po][:], lhsT=wv[:, kc, :], rhs=xn[:, kc, :], start=(kc == 0), stop=(kc == 7),
                          rd=[wr_, xn_r[kc]], wr=[ps_r[po]])
                    V("tensor_tensor", out=x[:, oc, cs], in0=P[po][:], in1=x[:, oc, cs], op=ALU.add,
                      rd=[ps_r[po], xr[c][oc]], wr=[xr[c][oc]])

            outs = []
            for s in range(NS):
                for c in range(NCH):
                    cs = slice(c * TC, (c + 1) * TC)
                    S.add("gpsimd", "dma_start", out=x[:, :, cs], in_=xT[s, :, :, cs],
                          wr=xr[c], dma_sem=x_sems[c], extra=(lst if s == 0 else []))
                for l in range(cfg.depth):
                    for c in range(NCH):
                        if cfg.stage >= 1:
                            ffn(c, l, 0)
                    has_mix = cfg.mixers and (l % 2 == 0)
                    has_ssm = cfg.mixers and (l % 2 == 1)
                    if has_ssm:
                        for c in range(NCH):
                            ssm_in(c, l)
                        S.barrier()
                        for c in range(NCH):
                            conv_chunk(c, l)
                        S.barrier()
                        s5_scan(l)
                        S.barrier()
                    if has_mix:
                        for c in range(NCH):
                            attn_in(c, l)
                        S.barrier()
                        attn_layer(l)
                        S.barrier()
                    for c in range(NCH):
                        if has_mix:
                            mix_out(c, l, "wao")
                        if has_ssm:
                            ssm_out(c, l)
                        if cfg.stage >= 2:
                            ffn(c, l, 1)
                        if cfg.stage >= 3:
                            ple(s, c, l)
                        if l == cfg.depth - 1:
                            cs = slice(c * TC, (c + 1) * TC)
                            o = S.add("gpsimd", "dma_start", out=yT[s, :, :, cs], in_=x[:, :, cs],
                                      rd=xr[c], dma_sem=x_sems[c])
                            outs.append(o)
            S.add("gpsimd", "nop", extra=([] if S.dry else outs))
            if S.dry:
                record.reqs = WS.reqs

        EPSC = ncst - 1
        S.dry = True
        record()
        S.dry = False
        record()
        S.emit(nc, block, eng_sems)
    return nc, S


_CACHE = {}


def prep_inputs(inp, cfg):
    xs = np.concatenate([np.asarray(inp["x_prompt"]), np.asarray(inp["x_sample"])], axis=0)
    ps = np.concatenate([np.asarray(inp["p_prompt"]), np.asarray(inp["p_sample"])], axis=1)
    W = pack_weights(inp)
    W["ball"] = attn_tables(inp)
    W.update(ssm_params(inp))
    cst = pack_consts(inp)
    cst = np.concatenate([cst, np.full((128, 1), EPS, np.float32)], axis=1)
    return xs, ps, W, cst


def core_inputs(xs, ps, W, cst, sl):
    NS = len(sl)
    xT = np.ascontiguousarray(xs[sl].reshape(NS, SEQ, 8, 128).transpose(0, 3, 2, 1))
    pT = np.ascontiguousarray(ps[:, sl].reshape(DEPTH, NS, SEQ, 2, 128).transpose(1, 0, 4, 3, 2))
    m = {"xT": xT, "pT": pT, "cst": cst}
    m.update(W)
    return m


def kernel(**inp):
    cfg = Cfg()
    inp = {k: np.asarray(v) for k, v in inp.items()}
    xs, ps, W, cst = prep_inputs(inp, cfg)
    nseq_total = xs.shape[0]
    wshapes = {nm: (a.shape[0], a.shape[2]) for nm, a in W.items() if nm.startswith("w")}
    nc, S = build_program(cfg, wshapes, cst.shape[1])
    in_maps = []
    NS = cfg.nseq
    for core in range(N_CORES):
        sl = [(core * NS + i) % nseq_total for i in range(NS)]
        in_maps.append(core_inputs(xs, ps, W, cst, sl))
    res = run_bass_kernel_spmd(nc, in_maps, core_ids=list(range(N_CORES)))
    ys = np.zeros((nseq_total, SEQ, D_MODEL), np.float32)
    for core in range(N_CORES):
        yT = res.results[core]["yT"]
        y = yT.transpose(0, 3, 2, 1).reshape(NS, SEQ, D_MODEL)
        for i in range(NS):
            ys[(core * NS + i) % nseq_total] = y[i]
    nb = inp["x_prompt"].shape[0]
    return ys[:nb], ys[nb:]
```

```python
import numpy as np
import concourse.bass as bass
import concourse.mybir as mybir
from concourse.bass_utils import run_bass_kernel_spmd

F32 = mybir.dt.float32
BF16 = mybir.dt.bfloat16
AF = mybir.ActivationFunctionType
ALU = mybir.AluOpType

D_MODEL = 1024
SEQ = 4096
DEPTH = 4
D_FF = 2816
NJ = D_FF // 128
PLE_DIM = 256
TC = 512
NCH = SEQ // TC
EPS = 1e-6
N_CORES = 8
SEQ_PER_CORE = 3


class Res:
    __slots__ = ("name", "w", "rd", "psum", "co")

    def __init__(self, name="", psum=False):
        self.name = name
        self.w = None
        self.rd = []
        self.psum = psum
        self.co = []


class Ins:
    __slots__ = ("eng", "meth", "args", "kw", "deps", "pos", "sig", "val", "sem", "is_dma")


ENGS = ["tensor", "scalar", "vector", "gpsimd", "sync"]


class Sched:
    def __init__(self):
        self.by_eng = {e: [] for e in ENGS}
        self.dry = False
        self.sem_counts = {}
        self.n = 0
        self.dma_pending = []
        self.last_dma = {}

    def add(self, eng, meth, *args, rd=(), wr=(), dma_sem=None, extra=(), nowaw=False, grp=False, **kw):
        if self.dry:
            return None
        i = Ins()
        i.eng = eng
        i.meth = meth
        i.args = args
        i.kw = kw
        i.pos = len(self.by_eng[eng])
        i.sig = False
        i.val = 0
        i.sem = dma_sem
        i.is_dma = dma_sem is not None
        if i.is_dma:
            self.sem_counts[id(dma_sem)] = self.sem_counts.get(id(dma_sem), 0) + 16
            i.val = self.sem_counts[id(dma_sem)]
        raw = set()
        oth = set()
        waw = set()
        for r in rd:
            if r.w is not None:
                raw.add(r.w)
            raw.update(r.co)
            if r.psum:
                for j in r.rd:
                    if j.eng != eng:
                        oth.add(j)
        for r in wr:
            if r.w is not None and not nowaw:
                oth.add(r.w)
                oth.update(r.co)
                waw.add(r.w)
                waw.update(r.co)
            oth.update(r.rd)
        deps = []
        raw.update(extra)
        if i.is_dma:
            prev = self.last_dma.get(id(dma_sem))
            if prev is not None and not grp:
                raw.add(prev)
            self.last_dma[id(dma_sem)] = i
        for d in raw | oth:
            if d is i:
                continue
            if d.is_dma or i.is_dma:
                deps.append(d)
            elif d.eng != eng:
                deps.append(d)
            else:
                if eng != "tensor":
                    deps.append(d)
        for d in deps:
            if not d.is_dma:
                d.sig = True
        i.deps = deps
        for r in rd:
            r.rd.append(i)
        for r in wr:
            if nowaw and r.w is not None:
                r.co = r.co[-64:] + [r.w]
            else:
                r.co = []
            r.w = i
            r.rd = []
        self.by_eng[eng].append(i)
        self.n += 1
        if i.is_dma:
            self.dma_pending.append(i)
        return i

    def barrier(self):
        if self.dry:
            return
        lasts = [self.by_eng[e][-1] for e in ENGS if self.by_eng[e]]
        deps = lasts + self.dma_pending
        self.dma_pending = []
        for e in ENGS:
            self.add(e, "nop", extra=deps)

    def emit(self, nc, block, eng_sems):
        for e in ENGS:
            cnt = 0
            for i in self.by_eng[e]:
                if i.is_dma:
                    continue
                if i.sig:
                    cnt += 1
                    i.val = cnt
                    i.sem = eng_sems[e]

        def run(e, handle):
            waited = {}
            for i in self.by_eng[e]:
                need = {}
                for d in i.deps:
                    key = id(d.sem)
                    if key not in need or need[key][1] < d.val:
                        need[key] = (d.sem, d.val)
                for key, (sm, val) in need.items():
                    if waited.get(key, 0) < val:
                        handle.wait_ge(sm, val)
                        waited[key] = val
                bi = getattr(handle, i.meth)(*i.args, **i.kw)
                if i.is_dma:
                    bi.then_inc(i.sem, 16)
                elif i.sig:
                    bi.then_inc(i.sem, 1)

        @block.tensor
        def _(h):
            run("tensor", h)

        @block.scalar
        def _(h):
            run("scalar", h)

        @block.vector
        def _(h):
            run("vector", h)

        @block.gpsimd
        def _(h):
            run("gpsimd", h)

        @block.sync
        def _(h):
            run("sync", h)


def blk_in(w, ncols_blocks=None):
    K, N = w.shape
    return np.ascontiguousarray(w.reshape(K // 128, 128, N // 128, 128).transpose(2, 1, 0, 3))


BIGW = {}


def pack_weights(inp):
    out = {}
    ffn_w = {"ffn1": (inp["w_ffn1_in"], inp["w_ffn1_out"]), "ffn2": (inp["w_ffn2_in"], inp["w_ffn2_out"])}
    for nm in ("ffn1", "ffn2"):
        wi, wo = ffn_w[nm]
        a = []
        for l in range(DEPTH):
            g = blk_in(wi[l][:, :D_FF])
            u = blk_in(wi[l][:, D_FF:])
            a.append(np.stack([g, u], axis=2))
        out["w%si" % nm[-1]] = np.stack(a).reshape(DEPTH * NJ, 128, 2 * 8 * 128)
        b = [blk_in(wo[l]) for l in range(DEPTH)]
        out["w%so" % nm[-1]] = np.stack(b).reshape(DEPTH * 8, 128, NJ * 128)
    out["wpg"] = np.stack([blk_in(inp["w_ple_gate"][l]) for l in range(DEPTH)]).reshape(DEPTH * 8, 128, 1024)
    out["wpp"] = np.ascontiguousarray(
        inp["w_ple_proj"].reshape(DEPTH, 2, 128, 1024).transpose(0, 2, 1, 3)).reshape(DEPTH, 128, 2048)
    A_Q, A_KV, B_Q = 512, 128, 512
    wai, wav, wao = [], [], []
    for j in range(2):
        wi = inp["w_attn_in"][j]
        qa = wi[:, :A_Q]
        cols = []
        for cc in range(4):
            cols.append(qa[:, cc * 64:(cc + 1) * 64])
            cols.append(qa[:, (4 + cc) * 64:(5 + cc) * 64])
        cols.append(wi[:, A_Q:A_Q + A_KV])
        cols.append(wi[:, A_Q + 2 * A_KV:A_Q + 2 * A_KV + B_Q])
        cols.append(wi[:, A_Q + 2 * A_KV + B_Q:A_Q + 2 * A_KV + 2 * B_Q])
        fm = np.concatenate(cols, axis=1)
        wai.append(blk_in(fm).reshape(13, 128, 1024))
        vv = np.concatenate([wi[:, A_Q + 2 * A_KV + 2 * B_Q:], wi[:, A_Q + A_KV:A_Q + 2 * A_KV]], axis=1)
        vv = vv.reshape(2, 4, 128, 640).transpose(0, 2, 1, 3)
        wav.append(np.ascontiguousarray(vv).reshape(2, 128, 2560))
        wo = inp["w_attn_out"][j]
        rows = []
        for cc in range(4):
            rows.append(wo[cc * 64:(cc + 1) * 64])
            rows.append(wo[(4 + cc) * 64:(5 + cc) * 64])
        rows.append(wo[512:])
        wao.append(blk_in(np.concatenate(rows, axis=0)).reshape(8, 128, 1024))
    out["wai"] = np.concatenate(wai)
    out["wav"] = np.concatenate(wav)
    out["wao"] = np.concatenate(wao)
    out["wsi"] = np.concatenate([blk_in(inp["w_ssm_in"][j]).reshape(12, 128, 1024) for j in range(2)])
    out["wsg"] = np.concatenate([blk_in(inp["w_glu_c"][j]).reshape(4, 128, 512) for j in range(2)])
    out["wso"] = np.concatenate([blk_in(inp["w_ssm_out"][j]).reshape(8, 128, 1024) for j in range(2)])
    return out


def ssm_params(inp):
    G, N, Pc = 32, 64, 16
    s5A = np.zeros((2, 128, 3, 64), np.float32)
    s5B = np.zeros((2, 2, 4, 128, 5, 64), np.float32)
    s5C = np.zeros((2, 2, 4, 128, 2, 128), np.float32)
    for j in range(2):
        for d in range(2):
            lre = inp["lam_re"][j, d]
            lim = inp["lam_im"][j, d]
            ldt = inp["log_dt"][j, d]
            cols = slice(d * 32, (d + 1) * 32)
            s5A[j, :, 0, cols] = np.tile(lre.T, (2, 1))
            s5A[j, :, 1, cols] = np.tile(lim.T, (2, 1))
            s5A[j, :, 2, cols] = np.broadcast_to(ldt[None, :], (128, 32))
            for q in range(4):
                gs = slice(q * 8, (q + 1) * 8)
                s5B[j, d, q, :, 0, :] = inp["b_re"][j, d, gs].transpose(0, 2, 1).reshape(128, N)
                s5B[j, d, q, :, 1, :] = inp["b_im"][j, d, gs].transpose(0, 2, 1).reshape(128, N)
                s5B[j, d, q, :, 2, :] = np.repeat(lre[gs], Pc, axis=0)
                s5B[j, d, q, :, 3, :] = np.repeat(lim[gs], Pc, axis=0)
                s5B[j, d, q, :, 4, :] = np.repeat(ldt[gs], Pc)[:, None]
                cre = inp["c_re"][j, d, gs].transpose(2, 0, 1).reshape(N, 128)
                cim = inp["c_im"][j, d, gs].transpose(2, 0, 1).reshape(N, 128)
                s5C[j, d, q, :64, 0, :] = cre
                s5C[j, d, q, 64:, 0, :] = cim
                s5C[j, d, q, :64, 1, :] = cim
                s5C[j, d, q, 64:, 1, :] = cre
    kc32 = np.zeros((128, 2, 128), np.float32)
    kc32[:, 0, :] = np.eye(128, dtype=np.float32)
    for pp in range(64):
        kc32[pp + 64, 1, pp] = 1.0
        kc32[pp, 1, pp + 64] = -1.0
    return {"s5A": s5A, "s5B": s5B, "s5C": s5C, "kc32": kc32}


NEG = -30000.0


def attn_tables(inp):
    out = np.full((2, 5, 8, 128, 1024), NEG, np.float32)
    k = np.arange(128)[:, None]
    q = np.arange(128)[None, :]
    slopes = [2.0 ** (-(i + 1)) for i in range(8)]
    for ti, n in enumerate((0, 1, 2, 30, 31)):
        lo = min(max(n - 2, 0), 27)
        low = min(max(n - 1, 0), 29)
        qtok = n * 128 + q
        r = qtok // 64
        c = qtok % 64
        rs = np.clip(r - 4, 0, 56)
        cs_ = np.clip(c - 8, 0, 48)
        for jj in range(5):
            ktok = (lo + jj) * 128 + k
            kr = ktok // 64
            kc = ktok % 64
            ok = (kr >= rs) & (kr < rs + 8) & (kc >= cs_) & (kc < cs_ + 16)
            dr = np.clip(kr - r + 7, 0, 14)
            dc = np.clip(kc - c + 15, 0, 30)
            for j in range(2):
                for h in range(8):
                    g = inp["rpb_b"][j, h][dr, dc]
                    out[j, ti, h, :, jj * 128:(jj + 1) * 128] = np.where(ok, g, NEG)
        for jj in range(3):
            ktok = (low + jj) * 128 + k
            dist = np.abs(ktok - qtok)
            okw = dist <= 128
            for h in range(8):
                tab = np.where(okw, (-slopes[h]) * dist.astype(np.float32), NEG).astype(np.float32)
                out[:, ti, h, :, 640 + jj * 128:640 + (jj + 1) * 128] = tab
    return out


def pack_consts(inp):
    cols = []

    def gain(v):
        return v.reshape(8, 128).T

    for l in range(DEPTH):
        for nm in ("norm_ffn1", "norm_mix", "norm_ffn2", "norm_ple", "norm_ple_post"):
            cols.append(gain(inp[nm][l]))
    for j in range(2):
        for nm in ("q_gain_a", "k_gain_a", "q_gain_b", "k_gain_b"):
            cols.append(np.tile(inp[nm][j], 2)[:, None])
        cols.append(np.broadcast_to(inp["sink_a"][j][None, :], (128, 8)))
    for j in range(2):
        cw = inp["conv_w"][j]
        for q in range(4):
            cols.append(cw[:, q * 128:(q + 1) * 128].T)
        for nm in ("conv_b", "ln_g_d", "ln_b_d", "d_skip", "b_glu_c"):
            cols.append(inp[nm][j].reshape(4, 128).T)
    gm = np.zeros((128, 8), np.float32)
    for g8 in range(8):
        gm[g8 * 16:(g8 + 1) * 16, g8] = 1.0
    cols.append(gm)
    cols.append(np.full((128, 1), np.pi / 2, np.float32))
    return np.ascontiguousarray(np.concatenate(cols, axis=1).astype(np.float32))


def gcol(l, which):
    return (l * 5 + which) * 8


ACOL = DEPTH * 5 * 8
SCOL = ACOL + 24
GMC = SCOL + 288
HPIC = GMC + 8
TS = 256


class Cfg:
    nseq = SEQ_PER_CORE
    depth = DEPTH
    mixers = True
    stage = 9


def build_program(cfg, wshapes, ncst):
    nc = bass.Bass("TRN2", target_bir_lowering=False)
    NS = cfg.nseq
    xT = nc.dram_tensor("xT", [NS, 128, 8, SEQ], F32, kind="ExternalInput").ap()
    pT = nc.dram_tensor("pT", [NS, DEPTH, 128, 2, SEQ], F32, kind="ExternalInput").ap()
    yT = nc.dram_tensor("yT", [NS, 128, 8, SEQ], F32, kind="ExternalOutput").ap()
    cst_d = nc.dram_tensor("cst", [128, ncst], F32, kind="ExternalInput").ap()
    wf = {}
    wb = {}
    for nm, (nb, fsz) in wshapes.items():
        wf[nm] = nc.dram_tensor(nm, [nb, 128, fsz], F32, kind="ExternalInput").ap()
        wb[nm] = nc.dram_tensor(nm + "_bf", [nb, 128, fsz], BF16, kind="Internal").ap()

    ball = nc.dram_tensor("ball", [2, 5, 8, 128, 1024], F32, kind="ExternalInput").ap()
    qkT = nc.dram_tensor("qkT_s", [13, 128, SEQ], BF16, kind="Internal").ap()
    vS = nc.dram_tensor("vS_s", [32, 128, 640], BF16, kind="Internal").ap()
    ymix = nc.dram_tensor("ymix_s", [128, 8, SEQ], BF16, kind="Internal").ap()

    s5A_d = nc.dram_tensor("s5A", [2, 128, 3, 64], F32, kind="ExternalInput").ap()
    s5B_d = nc.dram_tensor("s5B", [2, 2, 4, 128, 5, 64], F32, kind="ExternalInput").ap()
    s5C_d = nc.dram_tensor("s5C", [2, 2, 4, 128, 2, 128], F32, kind="ExternalInput").ap()
    kc32_d = nc.dram_tensor("kc32", [128, 2, 128], F32, kind="ExternalInput").ap()
    uS = nc.dram_tensor("uS_s", [4, 128, SEQ], BF16, kind="Internal").ap()
    hhS = nc.dram_tensor("hhS_s", [4, 128, SEQ + 32], BF16, kind="Internal").ap()
    zS = nc.dram_tensor("zS_s", [4, 128, SEQ], BF16, kind="Internal").ap()
    tabS = nc.dram_tensor("tabS_s", [2, 2, 32, 128, 2 * TS], F32, kind="Internal").ap()
    s5W = nc.dram_tensor("s5W_s", [2, 2, 4, 128, 8 * 4 * 128], BF16, kind="Internal").ap()
    dgS = nc.dram_tensor("dgS_s", [2, 4, 2, 128, 2048], BF16, kind="Internal").ap()

    D = 4
    WSLOT = 3072
    S = Sched()
    import contextlib
    with contextlib.ExitStack() as es:
        def sb(name, shape, dt):
            return es.enter_context(nc.sbuf_tensor(name, shape, dt))

        x = sb("x", [128, 8, SEQ], F32)
        wsl = sb("wsl", [128, D, WSLOT], BF16)
        xn = sb("xn", [128, 8, TC], BF16)
        hb = sb("hb", [128, NJ, TC], BF16)
        sq = sb("sq", [128, 2, TC], BF16)
        sg = sb("sg", [128, 2, TC], F32)
        pb = sb("pb", [128, 2, 2, TC], BF16)
        cst = sb("cstt", [128, ncst], F32)
        ones = sb("ones", [128, 128], BF16)
        bones = sb("bones", [128, 128], BF16)
        esink = sb("esink", [128, 16], F32)
        stA = sb("stA", [128, 2, TC], BF16)
        vst = sb("vst", [128, 2, 640], BF16)
        ident = sb("ident", [128, 128], BF16)
        swm = sb("swm", [128, 128], F32)
        s5p = sb("s5p", [128, 2, 3, 64], F32)
        s5wt = sb("s5wt", [128, 4, 4, 128], BF16)
        stc = sb("stc", [128, 16], F32)
        rhoT = sb("rhoT", [128, 2, TS], F32)
        zero16 = sb("zero16", [128, 16], BF16)
        P = [es.enter_context(nc.psum_tensor("ps%d" % i, [128, TC], F32)) for i in range(8)]
        nsem = 0

        def sem(name):
            return es.enter_context(nc.semaphore(name))

        eng_sems = {e: sem("e_" + e) for e in ENGS}
        w_sems = [sem("w%d" % i) for i in range(D)]
        x_sems = [sem("x%d" % i) for i in range(NCH)]
        p_sems = [sem("p%d" % i) for i in range(2)]
        c_sem = sem("cst")
        cv_sems = [sem("cv%d" % i) for i in range(2)]
        cvs_sems = [sem("cvs%d" % i) for i in range(2)]
        sta_sems = [sem("sta%d" % i) for i in range(2)]
        vst_sems = [sem("vst%d" % i) for i in range(2)]
        qt_sems = [sem("qt%d" % i) for i in range(2)]
        kv_sems = [sem("kv%d" % i) for i in range(3)]
        bi_sems = [sem("bi%d" % i) for i in range(2)]
        ys_sems = [sem("ys%d" % i) for i in range(2)]
        ym_sem = sem("ym")
        pr_sems = [sem("pr%d" % i) for i in range(6)]
        tb_sems = [sem("tb%d" % i) for i in range(2)]
        s5_sems = [sem("s5_%d" % i) for i in range(4)]
        zs_sems = [sem("zs%d" % i) for i in range(2)]
        block = es.enter_context(nc.Block())

        hb32 = hb[:].rearrange("p j t -> p (j t)").bitcast(F32)
        hbf = hb[:].rearrange("p j t -> p (j t)")
        Vt = hbf[:, 0:3200].rearrange("p (t f) -> p t f", t=5)
        kbT = hbf[:, 3200:5760].rearrange("p (c t) -> p c t", c=4)
        kaT = hbf[:, 5760:6400]
        pTt = hbf[:, 6400:7680].rearrange("p (u t) -> p u t", u=2)
        et = hbf[:, 7680:10240].bitcast(F32).rearrange("p (u t) -> p u t", u=2)
        biast = xn[:].rearrange("p k t -> p (k t)").bitcast(F32).rearrange("p (u t) -> p u t", u=2)
        qtt = sg[:].rearrange("p u t -> p (u t)").bitcast(BF16).rearrange("p (u c t) -> p u c t", u=2, c=8)
        ystt = pb[:].rearrange("p a b t -> p (a b t)").rearrange("p (u c t) -> p u c t", u=2, c=8)
        dent = sq[:].rearrange("p u t -> p (u t)").bitcast(F32).rearrange("p (u t) -> p u t", u=2)
        sgf = sg[:].rearrange("p u t -> p (u t)")
        pbf = pb[:].rearrange("p a b t -> p (a b t)")
        tab_ap = [sgf[:, 0:512], sgf[:, 512:1024], pbf[:, 0:1024].bitcast(F32), pbf[:, 1024:2048].bitcast(F32)]
        tab_ap = [t.rearrange("p (c t) -> p c t", c=2) for t in tab_ap]
        yq = hbf[:, 0:8192].bitcast(F32)
        rot2 = hbf[:, 8192:9216].bitcast(F32).rearrange("p (u t) -> p u t", u=2)
        rot512 = hbf[:, 8192:9216].bitcast(F32)
        gt2 = hbf[:, 9216:10240].bitcast(F32).rearrange("p (u t) -> p u t", u=2)
        gt512 = hbf[:, 9216:10240].bitcast(F32)
        G4 = hbf[:, 10240:11264].rearrange("p (u t) -> p u t", u=4)
        g512 = hbf[:, 10240:11264].rearrange("p (u t) -> p u t", u=2)
        uTq = xn[:].rearrange("p k t -> p (k t)")
        hhwin = hbf[:, 0:2176].rearrange("p (q t) -> p q t", q=4)
        hcv = hbf[:, 2176:6272].bitcast(F32).rearrange("p (q t) -> p q t", q=4)
        tmpc = hbf[:, 6272:8320].bitcast(F32).rearrange("p (u t) -> p u t", u=2)
        msq = hbf[:, 8320:9344].bitcast(F32)
        ystc = pbf.rearrange("p (q t) -> p q t", q=4)
        wkA = x[:, 2, :]
        wkT = x[:, 3, :]
        wkB = x[:, 4, :]
        wkC = x[:, 5, :]
        wkW = x[:, 6, :].bitcast(BF16)
        wkD = x[:, 7, :].bitcast(BF16)

        def record():
            xr = [[Res("x%d_%d" % (c, k)) for k in range(8)] for c in range(NCH)]
            xn_r = [Res() for _ in range(8)]
            h_r = [Res() for _ in range(NJ)]
            sq_r = [Res(), Res()]
            sg_r = [Res(), Res()]
            pb_r = [Res(), Res()]
            ps_r = [Res(psum=True) for _ in range(8)]
            ws_r = [Res() for _ in range(D)]
            cst_r = Res()
            wb_r = Res()
            sta_r = [Res(), Res()]
            vst_r = [Res(), Res()]
            qkT_r = Res()
            vS_r = Res()
            ym_r = Res()
            qt_r = [Res(), Res()]
            ka_r, kb_r, v_r = Res(), Res(), Res()
            bias_r = [Res(), Res()]
            e_r = [Res(), Res()]
            pT_r = [Res(), Res()]
            yst_r = [Res(), Res()]
            den_r = [Res(), Res()]

            class WS:
                reqs = []
                n = 0
                issued = 0

            if S.dry:
                WS.reqs = []
            else:
                WS.reqs = record.reqs

            def wget(dram_ap, fsz):
                if S.dry:
                    WS.reqs.append((dram_ap, fsz))
                    return wsl[:, 0, :fsz], ws_r[0]
                k = WS.n
                WS.n += 1
                while WS.issued < min(len(WS.reqs), k + D - 1):
                    m = WS.issued
                    ap_m, f_m = WS.reqs[m]
                    S.add("sync", "dma_start", out=wsl[:, m % D, :f_m], in_=ap_m,
                          rd=[wb_r], wr=[ws_r[m % D]], dma_sem=w_sems[m % D])
                    WS.issued += 1
                return wsl[:, k % D, :fsz], ws_r[k % D]

            S.add("gpsimd", "dma_start", out=cst[:], in_=cst_d, wr=[cst_r], dma_sem=c_sem)
            S.add("vector", "memset", ones[:], 1.0, wr=[cst_r])
            S.add("vector", "memset", bones[:], 0.0, wr=[cst_r])
            S.add("vector", "memset", bones[0:64, 0:64], 1.0, wr=[cst_r])
            S.add("vector", "memset", bones[64:128, 64:128], 1.0, wr=[cst_r])
            for j in range(2):
                S.add("scalar", "activation", out=esink[:, j * 8:(j + 1) * 8],
                      in_=cst[:, ACOL + j * 12 + 4:ACOL + j * 12 + 12], func=AF.Exp, rd=[cst_r], wr=[cst_r])
            xb = x[:].rearrange("p k t -> p (k t)").bitcast(BF16)
            stg_r = [Res(), Res()]
            last_st = [None, None]
            nconv = 0
            for nm, (nb, fsz) in wshapes.items():
                if fsz <= 2048:
                    G = 1
                    for g in (8, 4, 2, 1):
                        if g * fsz <= 8192 and nb % g == 0:
                            G = g
                            break
                    sub = None
                else:
                    G = 1
                    sub = 2
                    while fsz // sub > 2048 or fsz % sub:
                        sub += 1
                for b0 in range(0, nb, G):
                    k = nconv % 2
                    nconv += 1
                    st = xb[:, k * 8192: k * 8192 + G * fsz]
                    if sub is None:
                        o = st.rearrange("p (g f) -> p g f", g=G)
                        i_ = wf[nm][b0:b0 + G].rearrange("g p f -> p g f")
                        so = wb[nm][b0:b0 + G].rearrange("g p f -> p g f")
                    else:
                        o = st.rearrange("p (a f) -> p a f", a=sub)
                        i_ = wf[nm][b0].rearrange("p (a f) -> p a f", a=sub)
                        so = wb[nm][b0].rearrange("p (a f) -> p a f", a=sub)
                    S.add("gpsimd", "dma_start", out=o, in_=i_, wr=[stg_r[k]], dma_sem=cv_sems[k])
                    last_st[k] = S.add("sync", "dma_start", out=so, in_=o, rd=[stg_r[k]], dma_sem=cvs_sems[k])
            lst = [] if S.dry else [i for i in last_st if i is not None]
            S.add("sync", "nop", wr=[wb_r], extra=lst)


            def V(meth, *args, rd=(), wr=(), **kw):
                return S.add("vector", meth, *args, rd=rd, wr=wr, **kw)

            def A(meth, *args, rd=(), wr=(), **kw):
                return S.add("scalar", meth, *args, rd=rd, wr=wr, **kw)

            def T(*args, rd=(), wr=(), **kw):
                return S.add("tensor", "matmul", *args, rd=rd, wr=wr, **kw)

            def G(out, in_, rd=(), wr=(), sem=None, **kw):
                return S.add("gpsimd", "dma_start", out=out, in_=in_, rd=rd, wr=wr, dma_sem=sem, **kw)

            def dbl(c_in, s_in, c_out, s_out, t1, t2, r):
                V("tensor_tensor", out=t1, in0=c_in, in1=s_in, op=ALU.mult, rd=[r], wr=[r])
                V("tensor_tensor", out=t2, in0=s_in, in1=s_in, op=ALU.mult, rd=[r], wr=[r])
                V("tensor_tensor", out=c_out, in0=c_in, in1=c_in, op=ALU.mult, rd=[r], wr=[r])
                V("tensor_tensor", out=c_out, in0=c_out, in1=t2, op=ALU.subtract, rd=[r], wr=[r])
                V("tensor_scalar", out=s_out, in0=t1, scalar1=2.0, scalar2=None, op0=ALU.mult, rd=[r], wr=[r])

            def cis(th, c, s_, t1, t2, r):
                A("activation", out=s_, in_=th, func=AF.Sin, scale=0.125, rd=[r, cst_r], wr=[r])
                A("activation", out=c, in_=th, func=AF.Sin, scale=-0.125, bias=cst[:, HPIC:HPIC + 1],
                  rd=[r, cst_r], wr=[r])
                for _ in range(3):
                    dbl(c, s_, c, s_, t1, t2, r)

            def ssm_prologue():
                rA, rB, rC = Res(), Res(), Res()
                rT = [Res(), Res()]
                rW = [Res(), Res()]
                rD = [Res(), Res()]
                S.add("gpsimd", "dma_start", out=ident[:], in_=kc32_d[:, 0, :], wr=[cst_r], dma_sem=pr_sems[0])
                S.add("gpsimd", "dma_start", out=swm[:], in_=kc32_d[:, 1, :], wr=[cst_r], dma_sem=pr_sems[0], nowaw=True, grp=True)
                V("memset", zero16[:], 0.0, wr=[rC])
                for q in range(4):
                    S.add("gpsimd", "dma_start", out=hhS[q, :, 0:16], in_=zero16[:], rd=[rC], dma_sem=pr_sems[1])
                    S.add("gpsimd", "dma_start", out=hhS[q, :, SEQ + 16:SEQ + 32], in_=zero16[:], rd=[rC],
                          dma_sem=pr_sems[1])

                def a64(i):
                    return wkA[:, i * 64:(i + 1) * 64]

                def b64(i):
                    return wkB[:, i * 64:(i + 1) * 64]

                for j in range(2):
                    A3 = wkA[:, 0:192].rearrange("p (a b) -> p a b", a=3)
                    S.add("gpsimd", "dma_start", out=A3, in_=s5A_d[j], wr=[rA], dma_sem=pr_sems[2])
                    dt, th, t1, t2 = a64(3), a64(4), a64(5), a64(6)
                    CK = [a64(8 + k) for k in range(9)]
                    SK = [a64(17 + k) for k in range(9)]
                    A("activation", out=dt, in_=A3[:, 2, :], func=AF.Exp, rd=[rA], wr=[rA])
                    V("tensor_tensor", out=t1, in0=A3[:, 0, :], in1=dt, op=ALU.mult, rd=[rA], wr=[rA])
                    A("activation", out=s5p[:, j, 0, :], in_=t1, func=AF.Exp, rd=[rA], wr=[rA])
                    V("tensor_tensor", out=th, in0=A3[:, 1, :], in1=dt, op=ALU.mult, rd=[rA], wr=[rA])
                    cis(th, CK[0], SK[0], t1, t2, rA)
                    for k in range(8):
                        dbl(CK[k], SK[k], CK[k + 1], SK[k + 1], t1, t2, rA)
                    V("tensor_copy", out=s5p[:, j, 1, :], in_=CK[8], rd=[rA], wr=[rA])
                    V("tensor_copy", out=s5p[:, j, 2, :], in_=SK[8], rd=[rA], wr=[rA])
                    for dg in range(64):
                        k2 = dg % 2
                        base = k2 * 1024
                        tc_ = wkT[:, base:base + TS]
                        ts_ = wkT[:, base + TS:base + 2 * TS]
                        u1 = wkT[:, base + 2 * TS:base + 2 * TS + 128]
                        r = rT[k2]
                        V("tensor_copy", out=tc_[:, 0:1], in_=CK[0][:, dg:dg + 1], rd=[rA], wr=[r])
                        V("tensor_copy", out=ts_[:, 0:1], in_=SK[0][:, dg:dg + 1], rd=[rA], wr=[r])
                        for k in range(8):
                            n = 1 << k
                            ck = CK[k][:, dg:dg + 1]
                            sk = SK[k][:, dg:dg + 1]
                            V("tensor_scalar", out=u1[:, 0:n], in0=ts_[:, 0:n], scalar1=sk, scalar2=None, op0=ALU.mult,
                              rd=[r, rA], wr=[r])
                            V("scalar_tensor_tensor", out=tc_[:, n:2 * n], in0=tc_[:, 0:n], scalar=ck, in1=u1[:, 0:n],
                              op0=ALU.mult, op1=ALU.subtract, rd=[r, rA], wr=[r])
                            V("tensor_scalar", out=u1[:, 0:n], in0=ts_[:, 0:n], scalar1=ck, scalar2=None, op0=ALU.mult,
                              rd=[r, rA], wr=[r])
                            V("scalar_tensor_tensor", out=ts_[:, n:2 * n], in0=tc_[:, 0:n], scalar=sk, in1=u1[:, 0:n],
                              op0=ALU.mult, op1=ALU.add, rd=[r, rA], wr=[r])
                        S.add("gpsimd", "dma_start", out=tabS[j, dg // 32, dg % 32], in_=wkT[:, base:base + 2 * TS],
                              rd=[r], dma_sem=tb_sems[k2])
                    nw = 0
                    for d in range(2):
                        for q in range(4):
                            B5 = wkB[:, 0:320].rearrange("p (a b) -> p a b", a=5)
                            S.add("gpsimd", "dma_start", out=B5, in_=s5B_d[j, d, q], wr=[rB], dma_sem=pr_sems[3])
                            bre, bim, lre, lim, ldt = (B5[:, i, :] for i in range(5))
                            dtb, thb, c1, s1, u1, u2, rho, lbr, lbi, den, cr, ci = (b64(5 + i) for i in range(12))
                            BB = wkB[:, 1280:1408]
                            BBs = wkB[:, 1408:1536]
                            A("activation", out=dtb, in_=ldt, func=AF.Exp, rd=[rB], wr=[rB])
                            V("tensor_tensor", out=u1, in0=lre, in1=dtb, op=ALU.mult, rd=[rB], wr=[rB])
                            A("activation", out=rho, in_=u1, func=AF.Exp, rd=[rB], wr=[rB])
                            V("tensor_tensor", out=thb, in0=lim, in1=dtb, op=ALU.mult, rd=[rB], wr=[rB])
                            cis(thb, c1, s1, u1, u2, rB)
                            V("tensor_tensor", out=lbr, in0=rho, in1=c1, op=ALU.mult, rd=[rB], wr=[rB])
                            V("tensor_tensor", out=lbi, in0=rho, in1=s1, op=ALU.mult, rd=[rB], wr=[rB])
                            V("tensor_scalar", out=lbr, in0=lbr, scalar1=-1.0, scalar2=None, op0=ALU.add, rd=[rB], wr=[rB])
                            V("tensor_tensor", out=den, in0=lre, in1=lre, op=ALU.mult, rd=[rB], wr=[rB])
                            V("tensor_tensor", out=u1, in0=lim, in1=lim, op=ALU.mult, rd=[rB], wr=[rB])
                            V("tensor_tensor", out=den, in0=den, in1=u1, op=ALU.add, rd=[rB], wr=[rB])
                            V("reciprocal", out=den, in_=den, rd=[rB], wr=[rB])
                            V("tensor_tensor", out=cr, in0=lbr, in1=lre, op=ALU.mult, rd=[rB], wr=[rB])
                            V("tensor_tensor", out=u1, in0=lbi, in1=lim, op=ALU.mult, rd=[rB], wr=[rB])
                            V("tensor_tensor", out=cr, in0=cr, in1=u1, op=ALU.add, rd=[rB], wr=[rB])
                            V("tensor_tensor", out=cr, in0=cr, in1=den, op=ALU.mult, rd=[rB], wr=[rB])
                            V("tensor_tensor", out=ci, in0=lbi, in1=lre, op=ALU.mult, rd=[rB], wr=[rB])
                            V("tensor_tensor", out=u1, in0=lbr, in1=lim, op=ALU.mult, rd=[rB], wr=[rB])
                            V("tensor_tensor", out=ci, in0=ci, in1=u1, op=ALU.subtract, rd=[rB], wr=[rB])
                            V("tensor_tensor", out=ci, in0=ci, in1=den, op=ALU.mult, rd=[rB], wr=[rB])
                            V("tensor_tensor", out=BB[:, 0:64], in0=cr, in1=bre, op=ALU.mult, rd=[rB], wr=[rB])
                            V("tensor_tensor", out=u1, in0=ci, in1=bim, op=ALU.mult, rd=[rB], wr=[rB])
                            V("tensor_tensor", out=BB[:, 0:64], in0=BB[:, 0:64], in1=u1, op=ALU.subtract, rd=[rB], wr=[rB])
                            V("tensor_tensor", out=BB[:, 64:128], in0=cr, in1=bim, op=ALU.mult, rd=[rB], wr=[rB])
                            V("tensor_tensor", out=u1, in0=ci, in1=bre, op=ALU.mult, rd=[rB], wr=[rB])
                            V("tensor_tensor", out=BB[:, 64:128], in0=BB[:, 64:128], in1=u1, op=ALU.add, rd=[rB], wr=[rB])
                            V("tensor_copy", out=BBs[:, 0:64], in_=BB[:, 64:128], rd=[rB], wr=[rB])
                            V("tensor_scalar", out=BBs[:, 64:128], in0=BB[:, 0:64], scalar1=-1.0, scalar2=None,
                              op0=ALU.mult, rd=[rB], wr=[rB])
                            C2 = wkC[:, 0:256].rearrange("p (a b) -> p a b", a=2)
                            S.add("gpsimd", "dma_start", out=C2, in_=s5C_d[j, d, q], wr=[rC], dma_sem=pr_sems[4])
                            V("tensor_scalar", out=C2[64:128, 0, :], in0=C2[64:128, 0, :], scalar1=-1.0, scalar2=None,
                              op0=ALU.mult, rd=[rC], wr=[rC])
                            V("tensor_scalar", out=C2[:, 1, :], in0=C2[:, 1, :], scalar1=-1.0, scalar2=None,
                              op0=ALU.mult, rd=[rC], wr=[rC])
                            k2 = nw % 2
                            nw += 1
                            W8 = wkW[:, k2 * 4096:(k2 + 1) * 4096].rearrange("p (g a n) -> p g a n", g=8, a=4)
                            V("memset", wkW[:, k2 * 4096:(k2 + 1) * 4096], 0.0, wr=[rW[k2]])
                            for g8 in range(8):
                                gm = cst[:, GMC + g8:GMC + g8 + 1]
                                V("tensor_scalar", out=W8[:, g8, 0, :], in0=BB, scalar1=gm, scalar2=None, op0=ALU.mult,
                                  rd=[rB, cst_r, rW[k2]], wr=[rW[k2]])
                                V("tensor_scalar", out=W8[:, g8, 1, :], in0=BBs, scalar1=gm, scalar2=None, op0=ALU.mult,
                                  rd=[rB, cst_r, rW[k2]], wr=[rW[k2]])
                                cs16 = slice(16 * g8, 16 * g8 + 16)
                                V("tensor_copy", out=W8[:, g8, 2, cs16], in_=C2[:, 0, cs16], rd=[rC, rW[k2]], wr=[rW[k2]])
                                V("tensor_copy", out=W8[:, g8, 3, cs16], in_=C2[:, 1, cs16], rd=[rC, rW[k2]], wr=[rW[k2]])
                            S.add("gpsimd", "dma_start", out=s5W[j, d, q], in_=wkW[:, k2 * 4096:(k2 + 1) * 4096],
                                  rd=[rW[k2]], dma_sem=s5_sems[k2])
                    nd = 0
                    for q in range(4):
                        for half in range(2):
                            k2 = nd % 2
                            nd += 1
                            st_ = wkD[:, k2 * 2048:(k2 + 1) * 2048].rearrange("p (k n) -> p k n", k=16)
                            if half == 1:
                                V("memset", st_[:, 15, :], 0.0, wr=[rD[k2]])
                            for kk in range(16 if half == 0 else 15):
                                col = SCOL + j * 144 + q * 31 + half * 16 + kk
                                V("tensor_scalar", out=st_[:, kk, :], in0=ident[:], scalar1=cst[:, col:col + 1],
                                  scalar2=None, op0=ALU.mult, rd=[cst_r, rD[k2]], wr=[rD[k2]])
                            S.add("gpsimd", "dma_start", out=dgS[j, q, half], in_=wkD[:, k2 * 2048:(k2 + 1) * 2048],
                                  rd=[rD[k2]], dma_sem=s5_sems[2 + k2])

            if cfg.mixers and cfg.depth > 1:
                ssm_prologue()
            S.barrier()

            def rmsnorm(c, gc):
                cs = slice(c * TC, (c + 1) * TC)
                for kc in range(8):
                    S.add("scalar", "activation", out=sq[:, kc % 2, :], in_=x[:, kc, cs], func=AF.Square,
                          rd=[xr[c][kc]], wr=[sq_r[kc % 2]])
                    S.add("tensor", "matmul", P[6][:], lhsT=ones[:], rhs=sq[:, kc % 2, :],
                          start=(kc == 0), stop=(kc == 7), rd=[sq_r[kc % 2], cst_r], wr=[ps_r[6]])
                S.add("scalar", "activation", out=P[7][:], in_=P[6][:], func=AF.Sqrt, scale=1.0 / D_MODEL,
                      bias=cst[:, EPSC:EPSC + 1], rd=[ps_r[6], cst_r], wr=[ps_r[7]])
                S.add("vector", "reciprocal", out=P[7][:], in_=P[7][:], rd=[ps_r[7]], wr=[ps_r[7]])
                for kc in range(8):
                    S.add("vector", "scalar_tensor_tensor", out=xn[:, kc, :], in0=x[:, kc, cs],
                          scalar=cst[:, gc + kc:gc + kc + 1], in1=P[7][:], op0=ALU.mult, op1=ALU.mult,
                          rd=[xr[c][kc], ps_r[7], cst_r], wr=[xn_r[kc]])

            def ffn(c, l, which):
                cs = slice(c * TC, (c + 1) * TC)
                nm = "w1" if which == 0 else "w2"
                rmsnorm(c, gcol(l, 0 if which == 0 else 2))
                for j in range(NJ):
                    w, wr_ = wget(wb[nm + "i"][l * NJ + j], 2048)
                    wv = w.rearrange("p (g k n) -> p g k n", g=2, k=8)
                    pg = 2 * (j % 2)
                    pu = pg + 1
                    for kc in range(8):
                        S.add("tensor", "matmul", P[pg][:], lhsT=wv[:, 0, kc, :], rhs=xn[:, kc, :],
                              start=(kc == 0), stop=(kc == 7), rd=[wr_, xn_r[kc]], wr=[ps_r[pg]])
                    for kc in range(8):
                        S.add("tensor", "matmul", P[pu][:], lhsT=wv[:, 1, kc, :], rhs=xn[:, kc, :],
                              start=(kc == 0), stop=(kc == 7), rd=[wr_, xn_r[kc]], wr=[ps_r[pu]])
                    S.add("scalar", "activation", out=sg[:, j % 2, :], in_=P[pg][:], func=AF.Silu,
                          rd=[ps_r[pg]], wr=[sg_r[j % 2]])
                    S.add("vector", "tensor_tensor", out=hb[:, j, :], in0=sg[:, j % 2, :], in1=P[pu][:],
                          op=ALU.mult, rd=[sg_r[j % 2], ps_r[pu]], wr=[h_r[j]])
                for oc in range(8):
                    w, wr_ = wget(wb[nm + "o"][l * 8 + oc], NJ * 128)
                    wv = w.rearrange("p (h n) -> p h n", h=NJ)
                    po = 4 + oc % 2
                    for hc in range(NJ):
                        S.add("tensor", "matmul", P[po][:], lhsT=wv[:, hc, :], rhs=hb[:, hc, :],
                              start=(hc == 0), stop=(hc == NJ - 1), rd=[wr_, h_r[hc]], wr=[ps_r[po]])
                    S.add("vector", "scalar_tensor_tensor", out=x[:, oc, cs], in0=P[po][:], scalar=0.5,
                          in1=x[:, oc, cs], op0=ALU.mult, op1=ALU.add,
                          rd=[ps_r[po], xr[c][oc]], wr=[xr[c][oc]])

            def ple(s, c, l):
                cs = slice(c * TC, (c + 1) * TC)
                k = ple.n % 2
                ple.n += 1
                S.add("gpsimd", "dma_start", out=pb[:, k, :, :], in_=pT[s, l, :, :, cs],
                      wr=[pb_r[k]], dma_sem=p_sems[k])
                rmsnorm(c, gcol(l, 3))
                w, wr_ = wget(wb["wpp"][l], 2048)
                wv = w.rearrange("p (k n) -> p k n", k=2)
                for oc in range(8):
                    pp = oc % 2
                    for kc in range(2):
                        S.add("tensor", "matmul", P[pp][:], lhsT=wv[:, kc, oc * 128:(oc + 1) * 128],
                              rhs=pb[:, k, kc, :], start=(kc == 0), stop=(kc == 1),
                              rd=[wr_, pb_r[k]], wr=[ps_r[pp]])
                    S.add("vector", "tensor_copy", out=hb32[:, oc * TC:(oc + 1) * TC], in_=P[pp][:],
                          rd=[ps_r[pp]], wr=[h_r[2 * oc], h_r[2 * oc + 1]])
                    S.add("scalar", "activation", out=sq[:, oc % 2, :], in_=P[pp][:], func=AF.Square,
                          rd=[ps_r[pp]], wr=[sq_r[oc % 2]])
                    S.add("tensor", "matmul", P[6][:], lhsT=ones[:], rhs=sq[:, oc % 2, :],
                          start=(oc == 0), stop=(oc == 7), rd=[sq_r[oc % 2], cst_r], wr=[ps_r[6]])
                S.add("scalar", "activation", out=P[7][:], in_=P[6][:], func=AF.Sqrt, scale=1.0 / D_MODEL,
                      bias=cst[:, EPSC:EPSC + 1], rd=[ps_r[6], cst_r], wr=[ps_r[7]])
                S.add("vector", "reciprocal", out=P[7][:], in_=P[7][:], rd=[ps_r[7]], wr=[ps_r[7]])
                gp = gcol(l, 4)
                for oc in range(8):
                    w, wr_ = wget(wb["wpg"][l * 8 + oc], 1024)
                    wv = w.rearrange("p (k n) -> p k n", k=8)
                    pg = 2 + oc % 2
                    for kc in range(8):
                        S.add("tensor", "matmul", P[pg][:], lhsT=wv[:, kc, :], rhs=xn[:, kc, :],
                              start=(kc == 0), stop=(kc == 7), rd=[wr_, xn_r[kc]], wr=[ps_r[pg]])
                    S.add("scalar", "activation", out=P[pg][:], in_=P[pg][:], func=AF.Sigmoid,
                          rd=[ps_r[pg]], wr=[ps_r[pg]])
                    t = sg[:, oc % 2, :]
                    S.add("vector", "scalar_tensor_tensor", out=t, in0=hb32[:, oc * TC:(oc + 1) * TC],
                          scalar=cst[:, gp + oc:gp + oc + 1], in1=P[7][:], op0=ALU.mult, op1=ALU.mult,
                          rd=[h_r[2 * oc], h_r[2 * oc + 1], ps_r[7], cst_r], wr=[sg_r[oc % 2]])
                    S.add("vector", "tensor_tensor", out=t, in0=t, in1=P[pg][:], op=ALU.mult,
                          rd=[sg_r[oc % 2], ps_r[pg]], wr=[sg_r[oc % 2]])
                    S.add("vector", "tensor_tensor", out=x[:, oc, cs], in0=x[:, oc, cs], in1=t, op=ALU.add,
                          rd=[sg_r[oc % 2], xr[c][oc]], wr=[xr[c][oc]])

            ple.n = 0

            def attn_in(c, l):
                j = l // 2
                cs = slice(c * TC, (c + 1) * TC)
                rmsnorm(c, gcol(l, 1))
                for pc in range(13):
                    w, wr_ = wget(wb["wai"][j * 13 + pc], 1024)
                    wv = w.rearrange("p (k n) -> p k n", k=8)
                    pp = pc % 2
                    k = attn_in.n % 2
                    attn_in.n += 1
                    for kc in range(8):
                        S.add("tensor", "matmul", P[pp][:], lhsT=wv[:, kc, :], rhs=xn[:, kc, :],
                              start=(kc == 0), stop=(kc == 7), rd=[wr_, xn_r[kc]], wr=[ps_r[pp]])
                    S.add("scalar", "activation", out=sq[:, k, :], in_=P[pp][:], func=AF.Square,
                          rd=[ps_r[pp]], wr=[sq_r[k]])
                    S.add("tensor", "matmul", P[2 + pp][:], lhsT=bones[:], rhs=sq[:, k, :], start=True, stop=True,
                          rd=[sq_r[k], cst_r], wr=[ps_r[2 + pp]])
                    S.add("scalar", "activation", out=P[2 + pp][:], in_=P[2 + pp][:], func=AF.Sqrt, scale=1.0 / 64,
                          bias=cst[:, EPSC:EPSC + 1], rd=[ps_r[2 + pp], cst_r], wr=[ps_r[2 + pp]])
                    S.add("vector", "reciprocal", out=sg[:, k, :], in_=P[2 + pp][:], rd=[ps_r[2 + pp]], wr=[sg_r[k]])
                    gi = 0 if pc < 4 else 1 if pc == 4 else 2 if pc < 9 else 3
                    gc = ACOL + j * 12 + gi
                    S.add("vector", "scalar_tensor_tensor", out=stA[:, k, :], in0=P[pp][:], scalar=cst[:, gc:gc + 1],
                          in1=sg[:, k, :], op0=ALU.mult, op1=ALU.mult,
                          rd=[ps_r[pp], sg_r[k], cst_r], wr=[sta_r[k]])
                    S.add("gpsimd", "dma_start", out=qkT[pc, :, cs], in_=stA[:, k, :], rd=[sta_r[k]], wr=[qkT_r],
                          dma_sem=sta_sems[k], nowaw=True)
                w0, wr0 = wget(wb["wav"][j * 2 + 0], 2560)
                w1, wr1 = wget(wb["wav"][j * 2 + 1], 2560)
                wvh = [w0.rearrange("p (k f) -> p k f", k=4), w1.rearrange("p (k f) -> p k f", k=4)]
                wrh = [wr0, wr1]
                for tt in range(4):
                    k = attn_in.nv % 2
                    attn_in.nv += 1
                    for kc in range(8):
                        S.add("tensor", "matmul", P[4][:], lhsT=xn[:, kc, tt * 128:(tt + 1) * 128],
                              rhs=wvh[kc // 4][:, kc % 4, 0:512], start=(kc == 0), stop=(kc == 7),
                              rd=[wrh[kc // 4], xn_r[kc]], wr=[ps_r[4]])
                    for kc in range(8):
                        S.add("tensor", "matmul", P[5][:, 0:128], lhsT=xn[:, kc, tt * 128:(tt + 1) * 128],
                              rhs=wvh[kc // 4][:, kc % 4, 512:640], start=(kc == 0), stop=(kc == 7),
                              rd=[wrh[kc // 4], xn_r[kc]], wr=[ps_r[5]])
                    S.add("scalar", "activation", out=vst[:, k, 0:512], in_=P[4][:], func=AF.Copy,
                          rd=[ps_r[4]], wr=[vst_r[k]])
                    S.add("scalar", "activation", out=vst[:, k, 512:640], in_=P[5][:, 0:128], func=AF.Copy,
                          rd=[ps_r[5], vst_r[k]], wr=[vst_r[k]])
                    S.add("gpsimd", "dma_start", out=vS[c * 4 + tt], in_=vst[:, k, :], rd=[vst_r[k]], wr=[vS_r],
                          dma_sem=vst_sems[k], nowaw=True)

            attn_in.n = 0
            attn_in.nv = 0

            def attn_layer(l):
                j = l // 2
                passes = []
                for n in range(32):
                    for i in range(8):
                        for kind in (0, 1):
                            passes.append((n, i, kind, len(passes) % 2))

                def ctx(pz):
                    n, i, kind, u = pz
                    lo = min(max(n - 2, 0), 27)
                    low = min(max(n - 1, 0), 29)
                    if kind == 0:
                        cc, hf, nk = i % 4, i // 4, 3
                    else:
                        cc, hf, nk = i // 2, i % 2, 5
                    return dict(n=n, i=i, kind=kind, u=u, lo=lo, dw=low - lo, qs=n % 2, bs=(n * 8 + i) % 2,
                                typ={0: 0, 1: 1, 30: 3, 31: 4}.get(n, 2), ts=slice(n * 128, (n + 1) * 128),
                                cc=cc, hf=hf, nk=nk, rows=slice(64 * hf, 64 * hf + 64))

                def front(pz):
                    c_ = ctx(pz)
                    n, i, kind, u, qs, bs, rows, cc, nk = (c_[k] for k in ("n", "i", "kind", "u", "qs", "bs", "rows", "cc", "nk"))
                    if i == 0 and kind == 0:
                        ts = c_["ts"]
                        ks = slice(c_["lo"] * 128, c_["lo"] * 128 + 640)
                        G(qtt[:, qs, 0:4, :], qkT[0:4, :, ts].rearrange("j p t -> p j t"), rd=[qkT_r], wr=[qt_r[qs]],
                          sem=qt_sems[qs])
                        G(qtt[:, qs, 4:8, :], qkT[5:9, :, ts].rearrange("j p t -> p j t"), rd=[qkT_r], wr=[qt_r[qs]],
                          sem=qt_sems[qs], nowaw=True, grp=True)
                        G(kaT, qkT[4, :, ks], rd=[qkT_r], wr=[ka_r], sem=kv_sems[0])
                        G(kbT, qkT[9:13, :, ks].rearrange("j p t -> p j t"), rd=[qkT_r], wr=[kb_r], sem=kv_sems[1])
                    if kind == 0:
                        G(biast[:, bs, :], ball[j, c_["typ"], i], wr=[bias_r[bs]], sem=bi_sems[bs])
                    qap = qtt[rows, qs, (cc if kind == 0 else 4 + cc), :]
                    for jj in range(nk):
                        if kind == 0:
                            kap = kaT[rows, (c_["dw"] + jj) * 128:(c_["dw"] + jj + 1) * 128]
                            kres = ka_r
                        else:
                            kap = kbT[rows, cc, jj * 128:(jj + 1) * 128]
                            kres = kb_r
                        if jj < 4:
                            o, ores = P[u][:, jj * 128:(jj + 1) * 128], ps_r[u]
                        else:
                            o, ores = P[2 + u][:, 0:128], ps_r[2 + u]
                        T(o, lhsT=kap, rhs=qap, start=True, stop=True, rd=[kres, qt_r[qs]], wr=[ores])

                def mid(pz):
                    c_ = ctx(pz)
                    kind, u, bs, nk = c_["kind"], c_["u"], c_["bs"], c_["nk"]
                    if kind == 0:
                        V("scalar_tensor_tensor", out=et[:, u, 0:384], in0=P[u][:, 0:384], scalar=0.125,
                          in1=biast[:, bs, 640:1024], op0=ALU.mult, op1=ALU.add, rd=[ps_r[u], bias_r[bs]], wr=[e_r[u]])
                    else:
                        V("scalar_tensor_tensor", out=et[:, u, 0:512], in0=P[u][:, 0:512], scalar=0.125,
                          in1=biast[:, bs, 0:512], op0=ALU.mult, op1=ALU.add, rd=[ps_r[u], bias_r[bs]], wr=[e_r[u]])
                        V("scalar_tensor_tensor", out=et[:, u, 512:640], in0=P[2 + u][:, 0:128], scalar=0.125,
                          in1=biast[:, bs, 512:640], op0=ALU.mult, op1=ALU.add,
                          rd=[ps_r[2 + u], bias_r[bs], e_r[u]], wr=[e_r[u]])
                    A("activation", out=pTt[:, u, 0:nk * 128], in_=et[:, u, 0:nk * 128], func=AF.Exp,
                      rd=[e_r[u]], wr=[pT_r[u]])

                def back(pz):
                    c_ = ctx(pz)
                    n, i, kind, u, rows, nk, hf = (c_[k] for k in ("n", "i", "kind", "u", "rows", "nk", "hf"))
                    if i == 0 and kind == 0:
                        G(Vt, vS[c_["lo"]:c_["lo"] + 5].rearrange("t p f -> p t f"), rd=[vS_r], wr=[v_r], sem=kv_sems[2])
                    for jj in range(nk):
                        if kind == 0:
                            vap = Vt[:, c_["dw"] + jj, 512 + 64 * hf:512 + 64 * hf + 64]
                        else:
                            vap = Vt[:, jj, i * 64:(i + 1) * 64]
                        T(P[4 + u][rows, 0:128], lhsT=vap, rhs=pTt[:, u, jj * 128:(jj + 1) * 128], start=(jj == 0),
                          stop=(jj == nk - 1), rd=[v_r, pT_r[u]], wr=[ps_r[4 + u]])
                    for jj in range(nk):
                        T(P[6 + u][:, 0:128], lhsT=ones[:], rhs=pTt[:, u, jj * 128:(jj + 1) * 128], start=(jj == 0),
                          stop=(jj == nk - 1), rd=[cst_r, pT_r[u]], wr=[ps_r[6 + u]])

                def post(pz):
                    c_ = ctx(pz)
                    n, i, kind, u, rows, qs, cc = (c_[k] for k in ("n", "i", "kind", "u", "rows", "qs", "cc"))
                    if kind == 0:
                        A("activation", out=dent[rows, u, 0:128], in_=P[6 + u][rows, 0:128], func=AF.Ln,
                          bias=esink[rows, j * 8 + i:j * 8 + i + 1], rd=[ps_r[6 + u], cst_r], wr=[den_r[u]])
                    else:
                        A("activation", out=dent[rows, u, 0:128], in_=P[6 + u][rows, 0:128], func=AF.Ln,
                          rd=[ps_r[6 + u]], wr=[den_r[u]])
                    A("activation", out=dent[rows, u, 0:128], in_=dent[rows, u, 0:128], func=AF.Exp, scale=-1.0,
                      rd=[den_r[u]], wr=[den_r[u]])
                    ych = cc if kind == 0 else 4 + cc
                    V("tensor_tensor", out=ystt[rows, qs, ych, :], in0=P[4 + u][rows, 0:128], in1=dent[rows, u, 0:128],
                      op=ALU.mult, rd=[ps_r[4 + u], den_r[u]], wr=[yst_r[qs]], nowaw=True)
                    if i == 7 and kind == 1:
                        G(ymix[:, :, c_["ts"]], ystt[:, qs, :, :], rd=[yst_r[qs]], wr=[ym_r], sem=ys_sems[qs], nowaw=True)

                front(passes[0])
                for ii, pz in enumerate(passes):
                    if ii + 1 < len(passes):
                        front(passes[ii + 1])
                    mid(pz)
                    if ii >= 1:
                        post(passes[ii - 1])
                    back(pz)
                post(passes[-1])

            def mix_out(c, l, wname):
                j = l // 2
                cs = slice(c * TC, (c + 1) * TC)
                S.add("gpsimd", "dma_start", out=xn[:], in_=ymix[:, :, cs], rd=[ym_r], wr=xn_r, dma_sem=ym_sem)
                for oc in range(8):
                    w, wr_ = wget(wb[wname][j * 8 + oc], 1024)
                    wv = w.rearrange("p (k n) -> p k n", k=8)
                    po = 4 + oc % 2
                    for kc in range(8):
                        S.add("tensor", "matmul", P[po][:], lhsT=wv[:, kc, :], rhs=xn[:, kc, :],
                              start=(kc == 0), stop=(kc == 7), rd=[wr_, xn_r[kc]], wr=[ps_r[po]])
                    S.add("vector", "tensor_tensor", out=x[:, oc, cs], in0=P[po][:], in1=x[:, oc, cs], op=ALU.add,
                          rd=[ps_r[po], xr[c][oc]], wr=[xr[c][oc]])


            uS_r, hhS_r, zS_r = Res(), Res(), Res()
            hh_r, hc_r, tmpc_r, ystc_r = Res(), Res(), [Res(), Res()], Res()
            uq_r, yq_r, s5w_r = Res(), Res(), Res()
            tab_r = [Res() for _ in range(8)]
            rot_r, gt_r, G_r = [Res(), Res()], [Res(), Res()], [Res(), Res()]
            G2_r = [Res(), Res()]
            stc_r = [Res() for _ in range(8)]
            rho_r = [Res(), Res()]
            zt_r = Res()

            def ssm_in(c, l):
                j = l // 2
                cs = slice(c * TC, (c + 1) * TC)
                rmsnorm(c, gcol(l, 1))
                for q in range(4):
                    w, wr_ = wget(wb["wsi"][j * 12 + q], 1024)
                    wv = w.rearrange("p (k n) -> p k n", k=8)
                    pp = q % 2
                    k = attn_in.n % 2
                    attn_in.n += 1
                    for kc in range(8):
                        T(P[pp][:], lhsT=wv[:, kc, :], rhs=xn[:, kc, :], start=(kc == 0), stop=(kc == 7),
                          rd=[wr_, xn_r[kc]], wr=[ps_r[pp]])
                    A("activation", out=stA[:, k, :], in_=P[pp][:], func=AF.Copy, rd=[ps_r[pp]], wr=[sta_r[k]])
                    G(uS[q, :, cs], stA[:, k, :], rd=[sta_r[k]], wr=[uS_r], sem=sta_sems[k], nowaw=True)
                for q in range(4):
                    pa = 2 + 2 * (q % 2)
                    pg = pa + 1
                    k = attn_in.n % 2
                    attn_in.n += 1
                    for pc, pbank in ((4 + q, pa), (8 + q, pg)):
                        w, wr_ = wget(wb["wsi"][j * 12 + pc], 1024)
                        wv = w.rearrange("p (k n) -> p k n", k=8)
                        for kc in range(8):
                            T(P[pbank][:], lhsT=wv[:, kc, :], rhs=xn[:, kc, :], start=(kc == 0), stop=(kc == 7),
                              rd=[wr_, xn_r[kc]], wr=[ps_r[pbank]])
                    A("activation", out=sg[:, k, :], in_=P[pg][:], func=AF.Sigmoid, rd=[ps_r[pg]], wr=[sg_r[k]])
                    V("tensor_tensor", out=stA[:, k, :], in0=sg[:, k, :], in1=P[pa][:], op=ALU.mult,
                      rd=[sg_r[k], ps_r[pa]], wr=[sta_r[k]])
                    G(hhS[q, :, 16 + c * TC:16 + (c + 1) * TC], stA[:, k, :], rd=[sta_r[k]], wr=[hhS_r],
                      sem=sta_sems[k], nowaw=True)

            def conv_chunk(c, l):
                j = l // 2
                cs = slice(c * TC, (c + 1) * TC)
                base = SCOL + j * 144
                G(hhwin[:, :, 0:542], hhS[:, :, c * TC + 1:c * TC + 543].rearrange("q p t -> p q t"),
                  rd=[hhS_r], wr=[hh_r], sem=kv_sems[0])
                for q in range(4):
                    wA, rA_ = wget(dgS[j, q, 0], 2048)
                    wB, rB_ = wget(dgS[j, q, 1], 2048)
                    wv = [wA.rearrange("p (k n) -> p k n", k=16), wB.rearrange("p (k n) -> p k n", k=16)]
                    rr = [rA_, rB_]
                    pp = q % 2
                    for k in range(31):
                        T(P[pp][:], lhsT=wv[k // 16][:, k % 16, :], rhs=hhwin[:, q, k:k + TC], start=(k == 0),
                          stop=(k == 30), rd=[rr[k // 16], hh_r], wr=[ps_r[pp]])
                    cb = cst[:, base + 124 + q:base + 125 + q]
                    A("activation", out=hcv[:, q, :], in_=P[pp][:], func=AF.Identity, bias=cb,
                      rd=[ps_r[pp], cst_r], wr=[hc_r], nowaw=True)
                    A("activation", out=sq[:, 0, :], in_=P[pp][:], func=AF.Identity, bias=cb,
                      rd=[ps_r[pp], cst_r], wr=[sq_r[0]])
                    A("activation", out=sq[:, 1, :], in_=P[pp][:], func=AF.Square, bias=cb,
                      rd=[ps_r[pp], cst_r], wr=[sq_r[1]])
                    T(P[2][:], lhsT=ones[:], rhs=sq[:, 0, :], start=(q == 0), stop=(q == 3),
                      rd=[sq_r[0], cst_r], wr=[ps_r[2]])
                    T(P[3][:], lhsT=ones[:], rhs=sq[:, 1, :], start=(q == 0), stop=(q == 3),
                      rd=[sq_r[1], cst_r], wr=[ps_r[3]])
                A("activation", out=msq, in_=P[2][:], func=AF.Square, scale=1.0 / 512, rd=[ps_r[2]], wr=[tmpc_r[0]])
                A("activation", out=P[4][:], in_=P[2][:], func=AF.Copy, scale=1.0 / 512, rd=[ps_r[2]], wr=[ps_r[4]])
                V("scalar_tensor_tensor", out=msq, in0=P[3][:], scalar=1.0 / 512, in1=msq, op0=ALU.mult,
                  op1=ALU.subtract, rd=[ps_r[3], tmpc_r[0]], wr=[tmpc_r[0]])
                A("activation", out=P[5][:], in_=msq, func=AF.Sqrt, bias=cst[:, EPSC:EPSC + 1],
                  rd=[tmpc_r[0], cst_r], wr=[ps_r[5]])
                V("reciprocal", out=P[5][:], in_=P[5][:], rd=[ps_r[5]], wr=[ps_r[5]])
                for q in range(4):
                    k = q % 2
                    V("tensor_tensor", out=tmpc[:, k, :], in0=hcv[:, q, :], in1=P[4][:], op=ALU.subtract,
                      rd=[hc_r, ps_r[4]], wr=[tmpc_r[k]] if False else [rot_r[k]])
                    V("scalar_tensor_tensor", out=tmpc[:, k, :], in0=tmpc[:, k, :],
                      scalar=cst[:, base + 128 + q:base + 129 + q], in1=P[5][:], op0=ALU.mult, op1=ALU.mult,
                      rd=[rot_r[k], ps_r[5], cst_r], wr=[rot_r[k]])
                    A("activation", out=ystc[:, q, :], in_=tmpc[:, k, :], func=AF.Silu,
                      bias=cst[:, base + 132 + q:base + 133 + q], rd=[rot_r[k], cst_r], wr=[ystc_r], nowaw=True)
                G(ymix[:, 4:8, cs], ystc, rd=[ystc_r], wr=[ym_r], sem=ys_sems[0], nowaw=True)

            def s5_scan(l):
                j = l // 2
                base = SCOL + j * 144
                nch = SEQ // TS
                for q in range(4):
                    G(uTq, uS[q], rd=[uS_r], wr=[uq_r], sem=kv_sems[1])
                    for d, gh in ((0, 0), (0, 1), (1, 0), (1, 1)):
                        G(s5wt[:].rearrange("p g a n -> p (g a n)"), s5W[j, d, q][:, gh * 2048:(gh + 1) * 2048],
                          wr=[s5w_r], sem=kv_sems[2])
                        for g4 in range(4):
                            G(tab_ap[g4], tabS[j, d, q * 8 + gh * 4 + g4].rearrange("p (c t) -> p c t", c=2),
                              wr=[tab_r[g4]], sem=tb_sems[g4 % 2])
                            V("memset", stc[:, g4:g4 + 1], 0.0, wr=[stc_r[g4]])
                        its = []
                        for cc in range(nch):
                            for g8 in range(4):
                                u = s5_scan.it % 2
                                s5_scan.it += 1
                                its.append((cc, g8, u))

                        def ctx(it):
                            cc, g8, u = it
                            if d == 0:
                                rng = slice(cc * TS, (cc + 1) * TS)
                                urhs = uTq[:, rng]
                            else:
                                rng = slice(SEQ - (cc + 1) * TS, SEQ - cc * TS)
                                urhs = uTq[:, rng][:, ::-1]
                            return cc, g8, u, rng, urhs, 2 + cc % 2, d * 32 + q * 8 + gh * 4 + g8

                        def front(it):
                            cc, g8, u, rng, urhs, py, dg = ctx(it)
                            T(P[u][:, 0:TS], lhsT=s5wt[:, g8, 0, :], rhs=urhs, start=True, stop=True,
                              rd=[s5w_r, uq_r], wr=[ps_r[u]])
                            T(P[u][:, TS:2 * TS], lhsT=s5wt[:, g8, 1, :], rhs=urhs, start=True, stop=True,
                              rd=[s5w_r, uq_r], wr=[ps_r[u]])

                        def mid(it):
                            cc, g8, u, rng, urhs, py, dg = ctx(it)
                            cosT = tab_ap[g8][:, 0, :]
                            sinT = tab_ap[g8][:, 1, :]
                            A("activation", out=rhoT[:, u, :], in_=cosT, func=AF.Identity, scale=0.0,
                              bias=s5p[:, j, 0, dg:dg + 1], rd=[tab_r[g8], cst_r], wr=[rho_r[u]])
                            V("tensor_tensor", out=rot2[:, 0, :], in0=P[u][:, TS:2 * TS], in1=sinT, op=ALU.mult,
                              rd=[ps_r[u], tab_r[g8]], wr=[rot_r[0]])
                            V("tensor_tensor", out=rot2[:, 1, :], in0=P[u][:, 0:TS], in1=cosT, op=ALU.mult,
                              rd=[ps_r[u], tab_r[g8]], wr=[rot_r[1]])
                            V("tensor_tensor", out=rot2[:, 1, :], in0=rot2[:, 1, :], in1=rot2[:, 0, :], op=ALU.add,
                              rd=[rot_r[0], rot_r[1]], wr=[rot_r[1]])
                            V("tensor_tensor_scan", out=gt2[:, u, :], data0=rhoT[:, u, :], data1=rot2[:, 1, :],
                              initial=stc[:, g8:g8 + 1], op0=ALU.mult, op1=ALU.add,
                              rd=[rho_r[u], rot_r[1], stc_r[g8]], wr=[gt_r[u]])
                            S.add("gpsimd", "tensor_tensor", out=G4[:, u, :], in0=gt2[:, u, :], in1=cosT, op=ALU.mult,
                                  rd=[gt_r[u], tab_r[g8]], wr=[G_r[u]])
                            S.add("gpsimd", "tensor_tensor", out=G4[:, 2 + u, :], in0=gt2[:, u, :], in1=sinT, op=ALU.mult,
                                  rd=[gt_r[u], tab_r[g8]], wr=[G2_r[u]])

                        def back(it):
                            cc, g8, u, rng, urhs, py, dg = ctx(it)
                            T(P[4 + u][:, 0:2], lhsT=swm[:], rhs=gt2[:, u, TS - 2:TS], start=True, stop=True,
                              rd=[gt_r[u], cst_r], wr=[ps_r[4 + u]])
                            T(P[py][:, 0:TS], lhsT=s5wt[:, g8, 2, :], rhs=G4[:, u, :], start=(g8 == 0), stop=False,
                              rd=[s5w_r, G_r[u]], wr=[ps_r[py]])
                            T(P[py][:, 0:TS], lhsT=s5wt[:, g8, 3, :], rhs=G4[:, 2 + u, :], start=False,
                              stop=(g8 == 3), rd=[s5w_r, G2_r[u]], wr=[ps_r[py]])

                        def post(it):
                            cc, g8, u, rng, urhs, py, dg = ctx(it)
                            V("tensor_scalar", out=stc[:, 8 + g8:9 + g8], in0=P[4 + u][:, 1:2],
                              scalar1=s5p[:, j, 2, dg:dg + 1], scalar2=None, op0=ALU.mult,
                              rd=[ps_r[4 + u], cst_r], wr=[stc_r[g8]])
                            V("scalar_tensor_tensor", out=stc[:, g8:g8 + 1], in0=gt2[:, u, TS - 1:TS],
                              scalar=s5p[:, j, 1, dg:dg + 1], in1=stc[:, 8 + g8:9 + g8], op0=ALU.mult,
                              op1=ALU.subtract, rd=[gt_r[u], stc_r[g8], cst_r], wr=[stc_r[g8]])
                            if g8 != 3:
                                return
                            if d == 0 and gh == 0:
                                V("scalar_tensor_tensor", out=yq[:, rng], in0=uTq[:, rng],
                                  scalar=cst[:, base + 136 + q:base + 137 + q], in1=P[py][:, 0:TS], op0=ALU.mult,
                                  op1=ALU.add, rd=[uq_r, ps_r[py], cst_r], wr=[yq_r], nowaw=True)
                            elif d == 0:
                                V("tensor_tensor", out=yq[:, rng], in0=yq[:, rng], in1=P[py][:, 0:TS],
                                  op=ALU.add, rd=[yq_r, ps_r[py]], wr=[yq_r], nowaw=True)
                            else:
                                V("tensor_tensor", out=yq[:, rng][:, ::-1], in0=yq[:, rng][:, ::-1], in1=P[py][:, 0:TS],
                                  op=ALU.add, rd=[yq_r, ps_r[py]], wr=[yq_r], nowaw=True)

                        front(its[0])
                        for ii, it in enumerate(its):
                            if ii + 1 < len(its):
                                front(its[ii + 1])
                            mid(it)
                            if ii >= 1:
                                post(its[ii - 1])
                            back(it)
                        post(its[-1])
                    for c in range(NCH):
                        cs = slice(c * TC, (c + 1) * TC)
                        k = c % 2
                        A("activation", out=rot512, in_=yq[:, cs], func=AF.Square, rd=[yq_r], wr=[rot_r[0], rot_r[1]])
                        V("tensor_scalar", out=rot512, in0=rot512, scalar1=0.044715, scalar2=1.0, op0=ALU.mult,
                          op1=ALU.add, rd=[rot_r[0], rot_r[1]], wr=[rot_r[0], rot_r[1]])
                        V("tensor_tensor", out=rot512, in0=rot512, in1=yq[:, cs], op=ALU.mult,
                          rd=[rot_r[0], rot_r[1], yq_r], wr=[rot_r[0], rot_r[1]])
                        A("activation", out=gt512, in_=rot512, func=AF.Sigmoid, scale=1.5957691216057308,
                          rd=[rot_r[0], rot_r[1]], wr=[gt_r[0], gt_r[1]])
                        V("tensor_tensor", out=g512[:, k, :], in0=yq[:, cs], in1=gt512, op=ALU.mult,
                          rd=[yq_r, gt_r[0], gt_r[1]], wr=[G_r[0], G_r[1], G2_r[0], G2_r[1]])
                        G(zS[q, :, cs], g512[:, k, :], rd=[G_r[0], G_r[1], G2_r[0], G2_r[1]], wr=[zS_r], sem=zs_sems[k],
                          nowaw=True)

            s5_scan.it = 0

            def ssm_out(c, l):
                j = l // 2
                cs = slice(c * TC, (c + 1) * TC)
                base = SCOL + j * 144
                zt = pbf.rearrange("p (q t) -> p q t", q=4)
                G(zt, zS[:, :, cs].rearrange("q p t -> p q t"), rd=[zS_r], wr=[pb_r[0], pb_r[1]], sem=p_sems[0])
                G(xn[:, 4:8, :], ymix[:, 4:8, cs], rd=[ym_r], wr=xn_r[4:8], sem=ym_sem)
                for oc in range(4):
                    w, wr_ = wget(wb["wsg"][j * 4 + oc], 512)
                    wv = w.rearrange("p (k n) -> p k n", k=4)
                    pp = oc % 2
                    for kc in range(4):
                        T(P[pp][:], lhsT=wv[:, kc, :], rhs=zt[:, kc, :], start=(kc == 0), stop=(kc == 3),
                          rd=[wr_, pb_r[0], pb_r[1]], wr=[ps_r[pp]])
                    A("activation", out=sg[:, pp, :], in_=P[pp][:], func=AF.Sigmoid,
                      bias=cst[:, base + 140 + oc:base + 141 + oc], rd=[ps_r[pp], cst_r], wr=[sg_r[pp]])
                    V("tensor_tensor", out=xn[:, oc, :], in0=zt[:, oc, :], in1=sg[:, pp, :], op=ALU.mult,
                      rd=[pb_r[0], pb_r[1], sg_r[pp]], wr=[xn_r[oc]])
                for oc in range(8):
                    w, wr_ = wget(wb["wso"][j * 8 + oc], 1024)
                    wv = w.rearrange("p (k n) -> p k n", k=8)
                    po = 4 + oc % 2
                    for kc in range(8):
                        T(P[po][:], lhsT=wv[:, kc, :], rhs=xn[:, kc, :], start=(kc == 0), stop=(kc == 7),
                          rd=[wr_, xn_r[kc]], wr=[ps_r[po]])
                    V("tensor_tensor", out=x[:, oc, cs], in0=P[po][:], in1=x[:, oc, cs], op=ALU.add,
                      rd=[ps_r[po], xr[c][oc]], wr=[xr[c][oc]])

            outs = []
            for s in range(NS):
                for c in range(NCH):
                    cs = slice(c * TC, (c + 1) * TC)
                    S.add("gpsimd", "dma_start", out=x[:, :, cs], in_=xT[s, :, :, cs],
                          wr=xr[c], dma_sem=x_sems[c], extra=(lst if s == 0 else []))
                for l in range(cfg.depth):
                    for c in range(NCH):
                        if cfg.stage >= 1:
                            ffn(c, l, 0)
                    has_mix = cfg.mixers and (l % 2 == 0)
                    has_ssm = cfg.mixers and (l % 2 == 1)
                    if has_ssm:
                        for c in range(NCH):
                            ssm_in(c, l)
                        S.barrier()
                        for c in range(NCH):
                            conv_chunk(c, l)
                        S.barrier()
                        s5_scan(l)
                        S.barrier()
                    if has_mix:
                        for c in range(NCH):
                            attn_in(c, l)
                        S.barrier()
                        attn_layer(l)
                        S.barrier()
                    for c in range(NCH):
                        if has_mix:
                            mix_out(c, l, "wao")
                        if has_ssm:
                            ssm_out(c, l)
                        if cfg.stage >= 2:
                            ffn(c, l, 1)
                        if cfg.stage >= 3:
                            ple(s, c, l)
                        if l == cfg.depth - 1:
                            cs = slice(c * TC, (c + 1) * TC)
                            o = S.add("gpsimd", "dma_start", out=yT[s, :, :, cs], in_=x[:, :, cs],
                                      rd=xr[c], dma_sem=x_sems[c])
                            outs.append(o)
            S.add("gpsimd", "nop", extra=([] if S.dry else outs))
            if S.dry:
                record.reqs = WS.reqs

        EPSC = ncst - 1
        S.dry = True
        record()
        S.dry = False
        record()
        S.emit(nc, block, eng_sems)
    return nc, S


_CACHE = {}


def prep_inputs(inp, cfg):
    xs = np.concatenate([np.asarray(inp["x_prompt"]), np.asarray(inp["x_sample"])], axis=0)
    ps = np.concatenate([np.asarray(inp["p_prompt"]), np.asarray(inp["p_sample"])], axis=1)
    W = pack_weights(inp)
    W["ball"] = attn_tables(inp)
    W.update(ssm_params(inp))
    cst = pack_consts(inp)
    cst = np.concatenate([cst, np.full((128, 1), EPS, np.float32)], axis=1)
    return xs, ps, W, cst


def core_inputs(xs, ps, W, cst, sl):
    NS = len(sl)
    xT = np.ascontiguousarray(xs[sl].reshape(NS, SEQ, 8, 128).transpose(0, 3, 2, 1))
    pT = np.ascontiguousarray(ps[:, sl].reshape(DEPTH, NS, SEQ, 2, 128).transpose(1, 0, 4, 3, 2))
    m = {"xT": xT, "pT": pT, "cst": cst}
    m.update(W)
    return m


def kernel(**inp):
    cfg = Cfg()
    inp = {k: np.asarray(v) for k, v in inp.items()}
    xs, ps, W, cst = prep_inputs(inp, cfg)
    nseq_total = xs.shape[0]
    wshapes = {nm: (a.shape[0], a.shape[2]) for nm, a in W.items() if nm.startswith("w")}
    nc, S = build_program(cfg, wshapes, cst.shape[1])
    in_maps = []
    NS = cfg.nseq
    for core in range(N_CORES):
        sl = [(core * NS + i) % nseq_total for i in range(NS)]
        in_maps.append(core_inputs(xs, ps, W, cst, sl))
    res = run_bass_kernel_spmd(nc, in_maps, core_ids=list(range(N_CORES)))
    ys = np.zeros((nseq_total, SEQ, D_MODEL), np.float32)
    for core in range(N_CORES):
        yT = res.results[core]["yT"]
        y = yT.transpose(0, 3, 2, 1).reshape(NS, SEQ, D_MODEL)
        for i in range(NS):
            ys[(core * NS + i) % nseq_total] = y[i]
    nb = inp["x_prompt"].shape[0]
    return ys[:nb], ys[nb:]
```

```python
import numpy as np
import concourse.bass as bass
import concourse.mybir as mybir
from concourse.bass_utils import run_bass_kernel_spmd

F32 = mybir.dt.float32
BF16 = mybir.dt.bfloat16
AF = mybir.ActivationFunctionType
ALU = mybir.AluOpType

D_MODEL = 1024
SEQ = 4096
DEPTH = 4
D_FF = 2816
NJ = D_FF // 128
PLE_DIM = 256
TC = 512
NCH = SEQ // TC
EPS = 1e-6
N_CORES = 8
SEQ_PER_CORE = 3


class Res:
    __slots__ = ("name", "w", "rd", "psum", "co")

    def __init__(self, name="", psum=False):
        self.name = name
        self.w = None
        self.rd = []
        self.psum = psum
        self.co = []


class Ins:
    __slots__ = ("eng", "meth", "args", "kw", "deps", "pos", "sig", "val", "sem", "is_dma")


ENGS = ["tensor", "scalar", "vector", "gpsimd", "sync"]


class Sched:
    def __init__(self):
        self.by_eng = {e: [] for e in ENGS}
        self.dry = False
        self.sem_counts = {}
        self.n = 0
        self.dma_pending = []
        self.last_dma = {}

    def add(self, eng, meth, *args, rd=(), wr=(), dma_sem=None, extra=(), nowaw=False, grp=False, **kw):
        if self.dry:
            return None
        i = Ins()
        i.eng = eng
        i.meth = meth
        i.args = args
        i.kw = kw
        i.pos = len(self.by_eng[eng])
        i.sig = False
        i.val = 0
        i.sem = dma_sem
        i.is_dma = dma_sem is not None
        if i.is_dma:
            self.sem_counts[id(dma_sem)] = self.sem_counts.get(id(dma_sem), 0) + 16
            i.val = self.sem_counts[id(dma_sem)]
        raw = set()
        oth = set()
        waw = set()
        for r in rd:
            if r.w is not None:
                raw.add(r.w)
            raw.update(r.co)
            if r.psum:
                for j in r.rd:
                    if j.eng != eng:
                        oth.add(j)
        for r in wr:
            if r.w is not None and not nowaw:
                oth.add(r.w)
                oth.update(r.co)
                waw.add(r.w)
                waw.update(r.co)
            oth.update(r.rd)
        deps = []
        raw.update(extra)
        if i.is_dma:
            prev = self.last_dma.get(id(dma_sem))
            if prev is not None and not grp:
                raw.add(prev)
            self.last_dma[id(dma_sem)] = i
        for d in raw | oth:
            if d is i:
                continue
            if d.is_dma or i.is_dma:
                deps.append(d)
            elif d.eng != eng:
                deps.append(d)
            else:
                if eng != "tensor":
                    deps.append(d)
        for d in deps:
            if not d.is_dma:
                d.sig = True
        i.deps = deps
        for r in rd:
            r.rd.append(i)
        for r in wr:
            if nowaw and r.w is not None:
                r.co = r.co[-64:] + [r.w]
            else:
                r.co = []
            r.w = i
            r.rd = []
        self.by_eng[eng].append(i)
        self.n += 1
        if i.is_dma:
            self.dma_pending.append(i)
        return i

    def barrier(self):
        if self.dry:
            return
        lasts = [self.by_eng[e][-1] for e in ENGS if self.by_eng[e]]
        deps = lasts + self.dma_pending
        self.dma_pending = []
        for e in ENGS:
            self.add(e, "nop", extra=deps)

    def emit(self, nc, block, eng_sems):
        for e in ENGS:
            cnt = 0
            for i in self.by_eng[e]:
                if i.is_dma:
                    continue
                if i.sig:
                    cnt += 1
                    i.val = cnt
                    i.sem = eng_sems[e]

        def run(e, handle):
            waited = {}
            for i in self.by_eng[e]:
                need = {}
                for d in i.deps:
                    key = id(d.sem)
                    if key not in need or need[key][1] < d.val:
                        need[key] = (d.sem, d.val)
                for key, (sm, val) in need.items():
                    if waited.get(key, 0) < val:
                        handle.wait_ge(sm, val)
                        waited[key] = val
                bi = getattr(handle, i.meth)(*i.args, **i.kw)
                if i.is_dma:
                    bi.then_inc(i.sem, 16)
                elif i.sig:
                    bi.then_inc(i.sem, 1)

        @block.tensor
        def _(h):
            run("tensor", h)

        @block.scalar
        def _(h):
            run("scalar", h)

        @block.vector
        def _(h):
            run("vector", h)

        @block.gpsimd
        def _(h):
            run("gpsimd", h)

        @block.sync
        def _(h):
            run("sync", h)


def blk_in(w, ncols_blocks=None):
    K, N = w.shape
    return np.ascontiguousarray(w.reshape(K // 128, 128, N // 128, 128).transpose(2, 1, 0, 3))


BIGW = {}


def pack_weights(inp):
    out = {}
    ffn_w = {"ffn1": (inp["w_ffn1_in"], inp["w_ffn1_out"]), "ffn2": (inp["w_ffn2_in"], inp["w_ffn2_out"])}
    for nm in ("ffn1", "ffn2"):
        wi, wo = ffn_w[nm]
        a = []
        for l in range(DEPTH):
            g = blk_in(wi[l][:, :D_FF])
            u = blk_in(wi[l][:, D_FF:])
            a.append(np.stack([g, u], axis=2))
        out["w%si" % nm[-1]] = np.stack(a).reshape(DEPTH * NJ, 128, 2 * 8 * 128)
        b = [blk_in(wo[l]) for l in range(DEPTH)]
        out["w%so" % nm[-1]] = np.stack(b).reshape(DEPTH * 8, 128, NJ * 128)
    out["wpg"] = np.stack([blk_in(inp["w_ple_gate"][l]) for l in range(DEPTH)]).reshape(DEPTH * 8, 128, 1024)
    out["wpp"] = np.ascontiguousarray(
        inp["w_ple_proj"].reshape(DEPTH, 2, 128, 1024).transpose(0, 2, 1, 3)).reshape(DEPTH, 128, 2048)
    A_Q, A_KV, B_Q = 512, 128, 512
    wai, wav, wao = [], [], []
    for j in range(2):
        wi = inp["w_attn_in"][j]
        qa = wi[:, :A_Q]
        cols = []
        for cc in range(4):
            cols.append(qa[:, cc * 64:(cc + 1) * 64])
            cols.append(qa[:, (4 + cc) * 64:(5 + cc) * 64])
        cols.append(wi[:, A_Q:A_Q + A_KV])
        cols.append(wi[:, A_Q + 2 * A_KV:A_Q + 2 * A_KV + B_Q])
        cols.append(wi[:, A_Q + 2 * A_KV + B_Q:A_Q + 2 * A_KV + 2 * B_Q])
        fm = np.concatenate(cols, axis=1)
        wai.append(blk_in(fm).reshape(13, 128, 1024))
        vv = np.concatenate([wi[:, A_Q + 2 * A_KV + 2 * B_Q:], wi[:, A_Q + A_KV:A_Q + 2 * A_KV]], axis=1)
        vv = vv.reshape(2, 4, 128, 640).transpose(0, 2, 1, 3)
        wav.append(np.ascontiguousarray(vv).reshape(2, 128, 2560))
        wo = inp["w_attn_out"][j]
        rows = []
        for cc in range(4):
            rows.append(wo[cc * 64:(cc + 1) * 64])
            rows.append(wo[(4 + cc) * 64:(5 + cc) * 64])
        rows.append(wo[512:])
        wao.append(blk_in(np.concatenate(rows, axis=0)).reshape(8, 128, 1024))
    out["wai"] = np.concatenate(wai)
    out["wav"] = np.concatenate(wav)
    out["wao"] = np.concatenate(wao)
    out["wsi"] = np.concatenate([blk_in(inp["w_ssm_in"][j]).reshape(12, 128, 1024) for j in range(2)])
    out["wsg"] = np.concatenate([blk_in(inp["w_glu_c"][j]).reshape(4, 128, 512) for j in range(2)])
    out["wso"] = np.concatenate([blk_in(inp["w_ssm_out"][j]).reshape(8, 128, 1024) for j in range(2)])
    return out


def ssm_params(inp):
    G, N, Pc = 32, 64, 16
    s5A = np.zeros((2, 128, 3, 64), np.float32)
    s5B = np.zeros((2, 2, 4, 128, 5, 64), np.float32)
    s5C = np.zeros((2, 2, 4, 128, 2, 128), np.float32)
    for j in range(2):
        for d in range(2):
            lre = inp["lam_re"][j, d]
            lim = inp["lam_im"][j, d]
            ldt = inp["log_dt"][j, d]
            cols = slice(d * 32, (d + 1) * 32)
            s5A[j, :, 0, cols] = np.tile(lre.T, (2, 1))
            s5A[j, :, 1, cols] = np.tile(lim.T, (2, 1))
            s5A[j, :, 2, cols] = np.broadcast_to(ldt[None, :], (128, 32))
            for q in range(4):
                gs = slice(q * 8, (q + 1) * 8)
                s5B[j, d, q, :, 0, :] = inp["b_re"][j, d, gs].transpose(0, 2, 1).reshape(128, N)
                s5B[j, d, q, :, 1, :] = inp["b_im"][j, d, gs].transpose(0, 2, 1).reshape(128, N)
                s5B[j, d, q, :, 2, :] = np.repeat(lre[gs], Pc, axis=0)
                s5B[j, d, q, :, 3, :] = np.repeat(lim[gs], Pc, axis=0)
                s5B[j, d, q, :, 4, :] = np.repeat(ldt[gs], Pc)[:, None]
                cre = inp["c_re"][j, d, gs].transpose(2, 0, 1).reshape(N, 128)
                cim = inp["c_im"][j, d, gs].transpose(2, 0, 1).reshape(N, 128)
                s5C[j, d, q, :64, 0, :] = cre
                s5C[j, d, q, 64:, 0, :] = cim
                s5C[j, d, q, :64, 1, :] = cim
                s5C[j, d, q, 64:, 1, :] = cre
    kc32 = np.zeros((128, 2, 128), np.float32)
    kc32[:, 0, :] = np.eye(128, dtype=np.float32)
    for pp in range(64):
        kc32[pp + 64, 1, pp] = 1.0
        kc32[pp, 1, pp + 64] = -1.0
    return {"s5A": s5A, "s5B": s5B, "s5C": s5C, "kc32": kc32}


NEG = -30000.0


def attn_tables(inp):
    out = np.full((2, 5, 8, 128, 1024), NEG, np.float32)
    k = np.arange(128)[:, None]
    q = np.arange(128)[None, :]
    slopes = [2.0 ** (-(i + 1)) for i in range(8)]
    for ti, n in enumerate((0, 1, 2, 30, 31)):
        lo = min(max(n - 2, 0), 27)
        low = min(max(n - 1, 0), 29)
        qtok = n * 128 + q
        r = qtok // 64
        c = qtok % 64
        rs = np.clip(r - 4, 0, 56)
        cs_ = np.clip(c - 8, 0, 48)
        for jj in range(5):
            ktok = (lo + jj) * 128 + k
            kr = ktok // 64
            kc = ktok % 64
            ok = (kr >= rs) & (kr < rs + 8) & (kc >= cs_) & (kc < cs_ + 16)
            dr = np.clip(kr - r + 7, 0, 14)
            dc = np.clip(kc - c + 15, 0, 30)
            for j in range(2):
                for h in range(8):
                    g = inp["rpb_b"][j, h][dr, dc]
                    out[j, ti, h, :, jj * 128:(jj + 1) * 128] = np.where(ok, g, NEG)
        for jj in range(3):
            ktok = (low + jj) * 128 + k
            dist = np.abs(ktok - qtok)
            okw = dist <= 128
            for h in range(8):
                tab = np.where(okw, (-slopes[h]) * dist.astype(np.float32), NEG).astype(np.float32)
                out[:, ti, h, :, 640 + jj * 128:640 + (jj + 1) * 128] = tab
    return out


def pack_consts(inp):
    cols = []

    def gain(v):
        return v.reshape(8, 128).T

    for l in range(DEPTH):
        for nm in ("norm_ffn1", "norm_mix", "norm_ffn2", "norm_ple", "norm_ple_post"):
            cols.append(gain(inp[nm][l]))
    for j in range(2):
        for nm in ("q_gain_a", "k_gain_a", "q_gain_b", "k_gain_b"):
            cols.append(np.tile(inp[nm][j], 2)[:, None])
        cols.append(np.broadcast_to(inp["sink_a"][j][None, :], (128, 8)))
    for j in range(2):
        cw = inp["conv_w"][j]
        for q in range(4):
            cols.append(cw[:, q * 128:(q + 1) * 128].T)
        for nm in ("conv_b", "ln_g_d", "ln_b_d", "d_skip", "b_glu_c"):
            cols.append(inp[nm][j].reshape(4, 128).T)
    gm = np.zeros((128, 8), np.float32)
    for g8 in range(8):
        gm[g8 * 16:(g8 + 1) * 16, g8] = 1.0
    cols.append(gm)
    cols.append(np.full((128, 1), np.pi / 2, np.float32))
    return np.ascontiguousarray(np.concatenate(cols, axis=1).astype(np.float32))


def gcol(l, which):
    return (l * 5 + which) * 8


ACOL = DEPTH * 5 * 8
SCOL = ACOL + 24
GMC = SCOL + 288
HPIC = GMC + 8
TS = 256


class Cfg:
    nseq = SEQ_PER_CORE
    depth = DEPTH
    mixers = True
    stage = 9


def build_program(cfg, wshapes, ncst):
    nc = bass.Bass("TRN2", target_bir_lowering=False)
    NS = cfg.nseq
    xT = nc.dram_tensor("xT", [NS, 128, 8, SEQ], F32, kind="ExternalInput").ap()
    pT = nc.dram_tensor("pT", [NS, DEPTH, 128, 2, SEQ], F32, kind="ExternalInput").ap()
    yT = nc.dram_tensor("yT", [NS, 128, 8, SEQ], F32, kind="ExternalOutput").ap()
    cst_d = nc.dram_tensor("cst", [128, ncst], F32, kind="ExternalInput").ap()
    wf = {}
    wb = {}
    for nm, (nb, fsz) in wshapes.items():
        wf[nm] = nc.dram_tensor(nm, [nb, 128, fsz], F32, kind="ExternalInput").ap()
        wb[nm] = nc.dram_tensor(nm + "_bf", [nb, 128, fsz], BF16, kind="Internal").ap()

    ball = nc.dram_tensor("ball", [2, 5, 8, 128, 1024], F32, kind="ExternalInput").ap()
    qkT = nc.dram_tensor("qkT_s", [13, 128, SEQ], BF16, kind="Internal").ap()
    vS = nc.dram_tensor("vS_s", [32, 128, 640], BF16, kind="Internal").ap()
    ymix = nc.dram_tensor("ymix_s", [128, 8, SEQ], BF16, kind="Internal").ap()

    s5A_d = nc.dram_tensor("s5A", [2, 128, 3, 64], F32, kind="ExternalInput").ap()
    s5B_d = nc.dram_tensor("s5B", [2, 2, 4, 128, 5, 64], F32, kind="ExternalInput").ap()
    s5C_d = nc.dram_tensor("s5C", [2, 2, 4, 128, 2, 128], F32, kind="ExternalInput").ap()
    kc32_d = nc.dram_tensor("kc32", [128, 2, 128], F32, kind="ExternalInput").ap()
    uS = nc.dram_tensor("uS_s", [4, 128, SEQ], BF16, kind="Internal").ap()
    hhS = nc.dram_tensor("hhS_s", [4, 128, SEQ + 32], BF16, kind="Internal").ap()
    zS = nc.dram_tensor("zS_s", [4, 128, SEQ], BF16, kind="Internal").ap()
    tabS = nc.dram_tensor("tabS_s", [2, 2, 32, 128, 2 * TS], F32, kind="Internal").ap()
    s5W = nc.dram_tensor("s5W_s", [2, 2, 4, 128, 8 * 4 * 128], BF16, kind="Internal").ap()
    dgS = nc.dram_tensor("dgS_s", [2, 4, 2, 128, 2048], BF16, kind="Internal").ap()

    D = 4
    WSLOT = 3072
    S = Sched()
    import contextlib
    with contextlib.ExitStack() as es:
        def sb(name, shape, dt):
            return es.enter_context(nc.sbuf_tensor(name, shape, dt))

        x = sb("x", [128, 8, SEQ], F32)
        wsl = sb("wsl", [128, D, WSLOT], BF16)
        xn = sb("xn", [128, 8, TC], BF16)
        hb = sb("hb", [128, NJ, TC], BF16)
        sq = sb("sq", [128, 2, TC], BF16)
        sg = sb("sg", [128, 2, TC], F32)
        pb = sb("pb", [128, 2, 2, TC], BF16)
        cst = sb("cstt", [128, ncst], F32)
        ones = sb("ones", [128, 128], BF16)
        bones = sb("bones", [128, 128], BF16)
        esink = sb("esink", [128, 16], F32)
        stA = sb("stA", [128, 2, TC], BF16)
        vst = sb("vst", [128, 2, 640], BF16)
        ident = sb("ident", [128, 128], BF16)
        swm = sb("swm", [128, 128], F32)
        s5p = sb("s5p", [128, 2, 3, 64], F32)
        s5wt = sb("s5wt", [128, 4, 4, 128], BF16)
        stc = sb("stc", [128, 16], F32)
        rhoT = sb("rhoT", [128, 2, TS], F32)
        zero16 = sb("zero16", [128, 16], BF16)
        P = [es.enter_context(nc.psum_tensor("ps%d" % i, [128, TC], F32)) for i in range(8)]
        nsem = 0

        def sem(name):
            return es.enter_context(nc.semaphore(name))

        eng_sems = {e: sem("e_" + e) for e in ENGS}
        w_sems = [sem("w%d" % i) for i in range(D)]
        x_sems = [sem("x%d" % i) for i in range(NCH)]
        p_sems = [sem("p%d" % i) for i in range(2)]
        c_sem = sem("cst")
        cv_sems = [sem("cv%d" % i) for i in range(2)]
        cvs_sems = [sem("cvs%d" % i) for i in range(2)]
        sta_sems = [sem("sta%d" % i) for i in range(2)]
        vst_sems = [sem("vst%d" % i) for i in range(2)]
        qt_sems = [sem("qt%d" % i) for i in range(2)]
        kv_sems = [sem("kv%d" % i) for i in range(3)]
        bi_sems = [sem("bi%d" % i) for i in range(2)]
        ys_sems = [sem("ys%d" % i) for i in range(2)]
        ym_sem = sem("ym")
        pr_sems = [sem("pr%d" % i) for i in range(6)]
        tb_sems = [sem("tb%d" % i) for i in range(2)]
        tb2_sems = [sem("tbb%d" % i) for i in range(2)]
        s5_sems = [sem("s5_%d" % i) for i in range(4)]
        zs_sems = [sem("zs%d" % i) for i in range(2)]
        block = es.enter_context(nc.Block())

        hb32 = hb[:].rearrange("p j t -> p (j t)").bitcast(F32)
        hbf = hb[:].rearrange("p j t -> p (j t)")
        Vt = hbf[:, 0:3200].rearrange("p (t f) -> p t f", t=5)
        kbT = hbf[:, 3200:5760].rearrange("p (c t) -> p c t", c=4)
        kaT = hbf[:, 5760:6400]
        pTt = hbf[:, 6400:7680].rearrange("p (u t) -> p u t", u=2)
        et = hbf[:, 7680:10240].bitcast(F32).rearrange("p (u t) -> p u t", u=2)
        biast = xn[:].rearrange("p k t -> p (k t)").bitcast(F32).rearrange("p (u t) -> p u t", u=2)
        qtt = sg[:].rearrange("p u t -> p (u t)").bitcast(BF16).rearrange("p (u c t) -> p u c t", u=2, c=8)
        ystt = pb[:].rearrange("p a b t -> p (a b t)").rearrange("p (u c t) -> p u c t", u=2, c=8)
        dent = sq[:].rearrange("p u t -> p (u t)").bitcast(F32).rearrange("p (u t) -> p u t", u=2)
        sgf = sg[:].rearrange("p u t -> p (u t)")
        pbf = pb[:].rearrange("p a b t -> p (a b t)")
        tab_ap = [sgf[:, 0:512], sgf[:, 512:1024], pbf[:, 0:1024].bitcast(F32), pbf[:, 1024:2048].bitcast(F32)]
        tab_ap = [t.rearrange("p (c t) -> p c t", c=2) for t in tab_ap]
        yq = hbf[:, 0:8192].bitcast(F32)
        rot2 = hbf[:, 8192:9216].bitcast(F32).rearrange("p (u t) -> p u t", u=2)
        rot512 = hbf[:, 8192:9216].bitcast(F32)
        gt2 = hbf[:, 9216:10240].bitcast(F32).rearrange("p (u t) -> p u t", u=2)
        gt512 = hbf[:, 9216:10240].bitcast(F32)
        G4 = hbf[:, 10240:11264].rearrange("p (u t) -> p u t", u=4)
        g512 = hbf[:, 10240:11264].rearrange("p (u t) -> p u t", u=2)
        uTq = xn[:].rearrange("p k t -> p (k t)")
        hhwin = hbf[:, 0:2176].rearrange("p (q t) -> p q t", q=4)
        hcv = hbf[:, 2176:6272].bitcast(F32).rearrange("p (q t) -> p q t", q=4)
        tmpc = hbf[:, 6272:8320].bitcast(F32).rearrange("p (u t) -> p u t", u=2)
        msq = hbf[:, 8320:9344].bitcast(F32)
        ystc = pbf.rearrange("p (q t) -> p q t", q=4)
        wkA = x[:, 2, :]
        wkT = x[:, 3, :]
        wkB = x[:, 4, :]
        wkC = x[:, 5, :]
        wkW = x[:, 6, :].bitcast(BF16)
        wkD = x[:, 7, :].bitcast(BF16)

        def record():
            xr = [[Res("x%d_%d" % (c, k)) for k in range(8)] for c in range(NCH)]
            xn_r = [Res() for _ in range(8)]
            h_r = [Res() for _ in range(NJ)]
            sq_r = [Res(), Res()]
            sg_r = [Res(), Res()]
            pb_r = [Res(), Res()]
            ps_r = [Res(psum=True) for _ in range(8)]
            ws_r = [Res() for _ in range(D)]
            cst_r = Res()
            wb_r = Res()
            sta_r = [Res(), Res()]
            vst_r = [Res(), Res()]
            qkT_r = Res()
            vS_r = Res()
            ym_r = Res()
            qt_r = [Res(), Res()]
            ka_r, kb_r, v_r = Res(), Res(), Res()
            bias_r = [Res(), Res()]
            e_r = [Res(), Res()]
            pT_r = [Res(), Res()]
            yst_r = [Res(), Res()]
            den_r = [Res(), Res()]

            class WS:
                reqs = []
                n = 0
                issued = 0

            if S.dry:
                WS.reqs = []
            else:
                WS.reqs = record.reqs

            def wget(dram_ap, fsz, held=0):
                if S.dry:
                    WS.reqs.append((dram_ap, fsz))
                    return wsl[:, 0, :fsz], ws_r[0]
                k = WS.n
                WS.n += 1
                while WS.issued < min(len(WS.reqs), k + D - held):
                    m = WS.issued
                    ap_m, f_m = WS.reqs[m]
                    S.add("sync", "dma_start", out=wsl[:, m % D, :f_m], in_=ap_m,
                          rd=[wb_r], wr=[ws_r[m % D]], dma_sem=w_sems[m % D])
                    WS.issued += 1
                return wsl[:, k % D, :fsz], ws_r[k % D]

            S.add("gpsimd", "dma_start", out=cst[:], in_=cst_d, wr=[cst_r], dma_sem=c_sem)
            S.add("vector", "memset", ones[:], 1.0, wr=[cst_r])
            S.add("vector", "memset", bones[:], 0.0, wr=[cst_r])
            S.add("vector", "memset", bones[0:64, 0:64], 1.0, wr=[cst_r])
            S.add("vector", "memset", bones[64:128, 64:128], 1.0, wr=[cst_r])
            for j in range(2):
                S.add("scalar", "activation", out=esink[:, j * 8:(j + 1) * 8],
                      in_=cst[:, ACOL + j * 12 + 4:ACOL + j * 12 + 12], func=AF.Exp, rd=[cst_r], wr=[cst_r])
            def V(meth, *args, rd=(), wr=(), **kw):
                return S.add("vector", meth, *args, rd=rd, wr=wr, **kw)

            def A(meth, *args, rd=(), wr=(), **kw):
                return S.add("scalar", meth, *args, rd=rd, wr=wr, **kw)

            def T(*args, rd=(), wr=(), **kw):
                return S.add("tensor", "matmul", *args, rd=rd, wr=wr, **kw)

            def G(out, in_, rd=(), wr=(), sem=None, **kw):
                return S.add("gpsimd", "dma_start", out=out, in_=in_, rd=rd, wr=wr, dma_sem=sem, **kw)

            def dbl(c_in, s_in, c_out, s_out, t1, t2, r):
                V("tensor_tensor", out=t1, in0=c_in, in1=s_in, op=ALU.mult, rd=[r], wr=[r])
                V("tensor_tensor", out=t2, in0=s_in, in1=s_in, op=ALU.mult, rd=[r], wr=[r])
                V("tensor_tensor", out=c_out, in0=c_in, in1=c_in, op=ALU.mult, rd=[r], wr=[r])
                V("tensor_tensor", out=c_out, in0=c_out, in1=t2, op=ALU.subtract, rd=[r], wr=[r])
                V("tensor_scalar", out=s_out, in0=t1, scalar1=2.0, scalar2=None, op0=ALU.mult, rd=[r], wr=[r])

            def cis(th, c, s_, t1, t2, r):
                A("activation", out=s_, in_=th, func=AF.Sin, scale=0.125, rd=[r, cst_r], wr=[r])
                A("activation", out=c, in_=th, func=AF.Sin, scale=-0.125, bias=cst[:, HPIC:HPIC + 1],
                  rd=[r, cst_r], wr=[r])
                for _ in range(3):
                    dbl(c, s_, c, s_, t1, t2, r)

            def ssm_prologue():
                rA, rB, rC = Res(), Res(), Res()
                rT = [Res(), Res()]
                rW = [Res(), Res()]
                rD = [Res(), Res()]
                S.add("gpsimd", "dma_start", out=ident[:], in_=kc32_d[:, 0, :], wr=[cst_r], dma_sem=pr_sems[0])
                S.add("scalar", "dma_start", out=swm[:], in_=kc32_d[:, 1, :], wr=[cst_r], dma_sem=pr_sems[5], nowaw=True)
                V("memset", zero16[:], 0.0, wr=[rC])
                for q in range(4):
                    S.add("scalar", "dma_start", out=hhS[q, :, 0:16], in_=zero16[:], rd=[rC], dma_sem=pr_sems[1])
                    S.add("scalar", "dma_start", out=hhS[q, :, SEQ + 16:SEQ + 32], in_=zero16[:], rd=[rC],
                          dma_sem=pr_sems[1])

                def a64(i):
                    return wkA[:, i * 64:(i + 1) * 64]

                def b64(i):
                    return wkB[:, i * 64:(i + 1) * 64]

                for j in range(2):
                    A3 = wkA[:, 0:192].rearrange("p (a b) -> p a b", a=3)
                    S.add("scalar", "dma_start", out=A3, in_=s5A_d[j], wr=[rA], dma_sem=pr_sems[2])
                    dt, th, t1, t2 = a64(3), a64(4), a64(5), a64(6)
                    CK = [a64(8 + k) for k in range(9)]
                    SK = [a64(17 + k) for k in range(9)]
                    A("activation", out=dt, in_=A3[:, 2, :], func=AF.Exp, rd=[rA], wr=[rA])
                    V("tensor_tensor", out=t1, in0=A3[:, 0, :], in1=dt, op=ALU.mult, rd=[rA], wr=[rA])
                    A("activation", out=s5p[:, j, 0, :], in_=t1, func=AF.Exp, rd=[rA], wr=[rA])
                    V("tensor_tensor", out=th, in0=A3[:, 1, :], in1=dt, op=ALU.mult, rd=[rA], wr=[rA])
                    cis(th, CK[0], SK[0], t1, t2, rA)
                    for k in range(8):
                        dbl(CK[k], SK[k], CK[k + 1], SK[k + 1], t1, t2, rA)
                    V("tensor_copy", out=s5p[:, j, 1, :], in_=CK[8], rd=[rA], wr=[rA])
                    V("tensor_copy", out=s5p[:, j, 2, :], in_=SK[8], rd=[rA], wr=[rA])
                    for dg in range(64):
                        k2 = dg % 2
                        base = k2 * 1024
                        tc_ = wkT[:, base:base + TS]
                        ts_ = wkT[:, base + TS:base + 2 * TS]
                        u1 = wkT[:, base + 2 * TS:base + 2 * TS + 128]
                        r = rT[k2]
                        V("tensor_copy", out=tc_[:, 0:1], in_=CK[0][:, dg:dg + 1], rd=[rA], wr=[r])
                        V("tensor_copy", out=ts_[:, 0:1], in_=SK[0][:, dg:dg + 1], rd=[rA], wr=[r])
                        for k in range(8):
                            n = 1 << k
                            ck = CK[k][:, dg:dg + 1]
                            sk = SK[k][:, dg:dg + 1]
                            V("tensor_scalar", out=u1[:, 0:n], in0=ts_[:, 0:n], scalar1=sk, scalar2=None, op0=ALU.mult,
                              rd=[r, rA], wr=[r])
                            V("scalar_tensor_tensor", out=tc_[:, n:2 * n], in0=tc_[:, 0:n], scalar=ck, in1=u1[:, 0:n],
                              op0=ALU.mult, op1=ALU.subtract, rd=[r, rA], wr=[r])
                            V("tensor_scalar", out=u1[:, 0:n], in0=ts_[:, 0:n], scalar1=ck, scalar2=None, op0=ALU.mult,
                              rd=[r, rA], wr=[r])
                            V("scalar_tensor_tensor", out=ts_[:, n:2 * n], in0=tc_[:, 0:n], scalar=sk, in1=u1[:, 0:n],
                              op0=ALU.mult, op1=ALU.add, rd=[r, rA], wr=[r])
                        S.add("scalar", "dma_start", out=tabS[j, dg // 32, dg % 32], in_=wkT[:, base:base + 2 * TS],
                              rd=[r], dma_sem=tb_sems[k2])
                    nw = 0
                    for d in range(2):
                        for q in range(4):
                            B5 = wkB[:, 0:320].rearrange("p (a b) -> p a b", a=5)
                            S.add("scalar", "dma_start", out=B5, in_=s5B_d[j, d, q], wr=[rB], dma_sem=pr_sems[3])
                            bre, bim, lre, lim, ldt = (B5[:, i, :] for i in range(5))
                            dtb, thb, c1, s1, u1, u2, rho, lbr, lbi, den, cr, ci = (b64(5 + i) for i in range(12))
                            BB = wkB[:, 1280:1408]
                            BBs = wkB[:, 1408:1536]
                            A("activation", out=dtb, in_=ldt, func=AF.Exp, rd=[rB], wr=[rB])
                            V("tensor_tensor", out=u1, in0=lre, in1=dtb, op=ALU.mult, rd=[rB], wr=[rB])
                            A("activation", out=rho, in_=u1, func=AF.Exp, rd=[rB], wr=[rB])
                            V("tensor_tensor", out=thb, in0=lim, in1=dtb, op=ALU.mult, rd=[rB], wr=[rB])
                            cis(thb, c1, s1, u1, u2, rB)
                            V("tensor_tensor", out=lbr, in0=rho, in1=c1, op=ALU.mult, rd=[rB], wr=[rB])
                            V("tensor_tensor", out=lbi, in0=rho, in1=s1, op=ALU.mult, rd=[rB], wr=[rB])
                            V("tensor_scalar", out=lbr, in0=lbr, scalar1=-1.0, scalar2=None, op0=ALU.add, rd=[rB], wr=[rB])
                            V("tensor_tensor", out=den, in0=lre, in1=lre, op=ALU.mult, rd=[rB], wr=[rB])
                            V("tensor_tensor", out=u1, in0=lim, in1=lim, op=ALU.mult, rd=[rB], wr=[rB])
                            V("tensor_tensor", out=den, in0=den, in1=u1, op=ALU.add, rd=[rB], wr=[rB])
                            V("reciprocal", out=den, in_=den, rd=[rB], wr=[rB])
                            V("tensor_tensor", out=cr, in0=lbr, in1=lre, op=ALU.mult, rd=[rB], wr=[rB])
                            V("tensor_tensor", out=u1, in0=lbi, in1=lim, op=ALU.mult, rd=[rB], wr=[rB])
                            V("tensor_tensor", out=cr, in0=cr, in1=u1, op=ALU.add, rd=[rB], wr=[rB])
                            V("tensor_tensor", out=cr, in0=cr, in1=den, op=ALU.mult, rd=[rB], wr=[rB])
                            V("tensor_tensor", out=ci, in0=lbi, in1=lre, op=ALU.mult, rd=[rB], wr=[rB])
                            V("tensor_tensor", out=u1, in0=lbr, in1=lim, op=ALU.mult, rd=[rB], wr=[rB])
                            V("tensor_tensor", out=ci, in0=ci, in1=u1, op=ALU.subtract, rd=[rB], wr=[rB])
                            V("tensor_tensor", out=ci, in0=ci, in1=den, op=ALU.mult, rd=[rB], wr=[rB])
                            V("tensor_tensor", out=BB[:, 0:64], in0=cr, in1=bre, op=ALU.mult, rd=[rB], wr=[rB])
                            V("tensor_tensor", out=u1, in0=ci, in1=bim, op=ALU.mult, rd=[rB], wr=[rB])
                            V("tensor_tensor", out=BB[:, 0:64], in0=BB[:, 0:64], in1=u1, op=ALU.subtract, rd=[rB], wr=[rB])
                            V("tensor_tensor", out=BB[:, 64:128], in0=cr, in1=bim, op=ALU.mult, rd=[rB], wr=[rB])
                            V("tensor_tensor", out=u1, in0=ci, in1=bre, op=ALU.mult, rd=[rB], wr=[rB])
                            V("tensor_tensor", out=BB[:, 64:128], in0=BB[:, 64:128], in1=u1, op=ALU.add, rd=[rB], wr=[rB])
                            V("tensor_copy", out=BBs[:, 0:64], in_=BB[:, 64:128], rd=[rB], wr=[rB])
                            V("tensor_scalar", out=BBs[:, 64:128], in0=BB[:, 0:64], scalar1=-1.0, scalar2=None,
                              op0=ALU.mult, rd=[rB], wr=[rB])
                            C2 = wkC[:, 0:256].rearrange("p (a b) -> p a b", a=2)
                            S.add("scalar", "dma_start", out=C2, in_=s5C_d[j, d, q], wr=[rC], dma_sem=pr_sems[4])
                            V("tensor_scalar", out=C2[64:128, 0, :], in0=C2[64:128, 0, :], scalar1=-1.0, scalar2=None,
                              op0=ALU.mult, rd=[rC], wr=[rC])
                            V("tensor_scalar", out=C2[:, 1, :], in0=C2[:, 1, :], scalar1=-1.0, scalar2=None,
                              op0=ALU.mult, rd=[rC], wr=[rC])
                            k2 = nw % 2
                            nw += 1
                            W8 = wkW[:, k2 * 4096:(k2 + 1) * 4096].rearrange("p (g a n) -> p g a n", g=8, a=4)
                            V("memset", wkW[:, k2 * 4096:(k2 + 1) * 4096], 0.0, wr=[rW[k2]])
                            for g8 in range(8):
                                gm = cst[:, GMC + g8:GMC + g8 + 1]
                                V("tensor_scalar", out=W8[:, g8, 0, :], in0=BB, scalar1=gm, scalar2=None, op0=ALU.mult,
                                  rd=[rB, cst_r, rW[k2]], wr=[rW[k2]])
                                V("tensor_scalar", out=W8[:, g8, 1, :], in0=BBs, scalar1=gm, scalar2=None, op0=ALU.mult,
                                  rd=[rB, cst_r, rW[k2]], wr=[rW[k2]])
                                cs16 = slice(16 * g8, 16 * g8 + 16)
                                V("tensor_copy", out=W8[:, g8, 2, cs16], in_=C2[:, 0, cs16], rd=[rC, rW[k2]], wr=[rW[k2]])
                                V("tensor_copy", out=W8[:, g8, 3, cs16], in_=C2[:, 1, cs16], rd=[rC, rW[k2]], wr=[rW[k2]])
                            S.add("scalar", "dma_start", out=s5W[j, d, q], in_=wkW[:, k2 * 4096:(k2 + 1) * 4096],
                                  rd=[rW[k2]], dma_sem=s5_sems[k2])
                    nd = 0
                    for q in range(4):
                        for half in range(2):
                            k2 = nd % 2
                            nd += 1
                            st_ = wkD[:, k2 * 2048:(k2 + 1) * 2048].rearrange("p (k n) -> p k n", k=16)
                            if half == 1:
                                V("memset", st_[:, 15, :], 0.0, wr=[rD[k2]])
                            for kk in range(16 if half == 0 else 15):
                                col = SCOL + j * 144 + q * 31 + half * 16 + kk
                                V("tensor_scalar", out=st_[:, kk, :], in0=ident[:], scalar1=cst[:, col:col + 1],
                                  scalar2=None, op0=ALU.mult, rd=[cst_r, rD[k2]], wr=[rD[k2]])
                            S.add("scalar", "dma_start", out=dgS[j, q, half], in_=wkD[:, k2 * 2048:(k2 + 1) * 2048],
                                  rd=[rD[k2]], dma_sem=s5_sems[2 + k2])

            if cfg.mixers and cfg.depth > 1:
                ssm_prologue()
            xb = x[:].rearrange("p k t -> p (k t)").bitcast(BF16)
            stg_r = [Res(), Res()]
            last_st = [None, None]
            nconv = 0
            for nm, (nb, fsz) in wshapes.items():
                if fsz <= 2048:
                    GB = 1
                    for g in (8, 4, 2, 1):
                        if g * fsz <= 8192 and nb % g == 0:
                            GB = g
                            break
                    sub = None
                else:
                    GB = 1
                    sub = 2
                    while fsz // sub > 2048 or fsz % sub:
                        sub += 1
                for b0 in range(0, nb, GB):
                    k = nconv % 2
                    nconv += 1
                    st = xb[:, k * 8192: k * 8192 + GB * fsz]
                    if sub is None:
                        o = st.rearrange("p (g f) -> p g f", g=GB)
                        i_ = wf[nm][b0:b0 + GB].rearrange("g p f -> p g f")
                        so = wb[nm][b0:b0 + GB].rearrange("g p f -> p g f")
                    else:
                        o = st.rearrange("p (a f) -> p a f", a=sub)
                        i_ = wf[nm][b0].rearrange("p (a f) -> p a f", a=sub)
                        so = wb[nm][b0].rearrange("p (a f) -> p a f", a=sub)
                    S.add("gpsimd", "dma_start", out=o, in_=i_, wr=[stg_r[k]], dma_sem=cv_sems[k])
                    last_st[k] = S.add("sync", "dma_start", out=so, in_=o, rd=[stg_r[k]], dma_sem=cvs_sems[k])
            lst = [] if S.dry else [i for i in last_st if i is not None]
            S.add("sync", "nop", wr=[wb_r], extra=lst)


            S.barrier()

            def rmsnorm(c, gc):
                cs = slice(c * TC, (c + 1) * TC)
                for kc in range(8):
                    S.add("scalar", "activation", out=sq[:, kc % 2, :], in_=x[:, kc, cs], func=AF.Square,
                          rd=[xr[c][kc]], wr=[sq_r[kc % 2]])
                    S.add("tensor", "matmul", P[6][:], lhsT=ones[:], rhs=sq[:, kc % 2, :],
                          start=(kc == 0), stop=(kc == 7), rd=[sq_r[kc % 2], cst_r], wr=[ps_r[6]])
                S.add("scalar", "activation", out=P[7][:], in_=P[6][:], func=AF.Sqrt, scale=1.0 / D_MODEL,
                      bias=cst[:, EPSC:EPSC + 1], rd=[ps_r[6], cst_r], wr=[ps_r[7]])
                S.add("vector", "reciprocal", out=P[7][:], in_=P[7][:], rd=[ps_r[7]], wr=[ps_r[7]])
                for kc in range(8):
                    S.add("vector", "scalar_tensor_tensor", out=xn[:, kc, :], in0=x[:, kc, cs],
                          scalar=cst[:, gc + kc:gc + kc + 1], in1=P[7][:], op0=ALU.mult, op1=ALU.mult,
                          rd=[xr[c][kc], ps_r[7], cst_r], wr=[xn_r[kc]])

            def ffn(c, l, which):
                cs = slice(c * TC, (c + 1) * TC)
                nm = "w1" if which == 0 else "w2"
                rmsnorm(c, gcol(l, 0 if which == 0 else 2))
                for j in range(NJ):
                    w, wr_ = wget(wb[nm + "i"][l * NJ + j], 2048)
                    wv = w.rearrange("p (g k n) -> p g k n", g=2, k=8)
                    pg = 2 * (j % 2)
                    pu = pg + 1
                    for kc in range(8):
                        S.add("tensor", "matmul", P[pg][:], lhsT=wv[:, 0, kc, :], rhs=xn[:, kc, :],
                              start=(kc == 0), stop=(kc == 7), rd=[wr_, xn_r[kc]], wr=[ps_r[pg]])
                    for kc in range(8):
                        S.add("tensor", "matmul", P[pu][:], lhsT=wv[:, 1, kc, :], rhs=xn[:, kc, :],
                              start=(kc == 0), stop=(kc == 7), rd=[wr_, xn_r[kc]], wr=[ps_r[pu]])
                    S.add("scalar", "activation", out=sg[:, j % 2, :], in_=P[pg][:], func=AF.Silu,
                          rd=[ps_r[pg]], wr=[sg_r[j % 2]])
                    S.add("vector", "tensor_tensor", out=hb[:, j, :], in0=sg[:, j % 2, :], in1=P[pu][:],
                          op=ALU.mult, rd=[sg_r[j % 2], ps_r[pu]], wr=[h_r[j]])
                for oc in range(8):
                    w, wr_ = wget(wb[nm + "o"][l * 8 + oc], NJ * 128)
                    wv = w.rearrange("p (h n) -> p h n", h=NJ)
                    po = 4 + oc % 2
                    for hc in range(NJ):
                        S.add("tensor", "matmul", P[po][:], lhsT=wv[:, hc, :], rhs=hb[:, hc, :],
                              start=(hc == 0), stop=(hc == NJ - 1), rd=[wr_, h_r[hc]], wr=[ps_r[po]])
                    S.add("vector", "scalar_tensor_tensor", out=x[:, oc, cs], in0=P[po][:], scalar=0.5,
                          in1=x[:, oc, cs], op0=ALU.mult, op1=ALU.add,
                          rd=[ps_r[po], xr[c][oc]], wr=[xr[c][oc]])

            def ple(s, c, l):
                cs = slice(c * TC, (c + 1) * TC)
                k = ple.n % 2
                ple.n += 1
                S.add("gpsimd", "dma_start", out=pb[:, k, :, :], in_=pT[s, l, :, :, cs],
                      wr=[pb_r[k]], dma_sem=p_sems[k])
                rmsnorm(c, gcol(l, 3))
                w, wr_ = wget(wb["wpp"][l], 2048)
                wv = w.rearrange("p (k n) -> p k n", k=2)
                for oc in range(8):
                    pp = oc % 2
                    for kc in range(2):
                        S.add("tensor", "matmul", P[pp][:], lhsT=wv[:, kc, oc * 128:(oc + 1) * 128],
                              rhs=pb[:, k, kc, :], start=(kc == 0), stop=(kc == 1),
                              rd=[wr_, pb_r[k]], wr=[ps_r[pp]])
                    S.add("vector", "tensor_copy", out=hb32[:, oc * TC:(oc + 1) * TC], in_=P[pp][:],
                          rd=[ps_r[pp]], wr=[h_r[2 * oc], h_r[2 * oc + 1]])
                    S.add("scalar", "activation", out=sq[:, oc % 2, :], in_=P[pp][:], func=AF.Square,
                          rd=[ps_r[pp]], wr=[sq_r[oc % 2]])
                    S.add("tensor", "matmul", P[6][:], lhsT=ones[:], rhs=sq[:, oc % 2, :],
                          start=(oc == 0), stop=(oc == 7), rd=[sq_r[oc % 2], cst_r], wr=[ps_r[6]])
                S.add("scalar", "activation", out=P[7][:], in_=P[6][:], func=AF.Sqrt, scale=1.0 / D_MODEL,
                      bias=cst[:, EPSC:EPSC + 1], rd=[ps_r[6], cst_r], wr=[ps_r[7]])
                S.add("vector", "reciprocal", out=P[7][:], in_=P[7][:], rd=[ps_r[7]], wr=[ps_r[7]])
                gp = gcol(l, 4)
                for oc in range(8):
                    w, wr_ = wget(wb["wpg"][l * 8 + oc], 1024)
                    wv = w.rearrange("p (k n) -> p k n", k=8)
                    pg = 2 + oc % 2
                    for kc in range(8):
                        S.add("tensor", "matmul", P[pg][:], lhsT=wv[:, kc, :], rhs=xn[:, kc, :],
                              start=(kc == 0), stop=(kc == 7), rd=[wr_, xn_r[kc]], wr=[ps_r[pg]])
                    S.add("scalar", "activation", out=P[pg][:], in_=P[pg][:], func=AF.Sigmoid,
                          rd=[ps_r[pg]], wr=[ps_r[pg]])
                    t = sg[:, oc % 2, :]
                    S.add("vector", "scalar_tensor_tensor", out=t, in0=hb32[:, oc * TC:(oc + 1) * TC],
                          scalar=cst[:, gp + oc:gp + oc + 1], in1=P[7][:], op0=ALU.mult, op1=ALU.mult,
                          rd=[h_r[2 * oc], h_r[2 * oc + 1], ps_r[7], cst_r], wr=[sg_r[oc % 2]])
                    S.add("vector", "tensor_tensor", out=t, in0=t, in1=P[pg][:], op=ALU.mult,
                          rd=[sg_r[oc % 2], ps_r[pg]], wr=[sg_r[oc % 2]])
                    S.add("vector", "tensor_tensor", out=x[:, oc, cs], in0=x[:, oc, cs], in1=t, op=ALU.add,
                          rd=[sg_r[oc % 2], xr[c][oc]], wr=[xr[c][oc]])

            ple.n = 0

            def attn_in(c, l):
                j = l // 2
                cs = slice(c * TC, (c + 1) * TC)
                rmsnorm(c, gcol(l, 1))
                for pc in range(13):
                    w, wr_ = wget(wb["wai"][j * 13 + pc], 1024)
                    wv = w.rearrange("p (k n) -> p k n", k=8)
                    pp = pc % 2
                    k = attn_in.n % 2
                    attn_in.n += 1
                    for kc in range(8):
                        S.add("tensor", "matmul", P[pp][:], lhsT=wv[:, kc, :], rhs=xn[:, kc, :],
                              start=(kc == 0), stop=(kc == 7), rd=[wr_, xn_r[kc]], wr=[ps_r[pp]])
                    S.add("scalar", "activation", out=sq[:, k, :], in_=P[pp][:], func=AF.Square,
                          rd=[ps_r[pp]], wr=[sq_r[k]])
                    S.add("tensor", "matmul", P[2 + pp][:], lhsT=bones[:], rhs=sq[:, k, :], start=True, stop=True,
                          rd=[sq_r[k], cst_r], wr=[ps_r[2 + pp]])
                    S.add("scalar", "activation", out=sg[:, k, :], in_=P[2 + pp][:], func=AF.Ln, scale=1.0 / 64,
                          bias=cst[:, EPSC:EPSC + 1], rd=[ps_r[2 + pp], cst_r], wr=[sg_r[k]])
                    S.add("scalar", "activation", out=sg[:, k, :], in_=sg[:, k, :], func=AF.Exp, scale=-0.5,
                          rd=[sg_r[k]], wr=[sg_r[k]])
                    gi = 0 if pc < 4 else 1 if pc == 4 else 2 if pc < 9 else 3
                    gc = ACOL + j * 12 + gi
                    S.add("vector", "scalar_tensor_tensor", out=stA[:, k, :], in0=P[pp][:], scalar=cst[:, gc:gc + 1],
                          in1=sg[:, k, :], op0=ALU.mult, op1=ALU.mult,
                          rd=[ps_r[pp], sg_r[k], cst_r], wr=[sta_r[k]])
                    S.add("gpsimd", "dma_start", out=qkT[pc, :, cs], in_=stA[:, k, :], rd=[sta_r[k]], wr=[qkT_r],
                          dma_sem=sta_sems[k], nowaw=True)
                w0, wr0 = wget(wb["wav"][j * 2 + 0], 2560)
                w1, wr1 = wget(wb["wav"][j * 2 + 1], 2560, held=1)
                wvh = [w0.rearrange("p (k f) -> p k f", k=4), w1.rearrange("p (k f) -> p k f", k=4)]
                wrh = [wr0, wr1]
                for tt in range(4):
                    k = attn_in.nv % 2
                    attn_in.nv += 1
                    for kc in range(8):
                        S.add("tensor", "matmul", P[4][:], lhsT=xn[:, kc, tt * 128:(tt + 1) * 128],
                              rhs=wvh[kc // 4][:, kc % 4, 0:512], start=(kc == 0), stop=(kc == 7),
                              rd=[wrh[kc // 4], xn_r[kc]], wr=[ps_r[4]])
                    for kc in range(8):
                        S.add("tensor", "matmul", P[5][:, 0:128], lhsT=xn[:, kc, tt * 128:(tt + 1) * 128],
                              rhs=wvh[kc // 4][:, kc % 4, 512:640], start=(kc == 0), stop=(kc == 7),
                              rd=[wrh[kc // 4], xn_r[kc]], wr=[ps_r[5]])
                    S.add("scalar", "activation", out=vst[:, k, 0:512], in_=P[4][:], func=AF.Copy,
                          rd=[ps_r[4]], wr=[vst_r[k]])
                    S.add("scalar", "activation", out=vst[:, k, 512:640], in_=P[5][:, 0:128], func=AF.Copy,
                          rd=[ps_r[5], vst_r[k]], wr=[vst_r[k]])
                    S.add("gpsimd", "dma_start", out=vS[c * 4 + tt], in_=vst[:, k, :], rd=[vst_r[k]], wr=[vS_r],
                          dma_sem=vst_sems[k], nowaw=True)

            attn_in.n = 0
            attn_in.nv = 0

            def attn_layer(l):
                j = l // 2
                passes = []
                for n in range(32):
                    for i in range(8):
                        for kind in (0, 1):
                            passes.append((n, i, kind, len(passes) % 2))

                def ctx(pz):
                    n, i, kind, u = pz
                    lo = min(max(n - 2, 0), 27)
                    low = min(max(n - 1, 0), 29)
                    if kind == 0:
                        cc, hf, nk = i % 4, i // 4, 3
                    else:
                        cc, hf, nk = i // 2, i % 2, 5
                    return dict(n=n, i=i, kind=kind, u=u, lo=lo, dw=low - lo, qs=n % 2, bs=(n * 8 + i) % 2,
                                typ={0: 0, 1: 1, 30: 3, 31: 4}.get(n, 2), ts=slice(n * 128, (n + 1) * 128),
                                cc=cc, hf=hf, nk=nk, rows=slice(64 * hf, 64 * hf + 64))

                def front(pz):
                    c_ = ctx(pz)
                    n, i, kind, u, qs, bs, rows, cc, nk = (c_[k] for k in ("n", "i", "kind", "u", "qs", "bs", "rows", "cc", "nk"))
                    if i == 0 and kind == 0:
                        ts = c_["ts"]
                        ks = slice(c_["lo"] * 128, c_["lo"] * 128 + 640)
                        G(qtt[:, qs, 0:4, :], qkT[0:4, :, ts].rearrange("j p t -> p j t"), rd=[qkT_r], wr=[qt_r[qs]],
                          sem=qt_sems[qs])
                        G(qtt[:, qs, 4:8, :], qkT[5:9, :, ts].rearrange("j p t -> p j t"), rd=[qkT_r], wr=[qt_r[qs]],
                          sem=qt_sems[qs], nowaw=True, grp=True)
                        G(kaT, qkT[4, :, ks], rd=[qkT_r], wr=[ka_r], sem=kv_sems[0])
                        G(kbT, qkT[9:13, :, ks].rearrange("j p t -> p j t"), rd=[qkT_r], wr=[kb_r], sem=kv_sems[1])
                    if kind == 0:
                        G(biast[:, bs, :], ball[j, c_["typ"], i], wr=[bias_r[bs]], sem=bi_sems[bs])
                    qap = qtt[rows, qs, (cc if kind == 0 else 4 + cc), :]
                    for jj in range(nk):
                        if kind == 0:
                            kap = kaT[rows, (c_["dw"] + jj) * 128:(c_["dw"] + jj + 1) * 128]
                            kres = ka_r
                        else:
                            kap = kbT[rows, cc, jj * 128:(jj + 1) * 128]
                            kres = kb_r
                        if jj < 4:
                            o, ores = P[u][:, jj * 128:(jj + 1) * 128], ps_r[u]
                        else:
                            o, ores = P[2 + u][:, 0:128], ps_r[2 + u]
                        T(o, lhsT=kap, rhs=qap, start=True, stop=True, rd=[kres, qt_r[qs]], wr=[ores])

                def mid(pz):
                    c_ = ctx(pz)
                    kind, u, bs, nk = c_["kind"], c_["u"], c_["bs"], c_["nk"]
                    if kind == 0:
                        V("scalar_tensor_tensor", out=et[:, u, 0:384], in0=P[u][:, 0:384], scalar=0.125,
                          in1=biast[:, bs, 640:1024], op0=ALU.mult, op1=ALU.add, rd=[ps_r[u], bias_r[bs]], wr=[e_r[u]])
                    else:
                        V("scalar_tensor_tensor", out=et[:, u, 0:512], in0=P[u][:, 0:512], scalar=0.125,
                          in1=biast[:, bs, 0:512], op0=ALU.mult, op1=ALU.add, rd=[ps_r[u], bias_r[bs]], wr=[e_r[u]])
                        V("scalar_tensor_tensor", out=et[:, u, 512:640], in0=P[2 + u][:, 0:128], scalar=0.125,
                          in1=biast[:, bs, 512:640], op0=ALU.mult, op1=ALU.add,
                          rd=[ps_r[2 + u], bias_r[bs], e_r[u]], wr=[e_r[u]])
                    A("activation", out=pTt[:, u, 0:nk * 128], in_=et[:, u, 0:nk * 128], func=AF.Exp,
                      rd=[e_r[u]], wr=[pT_r[u]])

                def back(pz):
                    c_ = ctx(pz)
                    n, i, kind, u, rows, nk, hf = (c_[k] for k in ("n", "i", "kind", "u", "rows", "nk", "hf"))
                    if i == 0 and kind == 0:
                        G(Vt, vS[c_["lo"]:c_["lo"] + 5].rearrange("t p f -> p t f"), rd=[vS_r], wr=[v_r], sem=kv_sems[2])
                    for jj in range(nk):
                        if kind == 0:
                            vap = Vt[:, c_["dw"] + jj, 512 + 64 * hf:512 + 64 * hf + 64]
                        else:
                            vap = Vt[:, jj, i * 64:(i + 1) * 64]
                        T(P[4 + u][rows, 0:128], lhsT=vap, rhs=pTt[:, u, jj * 128:(jj + 1) * 128], start=(jj == 0),
                          stop=(jj == nk - 1), rd=[v_r, pT_r[u]], wr=[ps_r[4 + u]])
                    for jj in range(nk):
                        T(P[6 + u][:, 0:128], lhsT=ones[:], rhs=pTt[:, u, jj * 128:(jj + 1) * 128], start=(jj == 0),
                          stop=(jj == nk - 1), rd=[cst_r, pT_r[u]], wr=[ps_r[6 + u]])

                def post(pz):
                    c_ = ctx(pz)
                    n, i, kind, u, rows, qs, cc = (c_[k] for k in ("n", "i", "kind", "u", "rows", "qs", "cc"))
                    if kind == 0:
                        A("activation", out=dent[rows, u, 0:128], in_=P[6 + u][rows, 0:128], func=AF.Ln,
                          bias=esink[rows, j * 8 + i:j * 8 + i + 1], rd=[ps_r[6 + u], cst_r], wr=[den_r[u]])
                    else:
                        A("activation", out=dent[rows, u, 0:128], in_=P[6 + u][rows, 0:128], func=AF.Ln,
                          rd=[ps_r[6 + u]], wr=[den_r[u]])
                    A("activation", out=dent[rows, u, 0:128], in_=dent[rows, u, 0:128], func=AF.Exp, scale=-1.0,
                      rd=[den_r[u]], wr=[den_r[u]])
                    ych = cc if kind == 0 else 4 + cc
                    V("tensor_tensor", out=ystt[rows, qs, ych, :], in0=P[4 + u][rows, 0:128], in1=dent[rows, u, 0:128],
                      op=ALU.mult, rd=[ps_r[4 + u], den_r[u]], wr=[yst_r[qs]], nowaw=True)
                    if i == 7 and kind == 1:
                        G(ymix[:, :, c_["ts"]], ystt[:, qs, :, :], rd=[yst_r[qs]], wr=[ym_r], sem=ys_sems[qs], nowaw=True)

                front(passes[0])
                for ii, pz in enumerate(passes):
                    if ii + 1 < len(passes):
                        front(passes[ii + 1])
                    mid(pz)
                    if ii >= 1:
                        post(passes[ii - 1])
                    back(pz)
                post(passes[-1])

            def mix_out(c, l, wname):
                j = l // 2
                cs = slice(c * TC, (c + 1) * TC)
                S.add("gpsimd", "dma_start", out=xn[:], in_=ymix[:, :, cs], rd=[ym_r], wr=xn_r, dma_sem=ym_sem)
                for oc in range(8):
                    w, wr_ = wget(wb[wname][j * 8 + oc], 1024)
                    wv = w.rearrange("p (k n) -> p k n", k=8)
                    po = 4 + oc % 2
                    for kc in range(8):
                        S.add("tensor", "matmul", P[po][:], lhsT=wv[:, kc, :], rhs=xn[:, kc, :],
                              start=(kc == 0), stop=(kc == 7), rd=[wr_, xn_r[kc]], wr=[ps_r[po]])
                    S.add("vector", "tensor_tensor", out=x[:, oc, cs], in0=P[po][:], in1=x[:, oc, cs], op=ALU.add,
                          rd=[ps_r[po], xr[c][oc]], wr=[xr[c][oc]])


            uS_r, hhS_r, zS_r = Res(), Res(), Res()
            hh_r, hc_r, tmpc_r, ystc_r = Res(), Res(), [Res(), Res()], Res()
            uq_r, yq_r, s5w_r = Res(), Res(), Res()
            tab_r = [Res() for _ in range(8)]
            rot_r, gt_r, G_r = [Res(), Res()], [Res(), Res()], [Res(), Res()]
            G2_r = [Res(), Res()]
            stc_r = [Res() for _ in range(8)]
            rho_r = [Res(), Res()]
            zt_r = Res()

            def ssm_in(c, l):
                j = l // 2
                cs = slice(c * TC, (c + 1) * TC)
                rmsnorm(c, gcol(l, 1))
                for q in range(4):
                    w, wr_ = wget(wb["wsi"][j * 12 + q], 1024)
                    wv = w.rearrange("p (k n) -> p k n", k=8)
                    pp = q % 2
                    k = attn_in.n % 2
                    attn_in.n += 1
                    for kc in range(8):
                        T(P[pp][:], lhsT=wv[:, kc, :], rhs=xn[:, kc, :], start=(kc == 0), stop=(kc == 7),
                          rd=[wr_, xn_r[kc]], wr=[ps_r[pp]])
                    A("activation", out=stA[:, k, :], in_=P[pp][:], func=AF.Copy, rd=[ps_r[pp]], wr=[sta_r[k]])
                    G(uS[q, :, cs], stA[:, k, :], rd=[sta_r[k]], wr=[uS_r], sem=sta_sems[k], nowaw=True)
                for q in range(4):
                    pa = 2 + 2 * (q % 2)
                    pg = pa + 1
                    k = attn_in.n % 2
                    attn_in.n += 1
                    for pc, pbank in ((4 + q, pa), (8 + q, pg)):
                        w, wr_ = wget(wb["wsi"][j * 12 + pc], 1024)
                        wv = w.rearrange("p (k n) -> p k n", k=8)
                        for kc in range(8):
                            T(P[pbank][:], lhsT=wv[:, kc, :], rhs=xn[:, kc, :], start=(kc == 0), stop=(kc == 7),
                              rd=[wr_, xn_r[kc]], wr=[ps_r[pbank]])
                    A("activation", out=sg[:, k, :], in_=P[pg][:], func=AF.Sigmoid, rd=[ps_r[pg]], wr=[sg_r[k]])
                    V("tensor_tensor", out=stA[:, k, :], in0=sg[:, k, :], in1=P[pa][:], op=ALU.mult,
                      rd=[sg_r[k], ps_r[pa]], wr=[sta_r[k]])
                    G(hhS[q, :, 16 + c * TC:16 + (c + 1) * TC], stA[:, k, :], rd=[sta_r[k]], wr=[hhS_r],
                      sem=sta_sems[k], nowaw=True)

            def conv_chunk(c, l):
                j = l // 2
                cs = slice(c * TC, (c + 1) * TC)
                base = SCOL + j * 144
                G(hhwin[:, :, 0:542], hhS[:, :, c * TC + 1:c * TC + 543].rearrange("q p t -> p q t"),
                  rd=[hhS_r], wr=[hh_r], sem=kv_sems[0])
                for q in range(4):
                    wA, rA_ = wget(dgS[j, q, 0], 2048)
                    wB, rB_ = wget(dgS[j, q, 1], 2048, held=1)
                    wv = [wA.rearrange("p (k n) -> p k n", k=16), wB.rearrange("p (k n) -> p k n", k=16)]
                    rr = [rA_, rB_]
                    pp = q % 2
                    for k in range(31):
                        T(P[pp][:], lhsT=wv[k // 16][:, k % 16, :], rhs=hhwin[:, q, k:k + TC], start=(k == 0),
                          stop=(k == 30), rd=[rr[k // 16], hh_r], wr=[ps_r[pp]])
                    cb = cst[:, base + 124 + q:base + 125 + q]
                    A("activation", out=hcv[:, q, :], in_=P[pp][:], func=AF.Identity, bias=cb,
                      rd=[ps_r[pp], cst_r], wr=[hc_r], nowaw=True)
                    A("activation", out=sq[:, 0, :], in_=P[pp][:], func=AF.Identity, bias=cb,
                      rd=[ps_r[pp], cst_r], wr=[sq_r[0]])
                    A("activation", out=sq[:, 1, :], in_=P[pp][:], func=AF.Square, bias=cb,
                      rd=[ps_r[pp], cst_r], wr=[sq_r[1]])
                    T(P[2][:], lhsT=ones[:], rhs=sq[:, 0, :], start=(q == 0), stop=(q == 3),
                      rd=[sq_r[0], cst_r], wr=[ps_r[2]])
                    T(P[3][:], lhsT=ones[:], rhs=sq[:, 1, :], start=(q == 0), stop=(q == 3),
                      rd=[sq_r[1], cst_r], wr=[ps_r[3]])
                A("activation", out=msq, in_=P[2][:], func=AF.Square, scale=1.0 / 512, rd=[ps_r[2]], wr=[tmpc_r[0]])
                A("activation", out=P[4][:], in_=P[2][:], func=AF.Copy, scale=1.0 / 512, rd=[ps_r[2]], wr=[ps_r[4]])
                V("scalar_tensor_tensor", out=msq, in0=P[3][:], scalar=1.0 / 512, in1=msq, op0=ALU.mult,
                  op1=ALU.subtract, rd=[ps_r[3], tmpc_r[0]], wr=[tmpc_r[0]])
                A("activation", out=P[5][:], in_=msq, func=AF.Sqrt, bias=cst[:, EPSC:EPSC + 1],
                  rd=[tmpc_r[0], cst_r], wr=[ps_r[5]])
                V("reciprocal", out=P[5][:], in_=P[5][:], rd=[ps_r[5]], wr=[ps_r[5]])
                for q in range(4):
                    k = q % 2
                    V("tensor_tensor", out=tmpc[:, k, :], in0=hcv[:, q, :], in1=P[4][:], op=ALU.subtract,
                      rd=[hc_r, ps_r[4]], wr=[tmpc_r[k]] if False else [rot_r[k]])
                    V("scalar_tensor_tensor", out=tmpc[:, k, :], in0=tmpc[:, k, :],
                      scalar=cst[:, base + 128 + q:base + 129 + q], in1=P[5][:], op0=ALU.mult, op1=ALU.mult,
                      rd=[rot_r[k], ps_r[5], cst_r], wr=[rot_r[k]])
                    A("activation", out=ystc[:, q, :], in_=tmpc[:, k, :], func=AF.Silu,
                      bias=cst[:, base + 132 + q:base + 133 + q], rd=[rot_r[k], cst_r], wr=[ystc_r], nowaw=True)
                G(ymix[:, 4:8, cs], ystc, rd=[ystc_r], wr=[ym_r], sem=ys_sems[0], nowaw=True)

            def s5_scan(l):
                j = l // 2
                base = SCOL + j * 144
                nch = SEQ // TS
                for q in range(4):
                    G(uTq, uS[q], rd=[uS_r], wr=[uq_r], sem=kv_sems[1])
                    for d, gh in ((0, 0), (0, 1), (1, 0), (1, 1)):
                        G(s5wt[:].rearrange("p g a n -> p (g a n)"), s5W[j, d, q][:, gh * 2048:(gh + 1) * 2048],
                          wr=[s5w_r], sem=kv_sems[2])
                        for g4 in range(4):
                            G(tab_ap[g4], tabS[j, d, q * 8 + gh * 4 + g4].rearrange("p (c t) -> p c t", c=2),
                              wr=[tab_r[g4]], sem=tb2_sems[g4 % 2])
                            V("memset", stc[:, g4:g4 + 1], 0.0, wr=[stc_r[g4]])
                        its = []
                        for cc in range(nch):
                            for g8 in range(4):
                                u = s5_scan.it % 2
                                s5_scan.it += 1
                                its.append((cc, g8, u))

                        def ctx(it):
                            cc, g8, u = it
                            if d == 0:
                                rng = slice(cc * TS, (cc + 1) * TS)
                                urhs = uTq[:, rng]
                            else:
                                rng = slice(SEQ - (cc + 1) * TS, SEQ - cc * TS)
                                urhs = uTq[:, rng][:, ::-1]
                            return cc, g8, u, rng, urhs, 2 + cc % 2, d * 32 + q * 8 + gh * 4 + g8

                        def front(it):
                            cc, g8, u, rng, urhs, py, dg = ctx(it)
                            T(P[u][:, 0:TS], lhsT=s5wt[:, g8, 0, :], rhs=urhs, start=True, stop=True,
                              rd=[s5w_r, uq_r], wr=[ps_r[u]])
                            T(P[u][:, TS:2 * TS], lhsT=s5wt[:, g8, 1, :], rhs=urhs, start=True, stop=True,
                              rd=[s5w_r, uq_r], wr=[ps_r[u]])

                        def mid(it):
                            cc, g8, u, rng, urhs, py, dg = ctx(it)
                            cosT = tab_ap[g8][:, 0, :]
                            sinT = tab_ap[g8][:, 1, :]
                            A("activation", out=rhoT[:, u, :], in_=cosT, func=AF.Identity, scale=0.0,
                              bias=s5p[:, j, 0, dg:dg + 1], rd=[tab_r[g8], cst_r], wr=[rho_r[u]])
                            V("tensor_tensor", out=rot2[:, 0, :], in0=P[u][:, TS:2 * TS], in1=sinT, op=ALU.mult,
                              rd=[ps_r[u], tab_r[g8]], wr=[rot_r[0]])
                            V("tensor_tensor", out=P[u][:, 0:TS], in0=P[u][:, 0:TS], in1=cosT, op=ALU.mult,
                              rd=[ps_r[u], tab_r[g8]], wr=[ps_r[u]])
                            V("tensor_tensor", out=rot2[:, 1, :], in0=P[u][:, 0:TS], in1=rot2[:, 0, :], op=ALU.add,
                              rd=[rot_r[0], ps_r[u]], wr=[rot_r[1]])
                            V("tensor_tensor_scan", out=gt2[:, u, :], data0=rhoT[:, u, :], data1=rot2[:, 1, :],
                              initial=stc[:, g8:g8 + 1], op0=ALU.mult, op1=ALU.add,
                              rd=[rho_r[u], rot_r[1], stc_r[g8]], wr=[gt_r[u]])
                            S.add("gpsimd", "tensor_tensor", out=G4[:, u, :], in0=gt2[:, u, :], in1=cosT, op=ALU.mult,
                                  rd=[gt_r[u], tab_r[g8]], wr=[G_r[u]])
                            S.add("gpsimd", "tensor_tensor", out=G4[:, 2 + u, :], in0=gt2[:, u, :], in1=sinT, op=ALU.mult,
                                  rd=[gt_r[u], tab_r[g8]], wr=[G2_r[u]])

                        def back(it):
                            cc, g8, u, rng, urhs, py, dg = ctx(it)
                            T(P[4 + u][:, 0:2], lhsT=swm[:], rhs=gt2[:, u, TS - 2:TS], start=True, stop=True,
                              rd=[gt_r[u], cst_r], wr=[ps_r[4 + u]])
                            T(P[py][:, 0:TS], lhsT=s5wt[:, g8, 2, :], rhs=G4[:, u, :], start=(g8 == 0), stop=False,
                              rd=[s5w_r, G_r[u]], wr=[ps_r[py]])
                            T(P[py][:, 0:TS], lhsT=s5wt[:, g8, 3, :], rhs=G4[:, 2 + u, :], start=False,
                              stop=(g8 == 3), rd=[s5w_r, G2_r[u]], wr=[ps_r[py]])

                        def post(it):
                            cc, g8, u, rng, urhs, py, dg = ctx(it)
                            V("tensor_scalar", out=stc[:, 8 + g8:9 + g8], in0=P[4 + u][:, 1:2],
                              scalar1=s5p[:, j, 2, dg:dg + 1], scalar2=None, op0=ALU.mult,
                              rd=[ps_r[4 + u], cst_r], wr=[stc_r[g8]])
                            V("scalar_tensor_tensor", out=stc[:, g8:g8 + 1], in0=gt2[:, u, TS - 1:TS],
                              scalar=s5p[:, j, 1, dg:dg + 1], in1=stc[:, 8 + g8:9 + g8], op0=ALU.mult,
                              op1=ALU.subtract, rd=[gt_r[u], stc_r[g8], cst_r], wr=[stc_r[g8]])
                            if g8 != 3:
                                return
                            if d == 0 and gh == 0:
                                V("scalar_tensor_tensor", out=yq[:, rng], in0=uTq[:, rng],
                                  scalar=cst[:, base + 136 + q:base + 137 + q], in1=P[py][:, 0:TS], op0=ALU.mult,
                                  op1=ALU.add, rd=[uq_r, ps_r[py], cst_r], wr=[yq_r], nowaw=True)
                            elif d == 0:
                                V("tensor_tensor", out=yq[:, rng], in0=yq[:, rng], in1=P[py][:, 0:TS],
                                  op=ALU.add, rd=[yq_r, ps_r[py]], wr=[yq_r], nowaw=True)
                            else:
                                V("tensor_tensor", out=yq[:, rng][:, ::-1], in0=yq[:, rng][:, ::-1], in1=P[py][:, 0:TS],
                                  op=ALU.add, rd=[yq_r, ps_r[py]], wr=[yq_r], nowaw=True)

                        front(its[0])
                        for ii, it in enumerate(its):
                            if ii + 1 < len(its):
                                front(its[ii + 1])
                            mid(it)
                            if ii >= 1:
                                post(its[ii - 1])
                            back(it)
                        post(its[-1])
                    for c in range(NCH):
                        cs = slice(c * TC, (c + 1) * TC)
                        k = c % 2
                        A("activation", out=rot512, in_=yq[:, cs], func=AF.Square, rd=[yq_r], wr=[rot_r[0], rot_r[1]])
                        V("tensor_scalar", out=rot512, in0=rot512, scalar1=0.044715, scalar2=1.0, op0=ALU.mult,
                          op1=ALU.add, rd=[rot_r[0], rot_r[1]], wr=[rot_r[0], rot_r[1]])
                        V("tensor_tensor", out=rot512, in0=rot512, in1=yq[:, cs], op=ALU.mult,
                          rd=[rot_r[0], rot_r[1], yq_r], wr=[rot_r[0], rot_r[1]])
                        A("activation", out=gt512, in_=rot512, func=AF.Sigmoid, scale=1.5957691216057308,
                          rd=[rot_r[0], rot_r[1]], wr=[gt_r[0], gt_r[1]])
                        V("tensor_tensor", out=g512[:, k, :], in0=yq[:, cs], in1=gt512, op=ALU.mult,
                          rd=[yq_r, gt_r[0], gt_r[1]], wr=[G_r[0], G_r[1], G2_r[0], G2_r[1]])
                        G(zS[q, :, cs], g512[:, k, :], rd=[G_r[0], G_r[1], G2_r[0], G2_r[1]], wr=[zS_r], sem=zs_sems[k],
                          nowaw=True)

            s5_scan.it = 0

            def ssm_out(c, l):
                j = l // 2
                cs = slice(c * TC, (c + 1) * TC)
                base = SCOL + j * 144
                zt = pbf.rearrange("p (q t) -> p q t", q=4)
                G(zt, zS[:, :, cs].rearrange("q p t -> p q t"), rd=[zS_r], wr=[pb_r[0], pb_r[1]], sem=p_sems[0])
                G(xn[:, 4:8, :], ymix[:, 4:8, cs], rd=[ym_r], wr=xn_r[4:8], sem=ym_sem)
                for oc in range(4):
                    w, wr_ = wget(wb["wsg"][j * 4 + oc], 512)
                    wv = w.rearrange("p (k n) -> p k n", k=4)
                    pp = oc % 2
                    for kc in range(4):
                        T(P[pp][:], lhsT=wv[:, kc, :], rhs=zt[:, kc, :], start=(kc == 0), stop=(kc == 3),
                          rd=[wr_, pb_r[0], pb_r[1]], wr=[ps_r[pp]])
                    A("activation", out=sg[:, pp, :], in_=P[pp][:], func=AF.Sigmoid,
                      bias=cst[:, base + 140 + oc:base + 141 + oc], rd=[ps_r[pp], cst_r], wr=[sg_r[pp]])
                    V("tensor_tensor", out=xn[:, oc, :], in0=zt[:, oc, :], in1=sg[:, pp, :], op=ALU.mult,
                      rd=[pb_r[0], pb_r[1], sg_r[pp]], wr=[xn_r[oc]])
                for oc in range(8):
                    w, wr_ = wget(wb["wso"][j * 8 + oc], 1024)
                    wv = w.rearrange("p (k n) -> p k n", k=8)
                    po = 4 + oc % 2
                    for kc in range(8):
                        T(P[po][:], lhsT=wv[:, kc, :], rhs=xn[:, kc, :], start=(kc == 0), stop=(kc == 7),
                          rd=[wr_, xn_r[kc]], wr=[ps_r[po]])
                    V("tensor_tensor", out=x[:, oc, cs], in0=P[po][:], in1=x[:, oc, cs], op=ALU.add,
                      rd=[ps_r[po], xr[c][oc]], wr=[xr[c][oc]])

            outs = []
            for s in range(NS):
                for c in range(NCH):
                    cs = slice(c * TC, (c + 1) * TC)
                    S.add("gpsimd", "dma_start", out=x[:, :, cs], in_=xT[s, :, :, cs],
                          wr=xr[c], dma_sem=x_sems[c], extra=(lst if s == 0 else []))
                for l in range(cfg.depth):
                    for c in range(NCH):
                        if cfg.stage >= 1:
                            ffn(c, l, 0)
                    has_mix = cfg.mixers and (l % 2 == 0)
                    has_ssm = cfg.mixers and (l % 2 == 1)
                    if has_ssm:
                        for c in range(NCH):
                            ssm_in(c, l)
                        S.barrier()
                        for c in range(NCH):
                            conv_chunk(c, l)
                        S.barrier()
                        s5_scan(l)
                        S.barrier()
                    if has_mix:
                        for c in range(NCH):
                            attn_in(c, l)
                        S.barrier()
                        attn_layer(l)
                        S.barrier()
                    for c in range(NCH):
                        if has_mix:
                            mix_out(c, l, "wao")
                        if has_ssm:
                            ssm_out(c, l)
                        if cfg.stage >= 2:
                            ffn(c, l, 1)
                        if cfg.stage >= 3:
                            ple(s, c, l)
                        if l == cfg.depth - 1:
                            cs = slice(c * TC, (c + 1) * TC)
                            o = S.add("gpsimd", "dma_start", out=yT[s, :, :, cs], in_=x[:, :, cs],
                                      rd=xr[c], dma_sem=x_sems[c])
                            outs.append(o)
            S.add("gpsimd", "nop", extra=([] if S.dry else outs))
            if S.dry:
                record.reqs = WS.reqs

        EPSC = ncst - 1
        S.dry = True
        record()
        S.dry = False
        record()
        S.emit(nc, block, eng_sems)
    return nc, S


_CACHE = {}


def prep_inputs(inp, cfg):
    xs = np.concatenate([np.asarray(inp["x_prompt"]), np.asarray(inp["x_sample"])], axis=0)
    ps = np.concatenate([np.asarray(inp["p_prompt"]), np.asarray(inp["p_sample"])], axis=1)
    W = pack_weights(inp)
    W["ball"] = attn_tables(inp)
    W.update(ssm_params(inp))
    cst = pack_consts(inp)
    cst = np.concatenate([cst, np.full((128, 1), EPS, np.float32)], axis=1)
    return xs, ps, W, cst


def core_inputs(xs, ps, W, cst, sl):
    NS = len(sl)
    xT = np.ascontiguousarray(xs[sl].reshape(NS, SEQ, 8, 128).transpose(0, 3, 2, 1))
    pT = np.ascontiguousarray(ps[:, sl].reshape(DEPTH, NS, SEQ, 2, 128).transpose(1, 0, 4, 3, 2))
    m = {"xT": xT, "pT": pT, "cst": cst}
    m.update(W)
    return m


def kernel(**inp):
    cfg = Cfg()
    inp = {k: np.asarray(v) for k, v in inp.items()}
    xs, ps, W, cst = prep_inputs(inp, cfg)
    nseq_total = xs.shape[0]
    wshapes = {nm: (a.shape[0], a.shape[2]) for nm, a in W.items() if nm.startswith("w")}
    nc, S = build_program(cfg, wshapes, cst.shape[1])
    in_maps = []
    NS = cfg.nseq
    for core in range(N_CORES):
        sl = [(core * NS + i) % nseq_total for i in range(NS)]
        in_maps.append(core_inputs(xs, ps, W, cst, sl))
    res = run_bass_kernel_spmd(nc, in_maps, core_ids=list(range(N_CORES)))
    ys = np.zeros((nseq_total, SEQ, D_MODEL), np.float32)
    for core in range(N_CORES):
        yT = res.results[core]["yT"]
        y = yT.transpose(0, 3, 2, 1).reshape(NS, SEQ, D_MODEL)
        for i in range(NS):
            ys[(core * NS + i) % nseq_total] = y[i]
    nb = inp["x_prompt"].shape[0]
    return ys[:nb], ys[nb:]
```
